# Optimizing a Trainium2 kernel written in Bass

```python
import math
import numpy as np
import jax
import jax.numpy as jnp
from jax import lax

D_MODEL = 1024
BATCH = 8
SEQ = 8192
DEPTH = 2

D_FF = 2816
CONV_K = 4
CHUNK = 64
EPS = 1e-6
GDN_HEADS = 8
GDN_HEAD_DIM = 128
GDN_W = GDN_HEADS * GDN_HEAD_DIM
SSM_HEADS = 16
SSM_HEAD_DIM = 64
SSM_GROUPS = 2
SSM_STATE = 128
SSM_W = SSM_HEADS * SSM_HEAD_DIM
SSM_BC_W = 2 * SSM_GROUPS * SSM_STATE
MIX_W = GDN_W + SSM_W
IN_W = 4 * GDN_W + 2 * GDN_HEADS + 2 * SSM_W + SSM_BC_W + SSM_HEADS
DT_MIN = 0.001
DT_MAX = 0.1
A_MIN = 1.0
A_MAX = 16.0

kernel_name = "hybrid_gdn_mamba2_macaron"


def rms_norm(x, w):
    xf = x.astype(jnp.float32)
    y = xf * lax.rsqrt(jnp.mean(xf * xf, axis=-1, keepdims=True) + EPS)
    return (y * w.astype(jnp.float32)).astype(x.dtype)


def l2_normalize(x):
    return x * lax.rsqrt(jnp.sum(x * x, axis=-1, keepdims=True) + EPS)


def swiglu(h, w_gate, w_up, w_down):
    a = jnp.einsum("bld,df->blf", h, w_gate)
    b = jnp.einsum("bld,df->blf", h, w_up)
    return jnp.einsum("blf,fd->bld", jax.nn.silu(a) * b, w_down)


def causal_dwconv(x, w):
    ch = x.shape[-1]
    return lax.conv_general_dilated(
        x, w[:, None, :].astype(x.dtype), window_strides=(1,),
        padding=[(CONV_K - 1, 0)], dimension_numbers=("NWC", "WIO", "NWC"),
        feature_group_count=ch)


def gated_delta_rule(q, k, v, g, beta):
    bsz, seqlen, nh, dk = q.shape
    dv = v.shape[-1]
    nc = seqlen // CHUNK

    def to_chunks(t):
        t = t.reshape(bsz, nc, CHUNK, nh, *t.shape[3:])
        return jnp.moveaxis(t, 3, 1)

    q = to_chunks(l2_normalize(q) * (dk ** -0.5))
    k = to_chunks(l2_normalize(k))
    v = to_chunks(v)
    beta = to_chunks(beta)
    gc = jnp.cumsum(to_chunks(g), axis=-1)
    incl = jnp.tril(jnp.ones((CHUNK, CHUNK), bool))
    strict = jnp.tril(jnp.ones((CHUNK, CHUNK), bool), -1)
    decay = jnp.exp(jnp.where(incl, gc[..., :, None] - gc[..., None, :], -jnp.inf))
    kk = jnp.einsum("bhcld,bhcsd->bhcls", k, k)
    a_low = jnp.where(strict, beta[..., :, None] * kk * decay, 0.0)
    eye = jnp.eye(CHUNK, dtype=q.dtype)
    t_inv = lax.linalg.triangular_solve(
        eye + a_low, jnp.broadcast_to(eye, a_low.shape),
        left_side=True, lower=True, unit_diagonal=True)
    u = jnp.einsum("bhcls,bhcsd->bhcld", t_inv, v * beta[..., None])
    w = jnp.einsum("bhcls,bhcsd->bhcld", t_inv, k * (beta * jnp.exp(gc))[..., None])
    qk = jnp.einsum("bhcld,bhcsd->bhcls", q, k) * decay
    q_dec = q * jnp.exp(gc)[..., None]
    k_dec = k * jnp.exp(gc[..., -1:] - gc)[..., None]
    g_tot = jnp.exp(gc[..., -1])
    xs = tuple(jnp.moveaxis(t, 2, 0) for t in (qk, u, w, q_dec, k_dec, g_tot))

    def step(state, inp):
        qk_c, u_c, w_c, qd_c, kd_c, gt_c = inp
        v_new = u_c - jnp.einsum("bhld,bhde->bhle", w_c, state)
        o = (jnp.einsum("bhld,bhde->bhle", qd_c, state)
             + jnp.einsum("bhls,bhse->bhle", qk_c, v_new))
        state = state * gt_c[..., None, None] + jnp.einsum("bhld,bhle->bhde", kd_c, v_new)
        return state, o

    s0 = jnp.zeros((bsz, nh, dk, dv), jnp.float32)
    _, o = lax.scan(step, s0, xs)
    o = jnp.transpose(o, (1, 0, 3, 2, 4))
    return o.reshape(bsz, seqlen, nh, dv)


def ssd_scan(x, dt, a_neg, b_in, c_in):
    bsz, seqlen, nh, p = x.shape
    ng, n = b_in.shape[2], b_in.shape[3]
    nj = nh // ng
    nc = seqlen // CHUNK
    a = (dt * a_neg).reshape(bsz, nc, CHUNK, ng, nj)
    xdt = (x * dt[..., None]).reshape(bsz, nc, CHUNK, ng, nj, p)
    bc = b_in.reshape(bsz, nc, CHUNK, ng, n)
    cc = c_in.reshape(bsz, nc, CHUNK, ng, n)
    acs = jnp.cumsum(a, axis=2)
    incl = jnp.tril(jnp.ones((CHUNK, CHUNK), bool))[:, :, None, None]
    seg = jnp.exp(jnp.where(incl, acs[:, :, :, None] - acs[:, :, None, :], -jnp.inf))
    cb = jnp.einsum("bclgn,bcsgn->bclsg", cc, bc)
    y_diag = jnp.einsum("bclsgj,bcsgjp->bclgjp", cb[..., None] * seg, xdt)
    x_to_end = xdt * jnp.exp(acs[:, :, -1:] - acs)[..., None]
    xs = tuple(jnp.moveaxis(t, 1, 0) for t in
               (bc, cc, x_to_end, jnp.exp(acs), jnp.exp(acs[:, :, -1])))

    def step(state, inp):
        b_c, c_c, xe_c, din_c, dtot_c = inp
        y_off = jnp.einsum("blgn,bgjpn->blgjp", c_c, state) * din_c[..., None]
        state = state * dtot_c[..., None, None] + jnp.einsum("blgn,blgjp->bgjpn", b_c, xe_c)
        return state, y_off

    s0 = jnp.zeros((bsz, ng, nj, p, n), jnp.float32)
    _, y_off = lax.scan(step, s0, xs)
    y = y_diag + jnp.moveaxis(y_off, 0, 1)
    return y.reshape(bsz, seqlen, nh, p)


def hybrid_mixer(h, w_in, gdn_conv_w, gdn_a_log, gdn_dt_bias, gdn_norm_w,
                 ssm_conv_w, ssm_conv_b, ssm_a_log, ssm_dt_bias, ssm_d, ssm_norm_w, w_out):
    bsz, seqlen, _ = h.shape
    proj = jnp.einsum("bld,de->ble", h, w_in).astype(jnp.float32)
    sizes = (3 * GDN_W, GDN_W, GDN_HEADS, GDN_HEADS, SSM_W, SSM_W + SSM_BC_W, SSM_HEADS)
    offs = np.cumsum(sizes)[:-1].tolist()
    gdn_qkv, gdn_z, gdn_b, gdn_a, ssm_z, ssm_xbc, ssm_dt = jnp.split(proj, offs, axis=-1)

    qkv = jax.nn.silu(causal_dwconv(gdn_qkv, gdn_conv_w))
    q, k, v = [t.reshape(bsz, seqlen, GDN_HEADS, GDN_HEAD_DIM) for t in jnp.split(qkv, 3, axis=-1)]
    beta = jax.nn.sigmoid(gdn_b)
    g = -jnp.exp(gdn_a_log.astype(jnp.float32)) * jax.nn.softplus(gdn_a + gdn_dt_bias)
    o = gated_delta_rule(q, k, v, g, beta)
    o = rms_norm(o, gdn_norm_w) * jax.nn.silu(gdn_z.reshape(bsz, seqlen, GDN_HEADS, GDN_HEAD_DIM))
    o_gdn = o.reshape(bsz, seqlen, GDN_W)

    xbc = jax.nn.silu(causal_dwconv(ssm_xbc, ssm_conv_w) + ssm_conv_b)
    xs, bs, cs = jnp.split(xbc, [SSM_W, SSM_W + SSM_GROUPS * SSM_STATE], axis=-1)
    xs = xs.reshape(bsz, seqlen, SSM_HEADS, SSM_HEAD_DIM)
    dt = jax.nn.softplus(ssm_dt + ssm_dt_bias)
    a_neg = -jnp.exp(ssm_a_log.astype(jnp.float32))
    y = ssd_scan(xs, dt, a_neg,
                 bs.reshape(bsz, seqlen, SSM_GROUPS, SSM_STATE),
                 cs.reshape(bsz, seqlen, SSM_GROUPS, SSM_STATE))
    y = y + xs * ssm_d[:, None]
    y = y.reshape(bsz, seqlen, SSM_W) * jax.nn.silu(ssm_z)
    y = rms_norm(y.reshape(bsz, seqlen, SSM_GROUPS, SSM_W // SSM_GROUPS),
                 ssm_norm_w.reshape(SSM_GROUPS, SSM_W // SSM_GROUPS))
    o_ssm = y.reshape(bsz, seqlen, SSM_W)

    mixed = jnp.concatenate([o_gdn, o_ssm], axis=-1).astype(h.dtype)
    return jnp.einsum("ble,ed->bld", mixed, w_out)


def setup_inputs(seed: int = 0) -> dict:
    key = jax.random.key(seed)
    ks = jax.random.split(key, 23)
    f32 = jnp.float32

    def dense(k, shape, fan_in):
        return jax.random.normal(k, shape, f32) * (fan_in ** -0.5)

    def gain(k, shape):
        return 1.0 + 0.02 * jax.random.normal(k, shape, f32)

    def dt_bias(k, n):
        dt = jnp.exp(jax.random.uniform(k, (DEPTH, n), f32, math.log(DT_MIN), math.log(DT_MAX)))
        return dt + jnp.log(-jnp.expm1(-dt))

    def a_log(k, n):
        return jnp.log(jax.random.uniform(k, (DEPTH, n), f32, A_MIN, A_MAX))

    return {
        "x": jax.random.normal(ks[0], (BATCH, SEQ, D_MODEL), f32),
        "ffn1_norm": gain(ks[1], (DEPTH, D_MODEL)),
        "ffn1_w_gate": dense(ks[2], (DEPTH, D_MODEL, D_FF), D_MODEL),
        "ffn1_w_up": dense(ks[3], (DEPTH, D_MODEL, D_FF), D_MODEL),
        "ffn1_w_down": dense(ks[4], (DEPTH, D_FF, D_MODEL), D_FF),
        "mix_norm": gain(ks[5], (DEPTH, D_MODEL)),
        "w_in": dense(ks[6], (DEPTH, D_MODEL, IN_W), D_MODEL),
        "gdn_conv_w": dense(ks[7], (DEPTH, CONV_K, 3 * GDN_W), CONV_K),
        "gdn_a_log": a_log(ks[8], GDN_HEADS),
        "gdn_dt_bias": dt_bias(ks[9], GDN_HEADS),
        "gdn_norm_w": gain(ks[10], (DEPTH, GDN_HEAD_DIM)),
        "ssm_conv_w": dense(ks[11], (DEPTH, CONV_K, SSM_W + SSM_BC_W), CONV_K),
        "ssm_conv_b": 0.02 * jax.random.normal(ks[12], (DEPTH, SSM_W + SSM_BC_W), f32),
        "ssm_a_log": a_log(ks[13], SSM_HEADS),
        "ssm_dt_bias": dt_bias(ks[14], SSM_HEADS),
        "ssm_d": 1.0 + 0.1 * jax.random.normal(ks[15], (DEPTH, SSM_HEADS), f32),
        "ssm_norm_w": gain(ks[16], (DEPTH, SSM_W)),
        "w_out": dense(ks[17], (DEPTH, MIX_W, D_MODEL), MIX_W),
        "ffn2_norm": gain(ks[18], (DEPTH, D_MODEL)),
        "ffn2_w_gate": dense(ks[19], (DEPTH, D_MODEL, D_FF), D_MODEL),
        "ffn2_w_up": dense(ks[20], (DEPTH, D_MODEL, D_FF), D_MODEL),
        "ffn2_w_down": dense(ks[21], (DEPTH, D_FF, D_MODEL), D_FF),
        "final_norm": gain(ks[22], (D_MODEL,)),
    }


def reference(x, ffn1_norm, ffn1_w_gate, ffn1_w_up, ffn1_w_down, mix_norm, w_in,
              gdn_conv_w, gdn_a_log, gdn_dt_bias, gdn_norm_w,
              ssm_conv_w, ssm_conv_b, ssm_a_log, ssm_dt_bias, ssm_d, ssm_norm_w,
              w_out, ffn2_norm, ffn2_w_gate, ffn2_w_up, ffn2_w_down, final_norm):
    for i in range(DEPTH):
        x = x + 0.5 * swiglu(rms_norm(x, ffn1_norm[i]), ffn1_w_gate[i], ffn1_w_up[i], ffn1_w_down[i])
        x = x + hybrid_mixer(rms_norm(x, mix_norm[i]), w_in[i],
                             gdn_conv_w[i], gdn_a_log[i], gdn_dt_bias[i], gdn_norm_w[i],
                             ssm_conv_w[i], ssm_conv_b[i], ssm_a_log[i], ssm_dt_bias[i],
                             ssm_d[i], ssm_norm_w[i], w_out[i])
        x = x + 0.5 * swiglu(rms_norm(x, ffn2_norm[i]), ffn2_w_gate[i], ffn2_w_up[i], ffn2_w_down[i])
    return rms_norm(x, final_norm)
```

```python
import contextlib
import os
import numpy as np
import concourse.bass as bass
import concourse.mybir as mybir
from concourse.bass_utils import run_bass_kernel_spmd

F32 = mybir.dt.float32
BF16 = mybir.dt.bfloat16
AF = mybir.ActivationFunctionType
ALU = mybir.AluOpType
AX = mybir.AxisListType

D = 1024
DFF = 2816
SEQ = 8192
DEPTH = 2
INW = 6688
NB = 2
T = 128 * NB
EPS = 1e-6
BIG = float(os.environ.get("MK_BIG", "2000.0"))
KSLOT = 8
NSLOT = 6
LOOKAHEAD = 3
SEM_LIM = 16000
DBG = os.environ.get("MK_DBG", "F1,MX,F2").split(",")
MXSTOP = float(os.environ.get("MK_MXSTOP", "99"))
DUMP = os.environ.get("MK_DUMP", "")

O_QKV, O_GZ, O_GB, O_GA, O_SZ, O_XBC, O_DT = 0, 3072, 4096, 4104, 4112, 5136, 6672


class Buf:
    __slots__ = ("w", "r", "name", "excl")

    def __init__(self, name="", excl=False):
        self.w = {}
        self.r = {}
        self.name = name
        self.excl = excl


class Sched:
    def __init__(self, nc, es):
        self.nc = nc
        self.eng = {"pe": nc.tensor, "act": nc.scalar, "dve": nc.vector, "pool": nc.gpsimd, "sp": nc.sync}
        self.sems = {}
        self.cnt = {}
        self.seen = {e: {} for e in self.eng}
        self.es = es
        self.tot = {}
        for e in ("pe", "act", "dve", "pool"):
            self.tot[e] = 0
        self.ndma = 16
        for i in range(self.ndma):
            self.sems["d%d" % i] = es.enter_context(nc.semaphore("s_d%d" % i))
            self.cnt["d%d" % i] = 0
        self.dma_i = 0
        self.nwait = 0
        self.nins = 0

    def _need(self, eng, deps):
        sn = self.seen[eng]
        for k, c in deps.items():
            if sn.get(k, 0) < c:
                self.eng[eng].wait_ge(self.sems[k], c)
                sn[k] = c
                self.nwait += 1

    def _sync(self, eng, reads, writes, acc):
        deps = {}
        for b in reads:
            for k, c in b.w.items():
                if deps.get(k, 0) < c:
                    deps[k] = c
            if b.excl:
                for k, c in b.r.items():
                    if not k.startswith(eng + "_") and deps.get(k, 0) < c:
                        deps[k] = c
        for b in writes:
            for k, c in b.w.items():
                if acc and k.startswith("pe_"):
                    continue
                if deps.get(k, 0) < c:
                    deps[k] = c
            for k, c in b.r.items():
                if deps.get(k, 0) < c:
                    deps[k] = c
        self._need(eng, deps)

    def _mark(self, key, c, reads, writes):
        for b in reads:
            if b.r.get(key, 0) < c:
                b.r[key] = c
        for b in writes:
            b.w = {key: c}
            b.r = {}

    def op(self, eng, ins_fn, reads=(), writes=(), acc=False):
        self._sync(eng, reads, writes, acc)
        ins = ins_fn(self.eng[eng])
        key = "%s_%d" % (eng, self.tot[eng] // SEM_LIM)
        self.tot[eng] += 1
        if key not in self.sems:
            self.sems[key] = self.es.enter_context(self.nc.semaphore("s_" + key))
            self.cnt[key] = 0
        self.cnt[key] += 1
        ins.then_inc(self.sems[key], 1)
        self._mark(key, self.cnt[key], reads, writes)
        self.nins += 1
        return ins

    def dma(self, eng, out, in_, reads=(), writes=(), **kw):
        key = "d%d" % (self.dma_i % self.ndma)
        self.dma_i += 1
        if self.cnt[key] > 0:
            self._need(eng, {key: self.cnt[key]})
        self._sync(eng, reads, writes, False)
        ins = self.eng[eng].dma_start(out=out, in_=in_, **kw)
        self.cnt[key] += 16
        ins.then_inc(self.sems[key], 16)
        self._mark(key, self.cnt[key], reads, writes)
        self.nins += 1
        return ins

    def wait_all(self, eng, bufs):
        deps = {}
        for b in bufs:
            for dd in (b.w, b.r):
                for k, c in dd.items():
                    if deps.get(k, 0) < c:
                        deps[k] = c
        self._need(eng, deps)


def bcl(ap2, n):
    return ap2.unsqueeze(2).to_broadcast([ap2.shape[0], ap2.shape[1], n])


def bcm(ap2, h):
    return ap2.unsqueeze(1).to_broadcast([ap2.shape[0], h, ap2.shape[1]])


def build_program(n_tiles, depth=DEPTH):
    nc = bass.Bass("TRN2", target_bir_lowering=False)
    es = contextlib.ExitStack()
    with es:
        _build(nc, es, n_tiles, depth)
    return nc


def _build(nc, es, n_tiles, depth):
    S = Sched(nc, es)
    ntok = n_tiles * T

    def din(name, shape):
        return nc.dram_tensor(name, shape, F32, kind="ExternalInput").ap()

    x_d = din("x", [ntok, D])
    out_d = nc.dram_tensor("out", [ntok, D], F32, kind="ExternalOutput").ap()
    wnames = {"ffn1_w_gate": (D, DFF), "ffn1_w_up": (D, DFF), "ffn1_w_down": (DFF, D), "w_in": (D, INW),
              "w_out": (2 * D, D), "ffn2_w_gate": (D, DFF), "ffn2_w_up": (D, DFF), "ffn2_w_down": (DFF, D)}
    w_f32 = {k: din(k, [DEPTH, r, c]) for k, (r, c) in wnames.items()}
    w_bf = {k: nc.dram_tensor(k + "_bf", [DEPTH, r, c], BF16, kind="Internal").ap() for k, (r, c) in wnames.items()}
    w_buf = {(k, l): Buf("w_%s_%d" % (k, l)) for k in wnames for l in range(DEPTH)}
    pn = {"ffn1_norm": [DEPTH, D], "mix_norm": [DEPTH, D], "ffn2_norm": [DEPTH, D], "final_norm": [D],
          "gdn_conv_w": [DEPTH, 4, 3072], "gdn_a_log": [DEPTH, 8], "gdn_dt_bias": [DEPTH, 8],
          "gdn_norm_w": [DEPTH, 128], "ssm_conv_w": [DEPTH, 4, 1536], "ssm_conv_b": [DEPTH, 1536],
          "ssm_a_log": [DEPTH, 16], "ssm_dt_bias": [DEPTH, 16], "ssm_d": [DEPTH, 16], "ssm_norm_w": [DEPTH, D]}
    p_d = {k: din(k, s) for k, s in pn.items()}

    def sb(name, shape, dt=F32):
        return es.enter_context(nc.sbuf_tensor(name, shape, dt))

    def ps(name):
        return es.enter_context(nc.psum_tensor(name, [128, 512], F32))

    ident = sb("ident", [128, 128]); ones = sb("ones", [128, 128]); triu = sb("triu", [128, 128])
    ustrR = sb("ustrR", [128, 4, 128]); loR = sb("loR", [128, 4, 128])
    cB = Buf("consts")
    S.op("pool", lambda e: e.memset(ident[:], 0.0), writes=[cB])
    S.op("pool", lambda e: e.affine_select(out=ident[:], in_=ident[:], pattern=[[-1, 128]], compare_op=ALU.not_equal,
                                           fill=1.0, base=0, channel_multiplier=1), reads=[cB], writes=[cB])
    S.op("pool", lambda e: e.memset(ones[:], 1.0), writes=[cB])
    S.op("pool", lambda e: e.memset(triu[:], 1.0), writes=[cB])
    S.op("pool", lambda e: e.affine_select(out=triu[:], in_=triu[:], pattern=[[1, 128]], compare_op=ALU.is_ge,
                                           fill=0.0, base=0, channel_multiplier=-1), reads=[cB], writes=[cB])
    S.op("pool", lambda e: e.memset(ustrR[:], 0.0), writes=[cB])
    S.op("pool", lambda e: e.memset(loR[:], 0.0), writes=[cB])
    for r in range(4):
        S.op("pool", lambda e, r=r: e.affine_select(out=ustrR[:, r, :], in_=ustrR[:, r, :], pattern=[[-1, 128]],
                                                    compare_op=ALU.is_gt, fill=BIG, base=0, channel_multiplier=1),
             reads=[cB], writes=[cB])
        S.op("pool", lambda e, r=r: e.affine_select(out=loR[:, r, :], in_=loR[:, r, :], pattern=[[1, 128]],
                                                    compare_op=ALU.is_ge, fill=-BIG, base=0, channel_multiplier=-1),
             reads=[cB], writes=[cB])

    pb = [ps("pb%d" % i) for i in range(8)]
    pbB = [Buf("pb%d" % i, excl=True) for i in range(8)]

    NRA, NRB = 121, 68
    colsA = [sb("colsA%d" % l, [128, NRA]) for l in range(DEPTH)]
    colsB = [sb("colsB%d" % l, [128, NRB]) for l in range(DEPTH)]
    stg = sb("stg", [128, 128])
    stgB = Buf("stg")
    parB = Buf("params")
    fin_bc = sb("fin_bc", [128, D])
    S.dma("sp", fin_bc[:], p_d["final_norm"].partition_broadcast(128), writes=[parB])
    hb = [sb("hb%d" % l, [128, 64]) for l in range(DEPTH)]
    negA = [sb("negA%d" % l, [128, 24]) for l in range(DEPTH)]
    spb = [sb("spb%d" % l, [128, 32]) for l in range(DEPTH)]
    dsk = [sb("dsk%d" % l, [128, 16]) for l in range(DEPTH)]
    for l in range(DEPTH):
        def stage(rows_list, dst, nrows):
            r0 = 0
            for ap2 in rows_list:
                n = ap2.shape[0]
                S.dma("sp", stg[r0:r0 + n, :], ap2, writes=[stgB])
                r0 += n
            assert r0 == nrows, (r0, nrows)
            S.op("pe", lambda e: e.transpose(pb[0][:, 0:nrows], stg[0:nrows, :], ident[0:nrows, 0:nrows]),
                 reads=[stgB, cB], writes=[pbB[0]])
            S.op("dve", lambda e: e.tensor_copy(out=dst[:, 0:nrows], in_=pb[0][:, 0:nrows]), reads=[pbB[0]], writes=[parB])

        stage([p_d["gdn_conv_w"][l].rearrange("k (c p) -> (k c) p", p=128),
               p_d["ffn1_norm"][l].rearrange("(c p) -> c p", p=128),
               p_d["mix_norm"][l].rearrange("(c p) -> c p", p=128),
               p_d["ffn2_norm"][l].rearrange("(c p) -> c p", p=128),
               p_d["gdn_norm_w"][l].rearrange("(c p) -> c p", p=128)], colsA[l], NRA)
        stage([p_d["ssm_conv_w"][l].rearrange("k (c p) -> (k c) p", p=128),
               p_d["ssm_conv_b"][l].rearrange("(c p) -> c p", p=128),
               p_d["ssm_norm_w"][l].rearrange("(c p) -> c p", p=128)], colsB[l], NRB)
        S.dma("sp", hb[l][:, 0:8], p_d["gdn_a_log"][l].partition_broadcast(128), writes=[parB])
        S.dma("sp", hb[l][:, 8:24], p_d["ssm_a_log"][l].partition_broadcast(128), writes=[parB])
        S.op("pool", lambda e: e.memset(spb[l][:, 0:8], 0.0), writes=[parB])
        S.dma("sp", spb[l][:, 8:16], p_d["gdn_dt_bias"][l].partition_broadcast(128), writes=[parB])
        S.dma("sp", spb[l][:, 16:32], p_d["ssm_dt_bias"][l].partition_broadcast(128), writes=[parB])
        S.dma("sp", dsk[l][:, :], p_d["ssm_d"][l].partition_broadcast(128), writes=[parB])
        S.op("act", lambda e: e.activation(out=negA[l][:, :], in_=hb[l][:, 0:24], func=AF.Exp), reads=[parB], writes=[parB])
        S.op("dve", lambda e: e.tensor_scalar(out=negA[l][:, :], in0=negA[l][:, :], scalar1=-1.0, scalar2=None, op0=ALU.mult),
             reads=[parB], writes=[parB])

    def colA(l, kind, c=0):
        base = {"conv": 0, "ffn1": 96, "mix": 104, "ffn2": 112, "gnorm": 120}[kind]
        return colsA[l][:, base + c:base + c + 1]

    cast_order = ["ffn1_w_gate", "ffn1_w_up", "ffn1_w_down", "w_in", "w_out", "ffn2_w_gate", "ffn2_w_up", "ffn2_w_down"]
    for l in range(depth):
        for k in cast_order:
            r, c = wnames[k]
            step = 256
            for r0 in range(0, r, step):
                S.dma("pool", w_bf[k][l, r0:r0 + step, :], w_f32[k][l, r0:r0 + step, :], writes=[], reads=[])
                key = "d%d" % ((S.dma_i - 1) % S.ndma)
                w_buf[(k, l)].w[key] = S.cnt[key]

    slots = [sb("wslot%d" % i, [128, KSLOT, 512], BF16) for i in range(NSLOT)]
    slotB = [Buf("slot%d" % i) for i in range(NSLOT)]

    def wview(k, l):
        return w_bf[k][l].rearrange("(k p) f -> p k f", p=128)

    def ffn_specs(l, which):
        sp = []
        g, u, dn = ("ffn%d_w_gate" % which, "ffn%d_w_up" % which, "ffn%d_w_down" % which)
        for c0 in range(0, DFF, 512):
            cn = min(512, DFF - c0)
            sp.append([((g, l), 0, 8, c0, cn, 0)])
            sp.append([((u, l), 0, 8, c0, cn, 0)])
        for half in range(2):
            sp.append([((dn, l), 0, 8, half * 512, 512, 0)])
            sp.append([((dn, l), 8, 8, half * 512, 512, 0)])
            sp.append([((dn, l), 16, 6, half * 512, 512, 0)])
        return sp

    def mix_specs(l):
        sp = [[(("w_in", l), 0, 8, O_GB, 16, 0), (("w_in", l), 0, 8, O_DT, 16, 16)]]
        for c0 in range(0, 3072, 512):
            sp.append([(("w_in", l), 0, 8, O_QKV + c0, 512, 0)])
        for c0 in range(0, 1536, 512):
            sp.append([(("w_in", l), 0, 8, O_XBC + c0, 512, 0)])
        for c0 in (O_GZ, O_GZ + 512, O_SZ, O_SZ + 512):
            sp.append([(("w_in", l), 0, 8, c0, 512, 0)])
        for half in range(2):
            sp.append([(("w_out", l), 0, 8, half * 512, 512, 0)])
            sp.append([(("w_out", l), 8, 8, half * 512, 512, 0)])
        return sp

    schedule = []
    for t in range(n_tiles):
        for l in range(depth):
            schedule += ffn_specs(l, 1) + mix_specs(l) + ffn_specs(l, 2)
    wst = {"issued": 0, "taken": 0}

    def w_issue():
        i = wst["issued"]
        if i >= len(schedule):
            return
        slot = i % NSLOT
        for (wk, k0, kn, c0, cn, dc) in schedule[i]:
            S.dma("sp", slots[slot][:, 0:kn, dc:dc + cn], wview(*wk)[:, k0:k0 + kn, c0:c0 + cn],
                  reads=[w_buf[wk]], writes=[])
            key = "d%d" % ((S.dma_i - 1) % S.ndma)
            slotB[slot].w[key] = S.cnt[key]
        wst["issued"] += 1

    def w_issue_guarded():
        i = wst["issued"]
        if i >= len(schedule):
            return
        slot = i % NSLOT
        S.wait_all("sp", [slotB[slot]])
        slotB[slot].w = {}
        slotB[slot].r = {}
        w_issue()

    def w_next(expect):
        i = wst["taken"]
        assert schedule[i] == expect, (i, schedule[i], expect)
        while wst["issued"] < min(len(schedule), i + 1 + LOOKAHEAD):
            w_issue_guarded()
        wst["taken"] += 1
        return slots[i % NSLOT], slotB[i % NSLOT]

    x_sb = sb("x_sb", [128, NB, D]); xB = [Buf("x%d" % b) for b in range(NB)]
    fscr = [sb("fscr%d" % i, [128, D]) for i in range(6)]; fB = [Buf("fscr%d" % i) for i in range(6)]
    hscr = [sb("hscr%d" % i, [128, D], BF16) for i in range(5)]; hB = [Buf("hscr%d" % i) for i in range(5)]
    pt32 = sb("pt32", [128, D]); pt32B = Buf("pt32")
    hT = sb("hT", [128, 8, T], BF16); hTB = Buf("hT")
    zh = sb("zh", [128, NB, 2048])
    zhB = Buf("zh")
    hid = zh[:].rearrange("p b f -> p (b f)").bitcast(BF16)[:, 0:22 * T].rearrange("p (c t) -> p c t", t=T)
    stat = sb("stat", [128, 64]); statB = Buf("stat")
    qkv_tm = sb("qkv_tm", [128, NB, 3072]); qkvB = [Buf("qkv%d" % b) for b in range(NB)]
    xs_tm = sb("xs_tm", [128, NB, 1024]); xsB = [Buf("xs%d" % b) for b in range(NB)]
    b_tm = sb("b_tm", [128, NB, 256], BF16); btmB = [Buf("btm%d" % b) for b in range(NB)]
    bT = sb("bT", [128, 2, T], BF16); cT = sb("cT", [128, 2, T], BF16); bcTB = Buf("bcT")
    mixT = sb("mixT", [128, 16, T], BF16); mixTB = [Buf("mixT%d" % b) for b in range(NB)]
    xbuf = [sb("xbuf%d" % i, [128, T + 3]) for i in range(2)]; xbufB = [Buf("xbuf%d" % i) for i in range(2)]
    cacc = [sb("cacc%d" % i, [128, T]) for i in range(2)]; caccB = [Buf("cacc%d" % i) for i in range(2)]
    csil = [sb("csil%d" % i, [128, T]) for i in range(2)]; csilB = [Buf("csil%d" % i) for i in range(2)]
    tails = [sb("tails%d" % l, [128, 36, 3]) for l in range(DEPTH)]
    tailB = [[Buf("tail%d_%d" % (l, c)) for c in range(36)] for l in range(DEPTH)]
    sm = sb("sm", [128, NB, 32]); spv = sb("spv", [128, NB, 32]); beta = sb("beta", [128, NB, 8])
    gcat = sb("gcat", [128, NB, 24]); gct = sb("gct", [128, NB, 64]); egc = sb("egc", [128, NB, 24])
    edl = sb("edl", [128, NB, 24]); etot = sb("etot", [128, NB, 24]); biasA = sb("biasA", [128, NB, 8])
    ngc = sb("ngc", [128, NB, 24]); bg = sb("bg", [128, NB, 8]); dte = sb("dte", [128, NB, 16])
    smB = Buf("small")
    blk_small = sb("blk_small", [128, 64]); bsB = Buf("blk_small")
    Sg = [sb("Sg%d" % l, [128, 8, 128]) for l in range(DEPTH)]; SgB = [Buf("Sg%d" % l) for l in range(DEPTH)]
    Ss = [sb("Ss%d" % l, [128, 16, 64]) for l in range(DEPTH)]; SsB = [Buf("Ss%d" % l) for l in range(DEPTH)]
    for l in range(DEPTH):
        S.op("pool", lambda e: e.memset(Sg[l][:], 0.0), writes=[SgB[l]])
        S.op("pool", lambda e: e.memset(Ss[l][:], 0.0), writes=[SsB[l]])
        S.op("pool", lambda e: e.memset(tails[l][:], 0.0), writes=tailB[l])
    Sgb = sb("Sgb", [128, 8, 128], BF16); SgbB = Buf("Sgb")
    Ssb = sb("Ssb", [128, 16, 64], BF16); SsbB = Buf("Ssb")

    f3 = lambda ap, h: ap.rearrange("p (h e) -> p h e", h=h)

    def transpose_f32(src_ap, srcB, bank, col0, reads_extra=()):
        S.op("pe", lambda e: e.transpose(pb[bank][:, col0:col0 + 128], src_ap, ident[:]),
             reads=[srcB, cB] + list(reads_extra), writes=[pbB[bank]], acc=True)

    def rsqrt_small(ap, n, scale, readsB, writesB):
        S.op("act", lambda e: e.activation(out=ap, in_=ap, func=AF.Ln, bias=EPS, scale=scale), reads=readsB, writes=writesB)
        S.op("act", lambda e: e.activation(out=ap, in_=ap, func=AF.Exp, scale=-0.5), reads=writesB, writes=writesB)

    def norm_T(wcol_fn):
        for b in range(NB):
            S.op("act", lambda e: e.activation(out=fscr[0][:], in_=x_sb[:, b, :], func=AF.Square, accum_out=stat[:, b:b + 1]),
                 reads=[xB[b]], writes=[fB[0], statB])
        rsqrt_small(stat[:, 0:NB], NB, 1.0 / D, [statB], [statB])
        for b in range(NB):
            S.op("dve", lambda e: e.tensor_scalar(out=fscr[1][:], in0=x_sb[:, b, :], scalar1=stat[:, b:b + 1], scalar2=None,
                                                  op0=ALU.mult), reads=[xB[b], statB], writes=[fB[1]])
            for half in range(2):
                bank = 6 + half
                for c in range(4):
                    cc = half * 4 + c
                    transpose_f32(fscr[1][:, cc * 128:(cc + 1) * 128], fB[1], bank, c * 128)
                for c in range(4):
                    cc = half * 4 + c
                    S.op("act", lambda e: e.activation(out=hT[:, cc, b * 128:(b + 1) * 128], in_=pb[bank][:, c * 128:(c + 1) * 128],
                                                       func=AF.Identity, scale=wcol_fn(cc)), reads=[pbB[bank], parB], writes=[hTB])

    def ffn(l, which):
        kind = "ffn%d" % which
        norm_T(lambda cc: colA(l, kind, cc))
        specs = ffn_specs(l, which)
        si = 0
        hidB = zhB
        for c0 in range(0, DFF, 512):
            cn = min(512, DFF - c0)
            gs, gsB = w_next(specs[si]); us, usB = w_next(specs[si + 1]); si += 2
            for c in range(cn // 128):
                fc = c0 // 128 + c
                pg, pu = (0, 1) if fc % 2 == 0 else (2, 3)
                for k in range(8):
                    S.op("pe", lambda e: e.matmul(pb[pg][:, 0:T], lhsT=gs[:, k, c * 128:(c + 1) * 128], rhs=hT[:, k, :],
                                                  start=(k == 0), stop=(k == 7)), reads=[gsB, hTB], writes=[pbB[pg]], acc=(k > 0))
                for k in range(8):
                    S.op("pe", lambda e: e.matmul(pb[pu][:, 0:T], lhsT=us[:, k, c * 128:(c + 1) * 128], rhs=hT[:, k, :],
                                                  start=(k == 0), stop=(k == 7)), reads=[usB, hTB], writes=[pbB[pu]], acc=(k > 0))
                sc = fscr[2 + fc % 2]; scB = fB[2 + fc % 2]
                S.op("act", lambda e: e.activation(out=sc[:, 0:T], in_=pb[pg][:, 0:T], func=AF.Silu), reads=[pbB[pg]], writes=[scB])
                S.op("dve", lambda e: e.tensor_tensor(out=hid[:, fc, :], in0=pb[pu][:, 0:T], in1=sc[:, 0:T], op=ALU.mult),
                     reads=[pbB[pu], scB], writes=[hidB])
        for half in range(2):
            dsl = [w_next(specs[si]), w_next(specs[si + 1]), w_next(specs[si + 2])]; si += 3
            for b in range(NB):
                bank = 4 + b % 2
                for c in range(22):
                    ws, wsB = dsl[c // 8]
                    S.op("pe", lambda e: e.matmul(pb[bank][:, :], lhsT=hid[:, c, b * 128:(b + 1) * 128], rhs=ws[:, c % 8, :],
                                                  start=(c == 0), stop=(c == 21)), reads=[wsB, hidB], writes=[pbB[bank]], acc=(c > 0))
                xs_ = x_sb[:, b, half * 512:(half + 1) * 512]
                S.op("dve", lambda e: e.scalar_tensor_tensor(out=xs_, in0=pb[bank][:, :], scalar=0.5, in1=xs_, op0=ALU.mult, op1=ALU.add),
                     reads=[pbB[bank], xB[b]], writes=[xB[b]])

    def mixer(l):
        norm_T(lambda cc: colA(l, "mix", cc))
        specs = mix_specs(l)
        si = 0
        ws, wsB = w_next(specs[si]); si += 1
        for b in range(NB):
            for k in range(8):
                S.op("pe", lambda e: e.matmul(pb[0][:, 0:32], lhsT=hT[:, k, b * 128:(b + 1) * 128], rhs=ws[:, k, 0:32],
                                              start=(k == 0), stop=(k == 7)), reads=[wsB, hTB], writes=[pbB[0]], acc=(k > 0))
            S.op("dve", lambda e: e.tensor_copy(out=sm[:, b, :], in_=pb[0][:, 0:32]), reads=[pbB[0]], writes=[smB])
        S.op("act", lambda e: e.activation(out=beta[:], in_=sm[:, :, 0:8], func=AF.Tanh, scale=0.5), reads=[smB], writes=[smB])
        S.op("dve", lambda e: e.tensor_scalar(out=beta[:], in0=beta[:], scalar1=0.5, scalar2=0.5, op0=ALU.mult, op1=ALU.add),
             reads=[smB], writes=[smB])
        S.op("dve", lambda e: e.tensor_tensor(out=sm[:], in0=sm[:], in1=bcm(spb[l][:, :], NB), op=ALU.add), reads=[smB, parB], writes=[smB])
        S.op("dve", lambda e: e.tensor_scalar(out=sm[:, :, 0:8], in0=sm[:, :, 0:8], scalar1=-1.0, scalar2=None, op0=ALU.mult),
             reads=[smB], writes=[smB])
        S.op("act", lambda e: e.activation(out=spv[:], in_=sm[:], func=AF.Exp), reads=[smB], writes=[smB])
        S.op("act", lambda e: e.activation(out=spv[:], in_=spv[:], func=AF.Ln, bias=1.0, scale=1.0), reads=[smB], writes=[smB])
        S.op("dve", lambda e: e.tensor_tensor(out=gcat[:], in0=spv[:, :, 8:32], in1=bcm(negA[l][:, :], NB), op=ALU.mult),
             reads=[smB, parB], writes=[smB])
        for b in range(NB):
            S.op("pe", lambda e: e.matmul(pb[0][:, 0:24], lhsT=triu[:], rhs=gcat[:, b, :], start=True, stop=True),
                 reads=[smB, cB], writes=[pbB[0]])
            S.op("pe", lambda e: e.matmul(pb[0][:, 32:56], lhsT=ones[:], rhs=gcat[:, b, :], start=True, stop=True),
                 reads=[smB, cB], writes=[pbB[0]], acc=True)
            S.op("dve", lambda e: e.tensor_copy(out=gct[:, b, 0:56], in_=pb[0][:, 0:56]), reads=[pbB[0]], writes=[smB])
        S.op("act", lambda e: e.activation(out=egc[:], in_=gct[:, :, 0:24], func=AF.Exp), reads=[smB], writes=[smB])
        S.op("act", lambda e: e.activation(out=etot[:], in_=gct[:, :, 32:56], func=AF.Exp), reads=[smB], writes=[smB])
        S.op("dve", lambda e: e.tensor_tensor(out=edl[:], in0=gct[:, :, 32:56], in1=gct[:, :, 0:24], op=ALU.subtract), reads=[smB], writes=[smB])
        S.op("act", lambda e: e.activation(out=edl[:], in_=edl[:], func=AF.Exp), reads=[smB], writes=[smB])
        S.op("dve", lambda e: e.tensor_scalar(out=ngc[:], in0=gct[:, :, 0:24], scalar1=-1.0, scalar2=None, op0=ALU.mult), reads=[smB], writes=[smB])
        S.op("dve", lambda e: e.tensor_tensor(out=biasA[:], in0=gct[:, :, 0:8], in1=spv[:, :, 0:8], op=ALU.subtract), reads=[smB], writes=[smB])
        S.op("dve", lambda e: e.tensor_tensor(out=bg[:], in0=beta[:], in1=egc[:, :, 0:8], op=ALU.mult), reads=[smB], writes=[smB])
        S.op("dve", lambda e: e.tensor_tensor(out=dte[:], in0=spv[:, :, 16:32], in1=edl[:, :, 8:24], op=ALU.mult), reads=[smB], writes=[smB])

        if MXSTOP < 2:
            [w_next(s_) for s_ in specs[si:]]
            return
        chunk_i = 0
        for grp in range(9):
            ws, wsB = w_next(specs[si]); si += 1
            for c in range(4):
                fc = grp * 4 + c
                i2 = chunk_i % 2; chunk_i += 1
                bank = i2
                for k in range(8):
                    S.op("pe", lambda e: e.matmul(pb[bank][:, 0:T], lhsT=ws[:, k, c * 128:(c + 1) * 128], rhs=hT[:, k, :],
                                                  start=(k == 0), stop=(k == 7)), reads=[wsB, hTB], writes=[pbB[bank]], acc=(k > 0))
                xb, xbB = xbuf[i2], xbufB[i2]
                S.op("act", lambda e: e.copy(out=xb[:, 3:3 + T], in_=pb[bank][:, 0:T]), reads=[pbB[bank]], writes=[xbB])
                S.op("pool", lambda e: e.tensor_copy(out=xb[:, 0:3], in_=tails[l][:, fc, :]), reads=[tailB[l][fc]], writes=[xbB])
                S.op("pool", lambda e: e.tensor_copy(out=tails[l][:, fc, :], in_=xb[:, T:T + 3]), reads=[xbB], writes=[tailB[l][fc]])
                if fc < 24:
                    wc = lambda kk: colsA[l][:, kk * 24 + fc:kk * 24 + fc + 1]
                else:
                    wc = lambda kk: colsB[l][:, kk * 12 + (fc - 24):kk * 12 + (fc - 24) + 1]
                ca, caB = cacc[i2], caccB[i2]
                S.op("pool", lambda e: e.tensor_scalar(out=ca[:], in0=xb[:, 0:T], scalar1=wc(0), scalar2=None, op0=ALU.mult),
                     reads=[xbB, parB], writes=[caB])
                cs_, csB_ = csil[i2], csilB[i2]
                S.op("pool", lambda e: e.tensor_scalar(out=cs_[:], in0=xb[:, 1:1 + T], scalar1=wc(1), scalar2=None, op0=ALU.mult),
                     reads=[xbB, parB], writes=[csB_])
                S.op("pool", lambda e: e.tensor_tensor(out=ca[:], in0=ca[:], in1=cs_[:], op=ALU.add), reads=[csB_, caB], writes=[caB])
                S.op("dve", lambda e: e.scalar_tensor_tensor(out=ca[:], in0=xb[:, 2:2 + T], scalar=wc(2), in1=ca[:], op0=ALU.mult, op1=ALU.add),
                     reads=[xbB, parB, caB], writes=[caB])
                S.op("dve", lambda e: e.scalar_tensor_tensor(out=ca[:], in0=xb[:, 3:3 + T], scalar=wc(3), in1=ca[:], op0=ALU.mult, op1=ALU.add),
                     reads=[xbB, parB, caB], writes=[caB])
                cs_, csB_ = csil[i2], csilB[i2]
                if fc < 24:
                    S.op("act", lambda e: e.activation(out=cs_[:], in_=ca[:], func=AF.Silu), reads=[caB], writes=[csB_])
                else:
                    S.op("act", lambda e: e.activation(out=cs_[:], in_=ca[:], func=AF.Silu, bias=colsB[l][:, 48 + fc - 24:48 + fc - 24 + 1]),
                         reads=[caB, parB], writes=[csB_])
                if fc >= 32:
                    dst = bT if fc < 34 else cT
                    S.op("dve", lambda e: e.tensor_copy(out=dst[:, fc % 2, :], in_=cs_[:]), reads=[csB_], writes=[bcTB])
                if fc < 34:
                    tb = 2 + i2
                    for b in range(NB):
                        transpose_f32(cs_[:, b * 128:(b + 1) * 128], csB_, tb, b * 128)
                    for b in range(NB):
                        src = pb[tb][:, b * 128:(b + 1) * 128]
                        if fc < 24:
                            S.op("act", lambda e: e.copy(out=qkv_tm[:, b, fc * 128:(fc + 1) * 128], in_=src), reads=[pbB[tb]], writes=[qkvB[b]])
                        elif fc < 32:
                            S.op("act", lambda e: e.copy(out=xs_tm[:, b, (fc - 24) * 128:(fc - 23) * 128], in_=src), reads=[pbB[tb]], writes=[xsB[b]])
                        else:
                            S.op("act", lambda e: e.copy(out=b_tm[:, b, (fc - 32) * 128:(fc - 31) * 128], in_=src), reads=[pbB[tb]], writes=[btmB[b]])

        if MXSTOP < 3:
            [w_next(s_) for s_ in specs[si:]]
            return
        for zi in range(4):
            ws, wsB = w_next(specs[si]); si += 1
            for b in range(NB):
                bank = 4 + b % 2
                for k in range(8):
                    S.op("pe", lambda e: e.matmul(pb[bank][:, :], lhsT=hT[:, k, b * 128:(b + 1) * 128], rhs=ws[:, k, :],
                                                  start=(k == 0), stop=(k == 7)), reads=[wsB, hTB], writes=[pbB[bank]], acc=(k > 0))
                S.op("act", lambda e: e.activation(out=zh[:, b, zi * 512:(zi + 1) * 512], in_=pb[bank][:, :], func=AF.Silu),
                     reads=[pbB[bank]], writes=[zhB])

        if MXSTOP < 4:
            [w_next(s_) for s_ in specs[si:]]
            return
        for b in range(NB):
            if MXSTOP >= 4.0:
                gdn_block(l, b)
            if MXSTOP >= 4.2:
                ssd_block(l, b)
        if MXSTOP < 5:
            [w_next(s_) for s_ in specs[si:]]
            return

        wo = []
        for half in range(2):
            o0 = w_next(specs[si]); o1 = w_next(specs[si + 1]); si += 2
            for b in range(NB):
                bank = 4 + b % 2
                for c in range(16):
                    ws, wsB = o0 if c < 8 else o1
                    S.op("pe", lambda e: e.matmul(pb[bank][:, :], lhsT=mixT[:, c, b * 128:(b + 1) * 128], rhs=ws[:, c % 8, :],
                                                  start=(c == 0), stop=(c == 15)), reads=[wsB, mixTB[b]], writes=[pbB[bank]], acc=(c > 0))
                xs_ = x_sb[:, b, half * 512:(half + 1) * 512]
                S.op("dve", lambda e: e.tensor_tensor(out=xs_, in0=pb[bank][:, :], in1=xs_, op=ALU.add), reads=[pbB[bank], xB[b]], writes=[xB[b]])

    def gdn_block(l, b):
        q = qkv_tm[:, b, 0:1024]; k = qkv_tm[:, b, 1024:2048]; v = qkv_tm[:, b, 2048:3072]
        S.op("pool", lambda e: e.tensor_tensor(out=fscr[0][:], in0=q, in1=q, op=ALU.mult), reads=[qkvB[b]], writes=[fB[0]])
        S.op("dve", lambda e: e.tensor_reduce(out=blk_small[:, 0:8], in_=f3(fscr[0][:], 8), axis=AX.X, op=ALU.add), reads=[fB[0]], writes=[bsB])
        S.op("pool", lambda e: e.tensor_tensor(out=fscr[1][:], in0=k, in1=k, op=ALU.mult), reads=[qkvB[b]], writes=[fB[1]])
        S.op("dve", lambda e: e.tensor_reduce(out=blk_small[:, 8:16], in_=f3(fscr[1][:], 8), axis=AX.X, op=ALU.add), reads=[fB[1]], writes=[bsB])
        rsqrt_small(blk_small[:, 0:16], 16, 1.0, [bsB], [bsB])
        S.op("dve", lambda e: e.tensor_scalar(out=blk_small[:, 0:8], in0=blk_small[:, 0:8], scalar1=128.0 ** -0.5, scalar2=None, op0=ALU.mult),
             reads=[bsB], writes=[bsB])
        S.op("dve", lambda e: e.tensor_tensor(out=blk_small[:, 16:24], in0=blk_small[:, 8:16], in1=edl[:, b, 0:8], op=ALU.mult),
             reads=[bsB, smB], writes=[bsB])
        qn, qnB = fscr[0], fB[0]
        kn, knB = fscr[1], fB[1]
        S.op("dve", lambda e: e.tensor_tensor(out=f3(qn[:], 8), in0=f3(q, 8), in1=bcl(blk_small[:, 0:8], 128), op=ALU.mult),
             reads=[qkvB[b], bsB], writes=[qnB])
        S.op("pool", lambda e: e.tensor_tensor(out=f3(kn[:], 8), in0=f3(k, 8), in1=bcl(blk_small[:, 8:16], 128), op=ALU.mult),
             reads=[qkvB[b], bsB], writes=[knB])
        kdec, kdecB = hscr[0], hB[0]
        S.op("dve", lambda e: e.tensor_tensor(out=f3(kdec[:], 8), in0=f3(k, 8), in1=bcl(blk_small[:, 16:24], 128), op=ALU.mult),
             reads=[qkvB[b], bsB], writes=[kdecB])
        vb, vbB = fscr[2], fB[2]
        S.op("pool", lambda e: e.tensor_tensor(out=f3(vb[:], 8), in0=f3(v, 8), in1=bcl(beta[:, b, :], 128), op=ALU.mult),
             reads=[qkvB[b], smB], writes=[vbB])
        if MXSTOP < 4.01:
            return
        qT, qTB = hscr[1], hB[1]
        kT, kTB = hscr[2], hB[2]
        for (src, srcB, dst, dstB, banks) in ((qn, qnB, qT, qTB, (0, 1)), (kn, knB, kT, kTB, (2, 3))):
            for hh in range(2):
                for c in range(4):
                    transpose_f32(src[:, (hh * 4 + c) * 128:(hh * 4 + c + 1) * 128], srcB, banks[hh], c * 128)
                S.op("act", lambda e: e.copy(out=dst[:, hh * 512:(hh + 1) * 512], in_=pb[banks[hh]][:, :]), reads=[pbB[banks[hh]]], writes=[dstB])
        if MXSTOP < 4.02:
            return
        tg, tgB = fscr[3], fB[3]
        S.op("dve", lambda e: e.tensor_tensor(out=f3(tg[:], 8), in0=bcm(triu[:], 8), in1=bcl(gcat[:, b, 0:8], 128), op=ALU.mult),
             reads=[cB, smB], writes=[tgB])
        if MXSTOP < 4.021:
            return
        dS, dSB = fscr[4], fB[4]
        dT, dTB = fscr[5], fB[5]
        for hh in range(2):
            for (mask, bank) in ((ustrR, 4 + hh), (loR, 6 + hh)):
                S.op("pe", lambda e: e.matmul(pb[bank][:, :], lhsT=ones[:], rhs=tg[:, hh * 512:(hh + 1) * 512], start=True, stop=False),
                     reads=[tgB, cB], writes=[pbB[bank]])
                S.op("pe", lambda e: e.matmul(pb[bank][:, :], lhsT=ident[:], rhs=mask[:].rearrange("p r j -> p (r j)"), start=False, stop=True),
                     reads=[cB], writes=[pbB[bank]], acc=True)
            if MXSTOP < 4.022:
                continue
            for c in range(4):
                h = hh * 4 + c
                S.op("act", lambda e: e.activation(out=dS[:, h * 128:(h + 1) * 128], in_=pb[4 + hh][:, c * 128:(c + 1) * 128], func=AF.Exp,
                                                   scale=-1.0, bias=biasA[:, b, h:h + 1]), reads=[pbB[4 + hh], smB], writes=[dSB])
                S.op("act", lambda e: e.activation(out=dT[:, h * 128:(h + 1) * 128], in_=pb[6 + hh][:, c * 128:(c + 1) * 128], func=AF.Exp,
                                                   scale=1.0, bias=ngc[:, b, h:h + 1]), reads=[pbB[6 + hh], smB], writes=[dTB])
        if MXSTOP < 4.03:
            return
        for hh in range(2):
            for c in range(4):
                h = hh * 4 + c
                S.op("pe", lambda e: e.matmul(pb[0 + hh][:, c * 128:(c + 1) * 128], lhsT=kT[:, h * 128:(h + 1) * 128], rhs=kT[:, h * 128:(h + 1) * 128],
                                              start=True, stop=True), reads=[kTB], writes=[pbB[0 + hh]], acc=(c > 0))
            for c in range(4):
                h = hh * 4 + c
                S.op("pe", lambda e: e.matmul(pb[2 + hh][:, c * 128:(c + 1) * 128], lhsT=kT[:, h * 128:(h + 1) * 128], rhs=qT[:, h * 128:(h + 1) * 128],
                                              start=True, stop=True), reads=[kTB, qTB], writes=[pbB[2 + hh]], acc=(c > 0))
        qkm, qkmB = hscr[3], hB[3]
        for hh in range(2):
            sl = slice(hh * 512, (hh + 1) * 512)
            S.op("dve", lambda e: e.scalar_tensor_tensor(out=dS[:, sl], in0=pb[0 + hh][:, :], scalar=-1.0, in1=dS[:, sl], op0=ALU.mult, op1=ALU.mult),
                 reads=[pbB[0 + hh], dSB], writes=[dSB])
            S.op("dve", lambda e: e.tensor_tensor(out=qkm[:, sl], in0=pb[2 + hh][:, :], in1=dT[:, sl], op=ALU.mult),
                 reads=[pbB[2 + hh], dTB], writes=[qkmB])
        if MXSTOP < 4.04:
            return
        Pb, PbB = dS, dSB
        Ptb, PtbB = pt32, pt32B
        Ntb, NtbB = dT, dTB
        Nt, NtB = dT, dTB
        if MXSTOP < 4.0401:
            return
        for hh in range(2):
            for c in range(4):
                transpose_f32(dS[:, (hh * 4 + c) * 128:(hh * 4 + c + 1) * 128], dSB, 0 + hh, c * 128)
            sl = slice(hh * 512, (hh + 1) * 512)
            if MXSTOP < 4.0402:
                continue
            S.op("act", lambda e: e.copy(out=Ptb[:, sl], in_=pb[0 + hh][:, :]), reads=[pbB[0 + hh]], writes=[PtbB])
            if MXSTOP < 4.0403:
                continue
            S.op("dve", lambda e: e.tensor_copy(out=Nt[:, sl], in_=pb[0 + hh][:, :]), reads=[pbB[0 + hh], qkmB], writes=[NtB])
        if MXSTOP < 4.0404:
            return
        if MXSTOP < 4.041:
            return
        for lev in range(int(os.environ.get("MK_NLEV", "6"))):
            for hh in range(2):
                for c in range(4):
                    h = hh * 4 + c
                    hs = slice(h * 128, (h + 1) * 128)
                    S.op("pe", lambda e: e.matmul(pb[0 + hh][:, c * 128:(c + 1) * 128], lhsT=Ptb[:, hs], rhs=Pb[:, hs], start=True, stop=True),
                         reads=[PtbB, PbB], writes=[pbB[0 + hh]], acc=(c > 0))
                for c in range(4):
                    h = hh * 4 + c
                    hs = slice(h * 128, (h + 1) * 128)
                    S.op("pe", lambda e: e.matmul(pb[2 + hh][:, c * 128:(c + 1) * 128], lhsT=Pb[:, hs], rhs=Ptb[:, hs], start=True, stop=True),
                         reads=[PtbB, PbB], writes=[pbB[2 + hh]], acc=(c > 0))
            for hh in range(2):
                sl = slice(hh * 512, (hh + 1) * 512)
                S.op("act", lambda e: e.copy(out=Pb[:, sl], in_=pb[0 + hh][:, :]), reads=[pbB[0 + hh]], writes=[PbB])
                S.op("act", lambda e: e.copy(out=Ptb[:, sl], in_=pb[2 + hh][:, :]), reads=[pbB[2 + hh]], writes=[PtbB])
            if MXSTOP < 4.042:
                continue
            for hh in range(2):
                for c in range(4):
                    h = hh * 4 + c
                    hs = slice(h * 128, (h + 1) * 128)
                    S.op("pe", lambda e: e.matmul(pb[4 + hh][:, c * 128:(c + 1) * 128], lhsT=Pb[:, hs], rhs=Ntb[:, hs], start=True, stop=True),
                         reads=[NtbB, PbB], writes=[pbB[4 + hh]], acc=(c > 0))
            for hh in range(2):
                sl = slice(hh * 512, (hh + 1) * 512)
                S.op("dve", lambda e: e.tensor_tensor(out=Nt[:, sl], in0=pb[2 + hh][:, :], in1=Nt[:, sl], op=ALU.add), reads=[pbB[2 + hh], NtB], writes=[NtB])
                S.op("dve", lambda e: e.tensor_tensor(out=Nt[:, sl], in0=pb[4 + hh][:, :], in1=Nt[:, sl], op=ALU.add), reads=[pbB[4 + hh], NtB], writes=[NtB])
        TtB, TtBB = Nt, NtB
        S.op("dve", lambda e: e.tensor_tensor(out=f3(TtB[:], 8), in0=f3(Nt[:], 8), in1=bcm(ident[:], 8), op=ALU.add), reads=[NtB, cB], writes=[TtBB])
        if MXSTOP < 4.05:
            return
        S.op("act", lambda e: e.copy(out=Sgb[:].rearrange("p h e -> p (h e)"), in_=Sg[l][:].rearrange("p h e -> p (h e)")), reads=[SgB[l]], writes=[SgbB])
        for hh in range(2):
            for c in range(4):
                h = hh * 4 + c
                hs = slice(h * 128, (h + 1) * 128)
                S.op("pe", lambda e: e.matmul(pb[0 + hh][:, c * 128:(c + 1) * 128], lhsT=kT[:, hs], rhs=Sgb[:, h, :], start=True, stop=True),
                     reads=[kTB, SgbB], writes=[pbB[0 + hh]], acc=(c > 0))
            for c in range(4):
                h = hh * 4 + c
                hs = slice(h * 128, (h + 1) * 128)
                S.op("pe", lambda e: e.matmul(pb[2 + hh][:, c * 128:(c + 1) * 128], lhsT=qT[:, hs], rhs=Sgb[:, h, :], start=True, stop=True),
                     reads=[qTB, SgbB], writes=[pbB[2 + hh]], acc=(c > 0))
        r2, r2B = vb, vbB
        t_, tB_ = fscr[3], fB[3]
        for hh in range(2):
            sl = slice(hh * 512, (hh + 1) * 512)
            S.op("dve", lambda e: e.tensor_tensor(out=f3(t_[:, sl], 4), in0=f3(pb[0 + hh][:, :], 4), in1=bcl(bg[:, b, hh * 4:(hh + 1) * 4], 128), op=ALU.mult),
                 reads=[pbB[0 + hh], smB], writes=[tB_])
        S.op("dve", lambda e: e.tensor_tensor(out=r2[:], in0=vb[:], in1=t_[:], op=ALU.subtract), reads=[vbB, tB_], writes=[r2B])
        for hh in range(2):
            for c in range(4):
                h = hh * 4 + c
                hs = slice(h * 128, (h + 1) * 128)
                S.op("pe", lambda e: e.matmul(pb[4 + hh][:, c * 128:(c + 1) * 128], lhsT=TtB[:, hs], rhs=r2[:, hs], start=True, stop=True),
                     reads=[TtBB, r2B], writes=[pbB[4 + hh]], acc=(c > 0))
        vn, vnB = hscr[4], hB[4]
        for hh in range(2):
            sl = slice(hh * 512, (hh + 1) * 512)
            S.op("act", lambda e: e.copy(out=vn[:, sl], in_=pb[4 + hh][:, :]), reads=[pbB[4 + hh]], writes=[vnB])
        o_, oB_ = fscr[4], fB[4]
        for hh in range(2):
            sl = slice(hh * 512, (hh + 1) * 512)
            S.op("dve", lambda e: e.tensor_tensor(out=f3(o_[:, sl], 4), in0=f3(pb[2 + hh][:, :], 4), in1=bcl(egc[:, b, hh * 4:(hh + 1) * 4], 128), op=ALU.mult),
                 reads=[pbB[2 + hh], smB, PbB], writes=[oB_])
        for hh in range(2):
            for c in range(4):
                h = hh * 4 + c
                hs = slice(h * 128, (h + 1) * 128)
                S.op("pe", lambda e: e.matmul(pb[6 + hh][:, c * 128:(c + 1) * 128], lhsT=qkm[:, hs], rhs=vn[:, hs], start=True, stop=True),
                     reads=[qkmB, vnB], writes=[pbB[6 + hh]], acc=(c > 0))
            for c in range(4):
                h = hh * 4 + c
                hs = slice(h * 128, (h + 1) * 128)
                S.op("pe", lambda e: e.matmul(pb[0 + hh][:, c * 128:(c + 1) * 128], lhsT=kdec[:, hs], rhs=vn[:, hs], start=True, stop=True),
                     reads=[kdecB, vnB], writes=[pbB[0 + hh]], acc=(c > 0))
        Sf = Sg[l][:].rearrange("p h e -> p (h e)")
        for hh in range(2):
            sl = slice(hh * 512, (hh + 1) * 512)
            S.op("dve", lambda e: e.tensor_tensor(out=o_[:, sl], in0=pb[6 + hh][:, :], in1=o_[:, sl], op=ALU.add), reads=[pbB[6 + hh], oB_], writes=[oB_])
            S.op("pool", lambda e: e.tensor_tensor(out=f3(Sf[:, sl], 4), in0=f3(Sf[:, sl], 4), in1=bcl(etot[:, b, hh * 4:(hh + 1) * 4], 128), op=ALU.mult),
                 reads=[SgB[l], smB, SgbB], writes=[SgB[l]])
            S.op("dve", lambda e: e.tensor_tensor(out=Sf[:, sl], in0=pb[0 + hh][:, :], in1=Sf[:, sl], op=ALU.add), reads=[pbB[0 + hh], SgB[l]], writes=[SgB[l]])
        if DUMP == "gdn_o":
            S.dma("sp", out_d[b * 128:(b + 1) * 128, :], o_[:], reads=[oB_])
        if DUMP == "gdn_S":
            S.dma("sp", out_d[b * 128:(b + 1) * 128, :], Sf, reads=[SgB[l]])
        if MXSTOP < 4.06:
            return
        sq, sqB = fscr[3], fB[3]
        S.op("pool", lambda e: e.tensor_tensor(out=sq[:], in0=o_[:], in1=o_[:], op=ALU.mult), reads=[oB_], writes=[sqB])
        S.op("dve", lambda e: e.tensor_reduce(out=blk_small[:, 32:40], in_=f3(sq[:], 8), axis=AX.X, op=ALU.add), reads=[sqB], writes=[bsB])
        rsqrt_small(blk_small[:, 32:40], 8, 1.0 / 128, [bsB], [bsB])
        S.op("dve", lambda e: e.tensor_tensor(out=f3(o_[:], 8), in0=f3(o_[:], 8), in1=bcl(blk_small[:, 32:40], 128), op=ALU.mult), reads=[oB_, bsB], writes=[oB_])
        S.op("dve", lambda e: e.tensor_tensor(out=o_[:], in0=o_[:], in1=zh[:, b, 0:1024], op=ALU.mult), reads=[oB_, zhB], writes=[oB_])
        for hh in range(2):
            bank = 2 + hh
            for c in range(4):
                transpose_f32(o_[:, (hh * 4 + c) * 128:(hh * 4 + c + 1) * 128], oB_, bank, c * 128)
            for c in range(4):
                S.op("act", lambda e: e.activation(out=mixT[:, hh * 4 + c, b * 128:(b + 1) * 128], in_=pb[bank][:, c * 128:(c + 1) * 128],
                                                   func=AF.Identity, scale=colA(l, "gnorm")), reads=[pbB[bank], parB], writes=[mixTB[b]])

    def ssd_block(l, b):
        xs = xs_tm[:, b, :]
        dt_ = spv[:, b, 16:32]
        xdt, xdtB = hscr[0], hB[0]
        xe, xeB = hscr[1], hB[1]
        S.op("dve", lambda e: e.tensor_tensor(out=f3(xdt[:], 16), in0=f3(xs, 16), in1=bcl(dt_, 64), op=ALU.mult), reads=[xsB[b], smB], writes=[xdtB])
        S.op("pool", lambda e: e.tensor_tensor(out=f3(xe[:], 16), in0=f3(xs, 16), in1=bcl(dte[:, b, :], 64), op=ALU.mult), reads=[xsB[b], smB], writes=[xeB])
        S.op("act", lambda e: e.copy(out=Ssb[:].rearrange("p h e -> p (h e)"), in_=Ss[l][:].rearrange("p h e -> p (h e)")), reads=[SsB[l]], writes=[SsbB])
        tg, tgB = fscr[0], fB[0]
        seg, segB = fscr[1], fB[1]
        MT = (hscr[2], hscr[3]); MTB = (hB[2], hB[3])
        y, yB = fscr[2], fB[2]
        bs_ = slice(b * 128, (b + 1) * 128)
        for g in range(2):
            S.op("pe", lambda e: e.matmul(pb[4][:, g * 128:(g + 1) * 128], lhsT=bT[:, g, bs_], rhs=cT[:, g, bs_], start=True, stop=True),
                 reads=[bcTB], writes=[pbB[4]], acc=(g > 0))
        for g in range(2):
            S.op("dve", lambda e: e.tensor_tensor(out=f3(tg[:], 8), in0=bcm(triu[:], 8), in1=bcl(gcat[:, b, 8 + g * 8:16 + g * 8], 128), op=ALU.mult),
                 reads=[cB, smB], writes=[tgB])
            for hh in range(2):
                bank = 6 + hh
                S.op("pe", lambda e: e.matmul(pb[bank][:, :], lhsT=ones[:], rhs=tg[:, hh * 512:(hh + 1) * 512], start=True, stop=False),
                     reads=[tgB, cB], writes=[pbB[bank]])
                S.op("pe", lambda e: e.matmul(pb[bank][:, :], lhsT=ident[:], rhs=loR[:].rearrange("p r j -> p (r j)"), start=False, stop=True),
                     reads=[cB], writes=[pbB[bank]], acc=True)
                for c in range(4):
                    j = hh * 4 + c
                    h = g * 8 + j
                    S.op("act", lambda e: e.activation(out=seg[:, j * 128:(j + 1) * 128], in_=pb[bank][:, c * 128:(c + 1) * 128], func=AF.Exp,
                                                       scale=1.0, bias=ngc[:, b, 8 + h:9 + h]), reads=[pbB[bank], smB], writes=[segB])
            S.op("dve", lambda e: e.tensor_tensor(out=f3(MT[g][:], 8), in0=f3(seg[:], 8), in1=bcm(pb[4][:, g * 128:(g + 1) * 128], 8), op=ALU.mult),
                 reads=[segB, pbB[4]], writes=[MTB[g]])
        for g in range(2):
            for j in range(8):
                h = g * 8 + j
                S.op("pe", lambda e: e.matmul(pb[0 + g][:, j * 64:(j + 1) * 64], lhsT=MT[g][:, j * 128:(j + 1) * 128], rhs=xdt[:, h * 64:(h + 1) * 64],
                                              start=True, stop=True), reads=[MTB[g], xdtB], writes=[pbB[0 + g]], acc=(j > 0))
            S.op("pe", lambda e: e.matmul(pb[2 + g][:, :], lhsT=cT[:, g, bs_], rhs=Ssb[:, g * 8:(g + 1) * 8, :].rearrange("p h e -> p (h e)"),
                                          start=True, stop=True), reads=[bcTB, SsbB], writes=[pbB[2 + g]])
        for g in range(2):
            sl = slice(g * 512, (g + 1) * 512)
            S.op("dve", lambda e: e.tensor_tensor(out=f3(y[:, sl], 8), in0=f3(pb[2 + g][:, :], 8), in1=bcl(egc[:, b, 8 + g * 8:16 + g * 8], 64), op=ALU.mult),
                 reads=[pbB[2 + g], smB], writes=[yB])
            S.op("dve", lambda e: e.tensor_tensor(out=y[:, sl], in0=pb[0 + g][:, :], in1=y[:, sl], op=ALU.add), reads=[pbB[0 + g], yB], writes=[yB])
        Sf = Ss[l][:].rearrange("p h e -> p (h e)")
        for g in range(2):
            sl = slice(g * 512, (g + 1) * 512)
            S.op("pe", lambda e: e.matmul(pb[6 + g][:, :], lhsT=b_tm[:, b, g * 128:(g + 1) * 128], rhs=xe[:, sl], start=True, stop=True),
                 reads=[btmB[b], xeB], writes=[pbB[6 + g]])
            S.op("pool", lambda e: e.tensor_tensor(out=f3(Sf[:, sl], 8), in0=f3(Sf[:, sl], 8), in1=bcl(etot[:, b, 8 + g * 8:16 + g * 8], 64), op=ALU.mult),
                 reads=[SsB[l], smB, SsbB], writes=[SsB[l]])
            S.op("dve", lambda e: e.tensor_tensor(out=Sf[:, sl], in0=pb[6 + g][:, :], in1=Sf[:, sl], op=ALU.add), reads=[pbB[6 + g], SsB[l]], writes=[SsB[l]])
        t_, tB_ = fscr[3], fB[3]
        S.op("pool", lambda e: e.tensor_tensor(out=f3(t_[:], 16), in0=f3(xs, 16), in1=bcl(dsk[l][:, :], 64), op=ALU.mult), reads=[xsB[b], parB], writes=[tB_])
        S.op("dve", lambda e: e.tensor_tensor(out=y[:], in0=y[:], in1=t_[:], op=ALU.add), reads=[yB, tB_], writes=[yB])
        S.op("dve", lambda e: e.tensor_tensor(out=y[:], in0=y[:], in1=zh[:, b, 1024:2048], op=ALU.mult), reads=[yB, zhB], writes=[yB])
        for g in range(2):
            S.op("act", lambda e: e.activation(out=t_[:, g * 512:(g + 1) * 512], in_=y[:, g * 512:(g + 1) * 512], func=AF.Square,
                                               accum_out=blk_small[:, 40 + g:41 + g]), reads=[yB], writes=[tB_, bsB])
        rsqrt_small(blk_small[:, 40:42], 2, 1.0 / 512, [bsB], [bsB])
        S.op("dve", lambda e: e.tensor_tensor(out=f3(y[:], 2), in0=f3(y[:], 2), in1=bcl(blk_small[:, 40:42], 512), op=ALU.mult), reads=[yB, bsB], writes=[yB])
        for hh in range(2):
            bank = 4 + hh
            for c in range(4):
                transpose_f32(y[:, (hh * 4 + c) * 128:(hh * 4 + c + 1) * 128], yB, bank, c * 128)
            for c in range(4):
                cc = hh * 4 + c
                S.op("act", lambda e: e.activation(out=mixT[:, 8 + cc, b * 128:(b + 1) * 128], in_=pb[bank][:, c * 128:(c + 1) * 128],
                                                   func=AF.Identity, scale=colsB[l][:, 60 + cc:61 + cc]), reads=[pbB[bank], parB], writes=[mixTB[b]])

    for t in range(n_tiles):
        for b in range(NB):
            r0 = t * T + b * 128
            S.dma("sp", x_sb[:, b, :], x_d[r0:r0 + 128, :], writes=[xB[b]])
        for l in range(depth):
            if "F1" in DBG:
                ffn(l, 1)
            else:
                [w_next(s_) for s_ in ffn_specs(l, 1)]
            if "MX" in DBG:
                mixer(l)
            else:
                [w_next(s_) for s_ in mix_specs(l)]
            if "F2" in DBG:
                ffn(l, 2)
            else:
                [w_next(s_) for s_ in ffn_specs(l, 2)]
        for b in range(NB):
            S.op("act", lambda e: e.activation(out=fscr[0][:], in_=x_sb[:, b, :], func=AF.Square, accum_out=stat[:, b:b + 1]),
                 reads=[xB[b]], writes=[fB[0], statB])
        rsqrt_small(stat[:, 0:NB], NB, 1.0 / D, [statB], [statB])
        for b in range(NB):
            ob, obB = fscr[2 + b % 4], fB[2 + b % 4]
            S.op("dve", lambda e: e.scalar_tensor_tensor(out=ob[:], in0=x_sb[:, b, :], scalar=stat[:, b:b + 1], in1=fin_bc[:],
                                                         op0=ALU.mult, op1=ALU.mult), reads=[xB[b], statB, parB], writes=[obB])
            r0 = t * T + b * 128
            if not DUMP:
                S.dma("sp", out_d[r0:r0 + 128, :], ob[:], reads=[obB])
    S._need("sp", {"d%d" % i: S.cnt["d%d" % i] for i in range(S.ndma) if S.cnt["d%d" % i] > 0})
    print("program: %d instructions, %d waits" % (S.nins, S.nwait), S.tot)


_PROG = {}


def _get_prog(n_tiles, depth=DEPTH):
    key = (n_tiles, depth)
    if key not in _PROG:
        _PROG[key] = build_program(n_tiles, depth)
    return _PROG[key]


W_KEYS = ["ffn1_w_gate", "ffn1_w_up", "ffn1_w_down", "w_in", "w_out", "ffn2_w_gate", "ffn2_w_up", "ffn2_w_down"]
P_KEYS = ["ffn1_norm", "mix_norm", "ffn2_norm", "final_norm", "gdn_conv_w", "gdn_a_log", "gdn_dt_bias", "gdn_norm_w",
          "ssm_conv_w", "ssm_conv_b", "ssm_a_log", "ssm_dt_bias", "ssm_d", "ssm_norm_w"]


def run(inputs, n_cores=8, n_tiles=SEQ // T, depth=DEPTH):
    nc = _get_prog(n_tiles, depth)
    shared = {k: np.ascontiguousarray(np.asarray(inputs[k], dtype=np.float32)) for k in W_KEYS + P_KEYS}
    x = np.asarray(inputs["x"], dtype=np.float32)
    in_maps = []
    for c in range(n_cores):
        m = dict(shared)
        m["x"] = np.ascontiguousarray(x[c, : n_tiles * T, :])
        in_maps.append(m)
    res = run_bass_kernel_spmd(nc, in_maps, core_ids=list(range(n_cores)))
    return np.stack([np.asarray(r["out"]) for r in res.results], axis=0)


def kernel(**inputs):
    return run(inputs).astype(np.float32)
```

```python
import contextlib
import os
import numpy as np
import concourse.bass as bass
import concourse.mybir as mybir
from concourse.bass_utils import run_bass_kernel_spmd

F32 = mybir.dt.float32
BF16 = mybir.dt.bfloat16
AF = mybir.ActivationFunctionType
ALU = mybir.AluOpType
AX = mybir.AxisListType

D = 1024
DFF = 2816
SEQ = 8192
DEPTH = 2
INW = 6688
NB = 2
T = 128 * NB
EPS = 1e-6
BIG = float(os.environ.get("MK_BIG", "2000.0"))
KSLOT = 8
NSLOT = 6
LOOKAHEAD = 3
SEM_LIM = 16000
DBG = os.environ.get("MK_DBG", "F1,MX,F2").split(",")
MXSTOP = float(os.environ.get("MK_MXSTOP", "99"))
DUMP = os.environ.get("MK_DUMP", "")

O_QKV, O_GZ, O_GB, O_GA, O_SZ, O_XBC, O_DT = 0, 3072, 4096, 4104, 4112, 5136, 6672


class Buf:
    __slots__ = ("w", "r", "name", "excl")

    def __init__(self, name="", excl=False):
        self.w = {}
        self.r = {}
        self.name = name
        self.excl = excl


class Sched:
    def __init__(self, nc, es):
        self.nc = nc
        self.eng = {"pe": nc.tensor, "act": nc.scalar, "dve": nc.vector, "pool": nc.gpsimd, "sp": nc.sync}
        self.sems = {}
        self.cnt = {}
        self.seen = {e: {} for e in self.eng}
        self.es = es
        self.tot = {}
        for e in ("pe", "act", "dve", "pool"):
            self.tot[e] = 0
        self.ndma = 16
        for i in range(self.ndma):
            self.sems["d%d" % i] = es.enter_context(nc.semaphore("s_d%d" % i))
            self.cnt["d%d" % i] = 0
        self.dma_i = 0
        self.nwait = 0
        self.nins = 0

    def _need(self, eng, deps):
        sn = self.seen[eng]
        for k, c in deps.items():
            if sn.get(k, 0) < c:
                self.eng[eng].wait_ge(self.sems[k], c)
                sn[k] = c
                self.nwait += 1

    def _sync(self, eng, reads, writes, acc):
        deps = {}
        for b in reads:
            for k, c in b.w.items():
                if deps.get(k, 0) < c:
                    deps[k] = c
            if b.excl:
                for k, c in b.r.items():
                    if not k.startswith(eng + "_") and deps.get(k, 0) < c:
                        deps[k] = c
        for b in writes:
            for k, c in b.w.items():
                if acc and k.startswith("pe_"):
                    continue
                if deps.get(k, 0) < c:
                    deps[k] = c
            for k, c in b.r.items():
                if deps.get(k, 0) < c:
                    deps[k] = c
        self._need(eng, deps)

    def _mark(self, key, c, reads, writes):
        for b in reads:
            if b.r.get(key, 0) < c:
                b.r[key] = c
        for b in writes:
            b.w = {key: c}
            b.r = {}

    def op(self, eng, ins_fn, reads=(), writes=(), acc=False):
        self._sync(eng, reads, writes, acc)
        ins = ins_fn(self.eng[eng])
        key = "%s_%d" % (eng, self.tot[eng] // SEM_LIM)
        self.tot[eng] += 1
        if key not in self.sems:
            self.sems[key] = self.es.enter_context(self.nc.semaphore("s_" + key))
            self.cnt[key] = 0
        self.cnt[key] += 1
        ins.then_inc(self.sems[key], 1)
        self._mark(key, self.cnt[key], reads, writes)
        self.nins += 1
        return ins

    def dma(self, eng, out, in_, reads=(), writes=(), **kw):
        key = "d%d" % (self.dma_i % self.ndma)
        self.dma_i += 1
        if self.cnt[key] > 0:
            self._need(eng, {key: self.cnt[key]})
        self._sync(eng, reads, writes, False)
        ins = self.eng[eng].dma_start(out=out, in_=in_, **kw)
        self.cnt[key] += 16
        ins.then_inc(self.sems[key], 16)
        self._mark(key, self.cnt[key], reads, writes)
        self.nins += 1
        return ins

    def wait_all(self, eng, bufs):
        deps = {}
        for b in bufs:
            for dd in (b.w, b.r):
                for k, c in dd.items():
                    if deps.get(k, 0) < c:
                        deps[k] = c
        self._need(eng, deps)


def bcl(ap2, n):
    return ap2.unsqueeze(2).to_broadcast([ap2.shape[0], ap2.shape[1], n])


def bcm(ap2, h):
    return ap2.unsqueeze(1).to_broadcast([ap2.shape[0], h, ap2.shape[1]])


def build_program(n_tiles, depth=DEPTH):
    nc = bass.Bass("TRN2", target_bir_lowering=False)
    es = contextlib.ExitStack()
    with es:
        _build(nc, es, n_tiles, depth)
    return nc


def _build(nc, es, n_tiles, depth):
    S = Sched(nc, es)
    ntok = n_tiles * T

    def din(name, shape):
        return nc.dram_tensor(name, shape, F32, kind="ExternalInput").ap()

    x_d = din("x", [ntok, D])
    out_d = nc.dram_tensor("out", [ntok, D], F32, kind="ExternalOutput").ap()
    wnames = {"ffn1_w_gate": (D, DFF), "ffn1_w_up": (D, DFF), "ffn1_w_down": (DFF, D), "w_in": (D, INW),
              "w_out": (2 * D, D), "ffn2_w_gate": (D, DFF), "ffn2_w_up": (D, DFF), "ffn2_w_down": (DFF, D)}
    w_f32 = {k: din(k, [DEPTH, r, c]) for k, (r, c) in wnames.items()}
    w_bf = {k: nc.dram_tensor(k + "_bf", [DEPTH, r, c], BF16, kind="Internal").ap() for k, (r, c) in wnames.items()}
    w_buf = {(k, l): Buf("w_%s_%d" % (k, l)) for k in wnames for l in range(DEPTH)}
    pn = {"ffn1_norm": [DEPTH, D], "mix_norm": [DEPTH, D], "ffn2_norm": [DEPTH, D], "final_norm": [D],
          "gdn_conv_w": [DEPTH, 4, 3072], "gdn_a_log": [DEPTH, 8], "gdn_dt_bias": [DEPTH, 8],
          "gdn_norm_w": [DEPTH, 128], "ssm_conv_w": [DEPTH, 4, 1536], "ssm_conv_b": [DEPTH, 1536],
          "ssm_a_log": [DEPTH, 16], "ssm_dt_bias": [DEPTH, 16], "ssm_d": [DEPTH, 16], "ssm_norm_w": [DEPTH, D]}
    p_d = {k: din(k, s) for k, s in pn.items()}

    def sb(name, shape, dt=F32):
        return es.enter_context(nc.sbuf_tensor(name, shape, dt))

    def ps(name):
        return es.enter_context(nc.psum_tensor(name, [128, 512], F32))

    ident = sb("ident", [128, 128]); ones = sb("ones", [128, 128]); triu = sb("triu", [128, 128])
    ustrR = sb("ustrR", [128, 4, 128]); loR = sb("loR", [128, 4, 128])
    cB = Buf("consts")
    S.op("pool", lambda e: e.memset(ident[:], 0.0), writes=[cB])
    S.op("pool", lambda e: e.affine_select(out=ident[:], in_=ident[:], pattern=[[-1, 128]], compare_op=ALU.not_equal,
                                           fill=1.0, base=0, channel_multiplier=1), reads=[cB], writes=[cB])
    S.op("pool", lambda e: e.memset(ones[:], 1.0), writes=[cB])
    S.op("pool", lambda e: e.memset(triu[:], 1.0), writes=[cB])
    S.op("pool", lambda e: e.affine_select(out=triu[:], in_=triu[:], pattern=[[1, 128]], compare_op=ALU.is_ge,
                                           fill=0.0, base=0, channel_multiplier=-1), reads=[cB], writes=[cB])
    S.op("pool", lambda e: e.memset(ustrR[:], 0.0), writes=[cB])
    S.op("pool", lambda e: e.memset(loR[:], 0.0), writes=[cB])
    for r in range(4):
        S.op("pool", lambda e, r=r: e.affine_select(out=ustrR[:, r, :], in_=ustrR[:, r, :], pattern=[[-1, 128]],
                                                    compare_op=ALU.is_gt, fill=BIG, base=0, channel_multiplier=1),
             reads=[cB], writes=[cB])
        S.op("pool", lambda e, r=r: e.affine_select(out=loR[:, r, :], in_=loR[:, r, :], pattern=[[1, 128]],
                                                    compare_op=ALU.is_ge, fill=-BIG, base=0, channel_multiplier=-1),
             reads=[cB], writes=[cB])

    pb = [ps("pb%d" % i) for i in range(8)]
    pbB = [Buf("pb%d" % i, excl=True) for i in range(8)]

    NRA, NRB = 121, 68
    colsA = [sb("colsA%d" % l, [128, NRA]) for l in range(DEPTH)]
    colsB = [sb("colsB%d" % l, [128, NRB]) for l in range(DEPTH)]
    stg = sb("stg", [128, 128])
    stgB = Buf("stg")
    parB = Buf("params")
    fin_bc = sb("fin_bc", [128, D])
    S.dma("sp", fin_bc[:], p_d["final_norm"].partition_broadcast(128), writes=[parB])
    hb = [sb("hb%d" % l, [128, 64]) for l in range(DEPTH)]
    negA = [sb("negA%d" % l, [128, 24]) for l in range(DEPTH)]
    spb = [sb("spb%d" % l, [128, 32]) for l in range(DEPTH)]
    dsk = [sb("dsk%d" % l, [128, 16]) for l in range(DEPTH)]
    for l in range(DEPTH):
        def stage(rows_list, dst, nrows):
            r0 = 0
            for ap2 in rows_list:
                n = ap2.shape[0]
                S.dma("sp", stg[r0:r0 + n, :], ap2, writes=[stgB])
                r0 += n
            assert r0 == nrows, (r0, nrows)
            S.op("pe", lambda e: e.transpose(pb[0][:, 0:nrows], stg[0:nrows, :], ident[0:nrows, 0:nrows]),
                 reads=[stgB, cB], writes=[pbB[0]])
            S.op("dve", lambda e: e.tensor_copy(out=dst[:, 0:nrows], in_=pb[0][:, 0:nrows]), reads=[pbB[0]], writes=[parB])

        stage([p_d["gdn_conv_w"][l].rearrange("k (c p) -> (k c) p", p=128),
               p_d["ffn1_norm"][l].rearrange("(c p) -> c p", p=128),
               p_d["mix_norm"][l].rearrange("(c p) -> c p", p=128),
               p_d["ffn2_norm"][l].rearrange("(c p) -> c p", p=128),
               p_d["gdn_norm_w"][l].rearrange("(c p) -> c p", p=128)], colsA[l], NRA)
        stage([p_d["ssm_conv_w"][l].rearrange("k (c p) -> (k c) p", p=128),
               p_d["ssm_conv_b"][l].rearrange("(c p) -> c p", p=128),
               p_d["ssm_norm_w"][l].rearrange("(c p) -> c p", p=128)], colsB[l], NRB)
        S.dma("sp", hb[l][:, 0:8], p_d["gdn_a_log"][l].partition_broadcast(128), writes=[parB])
        S.dma("sp", hb[l][:, 8:24], p_d["ssm_a_log"][l].partition_broadcast(128), writes=[parB])
        S.op("pool", lambda e: e.memset(spb[l][:, 0:8], 0.0), writes=[parB])
        S.dma("sp", spb[l][:, 8:16], p_d["gdn_dt_bias"][l].partition_broadcast(128), writes=[parB])
        S.dma("sp", spb[l][:, 16:32], p_d["ssm_dt_bias"][l].partition_broadcast(128), writes=[parB])
        S.dma("sp", dsk[l][:, :], p_d["ssm_d"][l].partition_broadcast(128), writes=[parB])
        S.op("act", lambda e: e.activation(out=negA[l][:, :], in_=hb[l][:, 0:24], func=AF.Exp), reads=[parB], writes=[parB])
        S.op("dve", lambda e: e.tensor_scalar(out=negA[l][:, :], in0=negA[l][:, :], scalar1=-1.0, scalar2=None, op0=ALU.mult),
             reads=[parB], writes=[parB])

    def colA(l, kind, c=0):
        base = {"conv": 0, "ffn1": 96, "mix": 104, "ffn2": 112, "gnorm": 120}[kind]
        return colsA[l][:, base + c:base + c + 1]

    cast_order = ["ffn1_w_gate", "ffn1_w_up", "ffn1_w_down", "w_in", "w_out", "ffn2_w_gate", "ffn2_w_up", "ffn2_w_down"]
    for l in range(depth):
        for k in cast_order:
            r, c = wnames[k]
            step = 256
            for r0 in range(0, r, step):
                S.dma("pool", w_bf[k][l, r0:r0 + step, :], w_f32[k][l, r0:r0 + step, :], writes=[], reads=[])
                key = "d%d" % ((S.dma_i - 1) % S.ndma)
                w_buf[(k, l)].w[key] = S.cnt[key]

    slots = [sb("wslot%d" % i, [128, KSLOT, 512], BF16) for i in range(NSLOT)]
    slotB = [Buf("slot%d" % i) for i in range(NSLOT)]

    def wview(k, l):
        return w_bf[k][l].rearrange("(k p) f -> p k f", p=128)

    def ffn_specs(l, which):
        sp = []
        g, u, dn = ("ffn%d_w_gate" % which, "ffn%d_w_up" % which, "ffn%d_w_down" % which)
        for c0 in range(0, DFF, 512):
            cn = min(512, DFF - c0)
            sp.append([((g, l), 0, 8, c0, cn, 0)])
            sp.append([((u, l), 0, 8, c0, cn, 0)])
        for half in range(2):
            sp.append([((dn, l), 0, 8, half * 512, 512, 0)])
            sp.append([((dn, l), 8, 8, half * 512, 512, 0)])
            sp.append([((dn, l), 16, 6, half * 512, 512, 0)])
        return sp

    def mix_specs(l):
        sp = [[(("w_in", l), 0, 8, O_GB, 16, 0), (("w_in", l), 0, 8, O_DT, 16, 16)]]
        for c0 in range(0, 3072, 512):
            sp.append([(("w_in", l), 0, 8, O_QKV + c0, 512, 0)])
        for c0 in range(0, 1536, 512):
            sp.append([(("w_in", l), 0, 8, O_XBC + c0, 512, 0)])
        for c0 in (O_GZ, O_GZ + 512, O_SZ, O_SZ + 512):
            sp.append([(("w_in", l), 0, 8, c0, 512, 0)])
        for half in range(2):
            sp.append([(("w_out", l), 0, 8, half * 512, 512, 0)])
            sp.append([(("w_out", l), 8, 8, half * 512, 512, 0)])
        return sp

    schedule = []
    for t in range(n_tiles):
        for l in range(depth):
            schedule += ffn_specs(l, 1) + mix_specs(l) + ffn_specs(l, 2)
    wst = {"issued": 0, "taken": 0}

    def w_issue():
        i = wst["issued"]
        if i >= len(schedule):
            return
        slot = i % NSLOT
        for (wk, k0, kn, c0, cn, dc) in schedule[i]:
            S.dma("sp", slots[slot][:, 0:kn, dc:dc + cn], wview(*wk)[:, k0:k0 + kn, c0:c0 + cn],
                  reads=[w_buf[wk]], writes=[])
            key = "d%d" % ((S.dma_i - 1) % S.ndma)
            slotB[slot].w[key] = S.cnt[key]
        wst["issued"] += 1

    def w_issue_guarded():
        i = wst["issued"]
        if i >= len(schedule):
            return
        slot = i % NSLOT
        S.wait_all("sp", [slotB[slot]])
        slotB[slot].w = {}
        slotB[slot].r = {}
        w_issue()

    def w_next(expect):
        i = wst["taken"]
        assert schedule[i] == expect, (i, schedule[i], expect)
        while wst["issued"] < min(len(schedule), i + 1 + LOOKAHEAD):
            w_issue_guarded()
        wst["taken"] += 1
        return slots[i % NSLOT], slotB[i % NSLOT]

    x_sb = sb("x_sb", [128, NB, D]); xB = [Buf("x%d" % b) for b in range(NB)]
    fscr = [sb("fscr%d" % i, [128, D]) for i in range(6)]; fB = [Buf("fscr%d" % i) for i in range(6)]
    hscr = [sb("hscr%d" % i, [128, D], BF16) for i in range(5)]; hB = [Buf("hscr%d" % i) for i in range(5)]
    pt32 = sb("pt32", [128, D]); pt32B = Buf("pt32")
    hT = sb("hT", [128, 8, T], BF16); hTB = Buf("hT")
    zh = sb("zh", [128, NB, 2048])
    zhB = Buf("zh")
    hid = zh[:].rearrange("p b f -> p (b f)").bitcast(BF16)[:, 0:22 * T].rearrange("p (c t) -> p c t", t=T)
    stat = sb("stat", [128, 64]); statB = Buf("stat")
    qkv_tm = sb("qkv_tm", [128, NB, 3072]); qkvB = [Buf("qkv%d" % b) for b in range(NB)]
    xs_tm = sb("xs_tm", [128, NB, 1024]); xsB = [Buf("xs%d" % b) for b in range(NB)]
    b_tm = sb("b_tm", [128, NB, 256], BF16); btmB = [Buf("btm%d" % b) for b in range(NB)]
    bT = sb("bT", [128, 2, T], BF16); cT = sb("cT", [128, 2, T], BF16); bcTB = Buf("bcT")
    mixT = sb("mixT", [128, 16, T], BF16); mixTB = [Buf("mixT%d" % b) for b in range(NB)]
    xbuf = [sb("xbuf%d" % i, [128, T + 3]) for i in range(2)]; xbufB = [Buf("xbuf%d" % i) for i in range(2)]
    cacc = [sb("cacc%d" % i, [128, T]) for i in range(2)]; caccB = [Buf("cacc%d" % i) for i in range(2)]
    csil = [sb("csil%d" % i, [128, T]) for i in range(2)]; csilB = [Buf("csil%d" % i) for i in range(2)]
    tails = [sb("tails%d" % l, [128, 36, 3]) for l in range(DEPTH)]
    tailB = [[Buf("tail%d_%d" % (l, c)) for c in range(36)] for l in range(DEPTH)]
    sm = sb("sm", [128, NB, 32]); spv = sb("spv", [128, NB, 32]); beta = sb("beta", [128, NB, 8])
    gcat = sb("gcat", [128, NB, 24]); gct = sb("gct", [128, NB, 64]); egc = sb("egc", [128, NB, 24])
    edl = sb("edl", [128, NB, 24]); etot = sb("etot", [128, NB, 24]); biasA = sb("biasA", [128, NB, 8])
    ngc = sb("ngc", [128, NB, 24]); bg = sb("bg", [128, NB, 8]); dte = sb("dte", [128, NB, 16])
    smB = Buf("small")
    blk_small = sb("blk_small", [128, 64]); bsB = Buf("blk_small")
    Sg = [sb("Sg%d" % l, [128, 8, 128]) for l in range(DEPTH)]; SgB = [Buf("Sg%d" % l) for l in range(DEPTH)]
    Ss = [sb("Ss%d" % l, [128, 16, 64]) for l in range(DEPTH)]; SsB = [Buf("Ss%d" % l) for l in range(DEPTH)]
    for l in range(DEPTH):
        S.op("pool", lambda e: e.memset(Sg[l][:], 0.0), writes=[SgB[l]])
        S.op("pool", lambda e: e.memset(Ss[l][:], 0.0), writes=[SsB[l]])
        S.op("pool", lambda e: e.memset(tails[l][:], 0.0), writes=tailB[l])
    Sgb = sb("Sgb", [128, 8, 128], BF16); SgbB = Buf("Sgb")
    Ssb = sb("Ssb", [128, 16, 64], BF16); SsbB = Buf("Ssb")

    f3 = lambda ap, h: ap.rearrange("p (h e) -> p h e", h=h)

    def transpose_f32(src_ap, srcB, bank, col0, reads_extra=()):
        S.op("pe", lambda e: e.transpose(pb[bank][:, col0:col0 + 128], src_ap, ident[:]),
             reads=[srcB, cB] + list(reads_extra), writes=[pbB[bank]], acc=True)

    def rsqrt_small(ap, n, scale, readsB, writesB):
        S.op("act", lambda e: e.activation(out=ap, in_=ap, func=AF.Ln, bias=EPS, scale=scale), reads=readsB, writes=writesB)
        S.op("act", lambda e: e.activation(out=ap, in_=ap, func=AF.Exp, scale=-0.5), reads=writesB, writes=writesB)

    def norm_T(wcol_fn):
        for b in range(NB):
            S.op("act", lambda e: e.activation(out=fscr[0][:], in_=x_sb[:, b, :], func=AF.Square, accum_out=stat[:, b:b + 1]),
                 reads=[xB[b]], writes=[fB[0], statB])
        rsqrt_small(stat[:, 0:NB], NB, 1.0 / D, [statB], [statB])
        for b in range(NB):
            S.op("dve", lambda e: e.tensor_scalar(out=fscr[1][:], in0=x_sb[:, b, :], scalar1=stat[:, b:b + 1], scalar2=None,
                                                  op0=ALU.mult), reads=[xB[b], statB], writes=[fB[1]])
            for half in range(2):
                bank = 6 + half
                for c in range(4):
                    cc = half * 4 + c
                    transpose_f32(fscr[1][:, cc * 128:(cc + 1) * 128], fB[1], bank, c * 128)
                for c in range(4):
                    cc = half * 4 + c
                    S.op("act", lambda e: e.activation(out=hT[:, cc, b * 128:(b + 1) * 128], in_=pb[bank][:, c * 128:(c + 1) * 128],
                                                       func=AF.Identity, scale=wcol_fn(cc)), reads=[pbB[bank], parB], writes=[hTB])

    def ffn(l, which):
        kind = "ffn%d" % which
        norm_T(lambda cc: colA(l, kind, cc))
        specs = ffn_specs(l, which)
        si = 0
        hidB = zhB
        for c0 in range(0, DFF, 512):
            cn = min(512, DFF - c0)
            gs, gsB = w_next(specs[si]); us, usB = w_next(specs[si + 1]); si += 2
            for c in range(cn // 128):
                fc = c0 // 128 + c
                pg, pu = (0, 1) if fc % 2 == 0 else (2, 3)
                for k in range(8):
                    S.op("pe", lambda e: e.matmul(pb[pg][:, 0:T], lhsT=gs[:, k, c * 128:(c + 1) * 128], rhs=hT[:, k, :],
                                                  start=(k == 0), stop=(k == 7)), reads=[gsB, hTB], writes=[pbB[pg]], acc=(k > 0))
                for k in range(8):
                    S.op("pe", lambda e: e.matmul(pb[pu][:, 0:T], lhsT=us[:, k, c * 128:(c + 1) * 128], rhs=hT[:, k, :],
                                                  start=(k == 0), stop=(k == 7)), reads=[usB, hTB], writes=[pbB[pu]], acc=(k > 0))
                sc = fscr[2 + fc % 2]; scB = fB[2 + fc % 2]
                S.op("act", lambda e: e.activation(out=sc[:, 0:T], in_=pb[pg][:, 0:T], func=AF.Silu), reads=[pbB[pg]], writes=[scB])
                S.op("dve", lambda e: e.tensor_tensor(out=hid[:, fc, :], in0=pb[pu][:, 0:T], in1=sc[:, 0:T], op=ALU.mult),
                     reads=[pbB[pu], scB], writes=[hidB])
        for half in range(2):
            dsl = [w_next(specs[si]), w_next(specs[si + 1]), w_next(specs[si + 2])]; si += 3
            for b in range(NB):
                bank = 4 + b % 2
                for c in range(22):
                    ws, wsB = dsl[c // 8]
                    S.op("pe", lambda e: e.matmul(pb[bank][:, :], lhsT=hid[:, c, b * 128:(b + 1) * 128], rhs=ws[:, c % 8, :],
                                                  start=(c == 0), stop=(c == 21)), reads=[wsB, hidB], writes=[pbB[bank]], acc=(c > 0))
                xs_ = x_sb[:, b, half * 512:(half + 1) * 512]
                S.op("dve", lambda e: e.scalar_tensor_tensor(out=xs_, in0=pb[bank][:, :], scalar=0.5, in1=xs_, op0=ALU.mult, op1=ALU.add),
                     reads=[pbB[bank], xB[b]], writes=[xB[b]])

    def mixer(l):
        norm_T(lambda cc: colA(l, "mix", cc))
        specs = mix_specs(l)
        si = 0
        ws, wsB = w_next(specs[si]); si += 1
        for b in range(NB):
            for k in range(8):
                S.op("pe", lambda e: e.matmul(pb[0][:, 0:32], lhsT=hT[:, k, b * 128:(b + 1) * 128], rhs=ws[:, k, 0:32],
                                              start=(k == 0), stop=(k == 7)), reads=[wsB, hTB], writes=[pbB[0]], acc=(k > 0))
            S.op("dve", lambda e: e.tensor_copy(out=sm[:, b, :], in_=pb[0][:, 0:32]), reads=[pbB[0]], writes=[smB])
        S.op("act", lambda e: e.activation(out=beta[:], in_=sm[:, :, 0:8], func=AF.Tanh, scale=0.5), reads=[smB], writes=[smB])
        S.op("dve", lambda e: e.tensor_scalar(out=beta[:], in0=beta[:], scalar1=0.5, scalar2=0.5, op0=ALU.mult, op1=ALU.add),
             reads=[smB], writes=[smB])
        S.op("dve", lambda e: e.tensor_tensor(out=sm[:], in0=sm[:], in1=bcm(spb[l][:, :], NB), op=ALU.add), reads=[smB, parB], writes=[smB])
        S.op("dve", lambda e: e.tensor_scalar(out=sm[:, :, 0:8], in0=sm[:, :, 0:8], scalar1=-1.0, scalar2=None, op0=ALU.mult),
             reads=[smB], writes=[smB])
        S.op("act", lambda e: e.activation(out=spv[:], in_=sm[:], func=AF.Exp), reads=[smB], writes=[smB])
        S.op("act", lambda e: e.activation(out=spv[:], in_=spv[:], func=AF.Ln, bias=1.0, scale=1.0), reads=[smB], writes=[smB])
        S.op("dve", lambda e: e.tensor_tensor(out=gcat[:], in0=spv[:, :, 8:32], in1=bcm(negA[l][:, :], NB), op=ALU.mult),
             reads=[smB, parB], writes=[smB])
        for b in range(NB):
            S.op("pe", lambda e: e.matmul(pb[0][:, 0:24], lhsT=triu[:], rhs=gcat[:, b, :], start=True, stop=True),
                 reads=[smB, cB], writes=[pbB[0]])
            S.op("pe", lambda e: e.matmul(pb[0][:, 32:56], lhsT=ones[:], rhs=gcat[:, b, :], start=True, stop=True),
                 reads=[smB, cB], writes=[pbB[0]], acc=True)
            S.op("dve", lambda e: e.tensor_copy(out=gct[:, b, 0:56], in_=pb[0][:, 0:56]), reads=[pbB[0]], writes=[smB])
        S.op("act", lambda e: e.activation(out=egc[:], in_=gct[:, :, 0:24], func=AF.Exp), reads=[smB], writes=[smB])
        S.op("act", lambda e: e.activation(out=etot[:], in_=gct[:, :, 32:56], func=AF.Exp), reads=[smB], writes=[smB])
        S.op("dve", lambda e: e.tensor_tensor(out=edl[:], in0=gct[:, :, 32:56], in1=gct[:, :, 0:24], op=ALU.subtract), reads=[smB], writes=[smB])
        S.op("act", lambda e: e.activation(out=edl[:], in_=edl[:], func=AF.Exp), reads=[smB], writes=[smB])
        S.op("dve", lambda e: e.tensor_scalar(out=ngc[:], in0=gct[:, :, 0:24], scalar1=-1.0, scalar2=None, op0=ALU.mult), reads=[smB], writes=[smB])
        S.op("dve", lambda e: e.tensor_tensor(out=biasA[:], in0=gct[:, :, 0:8], in1=spv[:, :, 0:8], op=ALU.subtract), reads=[smB], writes=[smB])
        S.op("dve", lambda e: e.tensor_tensor(out=bg[:], in0=beta[:], in1=egc[:, :, 0:8], op=ALU.mult), reads=[smB], writes=[smB])
        S.op("dve", lambda e: e.tensor_tensor(out=dte[:], in0=spv[:, :, 16:32], in1=edl[:, :, 8:24], op=ALU.mult), reads=[smB], writes=[smB])

        if MXSTOP < 2:
            [w_next(s_) for s_ in specs[si:]]
            return
        chunk_i = 0
        for grp in range(9):
            ws, wsB = w_next(specs[si]); si += 1
            for c in range(4):
                fc = grp * 4 + c
                i2 = chunk_i % 2; chunk_i += 1
                bank = i2
                for k in range(8):
                    S.op("pe", lambda e: e.matmul(pb[bank][:, 0:T], lhsT=ws[:, k, c * 128:(c + 1) * 128], rhs=hT[:, k, :],
                                                  start=(k == 0), stop=(k == 7)), reads=[wsB, hTB], writes=[pbB[bank]], acc=(k > 0))
                xb, xbB = xbuf[i2], xbufB[i2]
                S.op("act", lambda e: e.copy(out=xb[:, 3:3 + T], in_=pb[bank][:, 0:T]), reads=[pbB[bank]], writes=[xbB])
                S.op("pool", lambda e: e.tensor_copy(out=xb[:, 0:3], in_=tails[l][:, fc, :]), reads=[tailB[l][fc]], writes=[xbB])
                S.op("pool", lambda e: e.tensor_copy(out=tails[l][:, fc, :], in_=xb[:, T:T + 3]), reads=[xbB], writes=[tailB[l][fc]])
                if fc < 24:
                    wc = lambda kk: colsA[l][:, kk * 24 + fc:kk * 24 + fc + 1]
                else:
                    wc = lambda kk: colsB[l][:, kk * 12 + (fc - 24):kk * 12 + (fc - 24) + 1]
                ca, caB = cacc[i2], caccB[i2]
                S.op("dve", lambda e: e.tensor_scalar(out=ca[:], in0=xb[:, 0:T], scalar1=wc(0), scalar2=None, op0=ALU.mult),
                     reads=[xbB, parB], writes=[caB])
                S.op("dve", lambda e: e.scalar_tensor_tensor(out=ca[:], in0=xb[:, 1:1 + T], scalar=wc(1), in1=ca[:], op0=ALU.mult, op1=ALU.add),
                     reads=[xbB, parB, caB], writes=[caB])
                S.op("dve", lambda e: e.scalar_tensor_tensor(out=ca[:], in0=xb[:, 2:2 + T], scalar=wc(2), in1=ca[:], op0=ALU.mult, op1=ALU.add),
                     reads=[xbB, parB, caB], writes=[caB])
                S.op("dve", lambda e: e.scalar_tensor_tensor(out=ca[:], in0=xb[:, 3:3 + T], scalar=wc(3), in1=ca[:], op0=ALU.mult, op1=ALU.add),
                     reads=[xbB, parB, caB], writes=[caB])
                cs_, csB_ = csil[i2], csilB[i2]
                if fc < 24:
                    S.op("act", lambda e: e.activation(out=cs_[:], in_=ca[:], func=AF.Silu), reads=[caB], writes=[csB_])
                else:
                    S.op("act", lambda e: e.activation(out=cs_[:], in_=ca[:], func=AF.Silu, bias=colsB[l][:, 48 + fc - 24:48 + fc - 24 + 1]),
                         reads=[caB, parB], writes=[csB_])
                if fc >= 32:
                    dst = bT if fc < 34 else cT
                    S.op("dve", lambda e: e.tensor_copy(out=dst[:, fc % 2, :], in_=cs_[:]), reads=[csB_], writes=[bcTB])
                if fc < 34:
                    tb = 2 + i2
                    for b in range(NB):
                        transpose_f32(cs_[:, b * 128:(b + 1) * 128], csB_, tb, b * 128)
                    for b in range(NB):
                        src = pb[tb][:, b * 128:(b + 1) * 128]
                        if fc < 24:
                            S.op("act", lambda e: e.copy(out=qkv_tm[:, b, fc * 128:(fc + 1) * 128], in_=src), reads=[pbB[tb]], writes=[qkvB[b]])
                        elif fc < 32:
                            S.op("act", lambda e: e.copy(out=xs_tm[:, b, (fc - 24) * 128:(fc - 23) * 128], in_=src), reads=[pbB[tb]], writes=[xsB[b]])
                        else:
                            S.op("act", lambda e: e.copy(out=b_tm[:, b, (fc - 32) * 128:(fc - 31) * 128], in_=src), reads=[pbB[tb]], writes=[btmB[b]])

        if MXSTOP < 3:
            [w_next(s_) for s_ in specs[si:]]
            return
        for zi in range(4):
            ws, wsB = w_next(specs[si]); si += 1
            for b in range(NB):
                bank = 4 + b % 2
                for k in range(8):
                    S.op("pe", lambda e: e.matmul(pb[bank][:, :], lhsT=hT[:, k, b * 128:(b + 1) * 128], rhs=ws[:, k, :],
                                                  start=(k == 0), stop=(k == 7)), reads=[wsB, hTB], writes=[pbB[bank]], acc=(k > 0))
                S.op("act", lambda e: e.activation(out=zh[:, b, zi * 512:(zi + 1) * 512], in_=pb[bank][:, :], func=AF.Silu),
                     reads=[pbB[bank]], writes=[zhB])

        if MXSTOP < 4:
            [w_next(s_) for s_ in specs[si:]]
            return
        for b in range(NB):
            if MXSTOP >= 4.0:
                gdn_block(l, b)
            if MXSTOP >= 4.2:
                ssd_block(l, b)
        if MXSTOP < 5:
            [w_next(s_) for s_ in specs[si:]]
            return

        wo = []
        for half in range(2):
            o0 = w_next(specs[si]); o1 = w_next(specs[si + 1]); si += 2
            for b in range(NB):
                bank = 4 + b % 2
                for c in range(16):
                    ws, wsB = o0 if c < 8 else o1
                    S.op("pe", lambda e: e.matmul(pb[bank][:, :], lhsT=mixT[:, c, b * 128:(b + 1) * 128], rhs=ws[:, c % 8, :],
                                                  start=(c == 0), stop=(c == 15)), reads=[wsB, mixTB[b]], writes=[pbB[bank]], acc=(c > 0))
                xs_ = x_sb[:, b, half * 512:(half + 1) * 512]
                S.op("dve", lambda e: e.tensor_tensor(out=xs_, in0=pb[bank][:, :], in1=xs_, op=ALU.add), reads=[pbB[bank], xB[b]], writes=[xB[b]])

    def gdn_block(l, b):
        q = qkv_tm[:, b, 0:1024]; k = qkv_tm[:, b, 1024:2048]; v = qkv_tm[:, b, 2048:3072]
        S.op("pool", lambda e: e.tensor_tensor(out=fscr[0][:], in0=q, in1=q, op=ALU.mult), reads=[qkvB[b]], writes=[fB[0]])
        S.op("dve", lambda e: e.tensor_reduce(out=blk_small[:, 0:8], in_=f3(fscr[0][:], 8), axis=AX.X, op=ALU.add), reads=[fB[0]], writes=[bsB])
        S.op("pool", lambda e: e.tensor_tensor(out=fscr[1][:], in0=k, in1=k, op=ALU.mult), reads=[qkvB[b]], writes=[fB[1]])
        S.op("dve", lambda e: e.tensor_reduce(out=blk_small[:, 8:16], in_=f3(fscr[1][:], 8), axis=AX.X, op=ALU.add), reads=[fB[1]], writes=[bsB])
        rsqrt_small(blk_small[:, 0:16], 16, 1.0, [bsB], [bsB])
        S.op("dve", lambda e: e.tensor_scalar(out=blk_small[:, 0:8], in0=blk_small[:, 0:8], scalar1=128.0 ** -0.5, scalar2=None, op0=ALU.mult),
             reads=[bsB], writes=[bsB])
        S.op("dve", lambda e: e.tensor_tensor(out=blk_small[:, 16:24], in0=blk_small[:, 8:16], in1=edl[:, b, 0:8], op=ALU.mult),
             reads=[bsB, smB], writes=[bsB])
        qn, qnB = fscr[0], fB[0]
        kn, knB = fscr[1], fB[1]
        S.op("dve", lambda e: e.tensor_tensor(out=f3(qn[:], 8), in0=f3(q, 8), in1=bcl(blk_small[:, 0:8], 128), op=ALU.mult),
             reads=[qkvB[b], bsB], writes=[qnB])
        S.op("pool", lambda e: e.tensor_tensor(out=f3(kn[:], 8), in0=f3(k, 8), in1=bcl(blk_small[:, 8:16], 128), op=ALU.mult),
             reads=[qkvB[b], bsB], writes=[knB])
        kdec, kdecB = hscr[0], hB[0]
        S.op("dve", lambda e: e.tensor_tensor(out=f3(kdec[:], 8), in0=f3(k, 8), in1=bcl(blk_small[:, 16:24], 128), op=ALU.mult),
             reads=[qkvB[b], bsB], writes=[kdecB])
        vb, vbB = fscr[2], fB[2]
        S.op("pool", lambda e: e.tensor_tensor(out=f3(vb[:], 8), in0=f3(v, 8), in1=bcl(beta[:, b, :], 128), op=ALU.mult),
             reads=[qkvB[b], smB], writes=[vbB])
        if MXSTOP < 4.01:
            return
        qT, qTB = hscr[1], hB[1]
        kT, kTB = hscr[2], hB[2]
        for (src, srcB, dst, dstB, banks) in ((qn, qnB, qT, qTB, (0, 1)), (kn, knB, kT, kTB, (2, 3))):
            for hh in range(2):
                for c in range(4):
                    transpose_f32(src[:, (hh * 4 + c) * 128:(hh * 4 + c + 1) * 128], srcB, banks[hh], c * 128)
                S.op("act", lambda e: e.copy(out=dst[:, hh * 512:(hh + 1) * 512], in_=pb[banks[hh]][:, :]), reads=[pbB[banks[hh]]], writes=[dstB])
        if MXSTOP < 4.02:
            return
        tg, tgB = fscr[3], fB[3]
        S.op("dve", lambda e: e.tensor_tensor(out=f3(tg[:], 8), in0=bcm(triu[:], 8), in1=bcl(gcat[:, b, 0:8], 128), op=ALU.mult),
             reads=[cB, smB], writes=[tgB])
        if MXSTOP < 4.021:
            return
        dS, dSB = fscr[4], fB[4]
        dT, dTB = fscr[5], fB[5]
        for hh in range(2):
            for (mask, bank) in ((ustrR, 4 + hh), (loR, 6 + hh)):
                S.op("pe", lambda e: e.matmul(pb[bank][:, :], lhsT=ones[:], rhs=tg[:, hh * 512:(hh + 1) * 512], start=True, stop=False),
                     reads=[tgB, cB], writes=[pbB[bank]])
                S.op("pe", lambda e: e.matmul(pb[bank][:, :], lhsT=ident[:], rhs=mask[:].rearrange("p r j -> p (r j)"), start=False, stop=True),
                     reads=[cB], writes=[pbB[bank]], acc=True)
            if MXSTOP < 4.022:
                continue
            for c in range(4):
                h = hh * 4 + c
                S.op("act", lambda e: e.activation(out=dS[:, h * 128:(h + 1) * 128], in_=pb[4 + hh][:, c * 128:(c + 1) * 128], func=AF.Exp,
                                                   scale=-1.0, bias=biasA[:, b, h:h + 1]), reads=[pbB[4 + hh], smB], writes=[dSB])
                S.op("act", lambda e: e.activation(out=dT[:, h * 128:(h + 1) * 128], in_=pb[6 + hh][:, c * 128:(c + 1) * 128], func=AF.Exp,
                                                   scale=1.0, bias=ngc[:, b, h:h + 1]), reads=[pbB[6 + hh], smB], writes=[dTB])
        if MXSTOP < 4.03:
            return
        for hh in range(2):
            for c in range(4):
                h = hh * 4 + c
                S.op("pe", lambda e: e.matmul(pb[0 + hh][:, c * 128:(c + 1) * 128], lhsT=kT[:, h * 128:(h + 1) * 128], rhs=kT[:, h * 128:(h + 1) * 128],
                                              start=True, stop=True), reads=[kTB], writes=[pbB[0 + hh]], acc=(c > 0))
            for c in range(4):
                h = hh * 4 + c
                S.op("pe", lambda e: e.matmul(pb[2 + hh][:, c * 128:(c + 1) * 128], lhsT=kT[:, h * 128:(h + 1) * 128], rhs=qT[:, h * 128:(h + 1) * 128],
                                              start=True, stop=True), reads=[kTB, qTB], writes=[pbB[2 + hh]], acc=(c > 0))
        qkm, qkmB = hscr[3], hB[3]
        for hh in range(2):
            sl = slice(hh * 512, (hh + 1) * 512)
            S.op("dve", lambda e: e.scalar_tensor_tensor(out=dS[:, sl], in0=pb[0 + hh][:, :], scalar=-1.0, in1=dS[:, sl], op0=ALU.mult, op1=ALU.mult),
                 reads=[pbB[0 + hh], dSB], writes=[dSB])
            S.op("dve", lambda e: e.tensor_tensor(out=qkm[:, sl], in0=pb[2 + hh][:, :], in1=dT[:, sl], op=ALU.mult),
                 reads=[pbB[2 + hh], dTB], writes=[qkmB])
        if MXSTOP < 4.04:
            return
        Pb, PbB = dS, dSB
        Ptb, PtbB = pt32, pt32B
        Ntb, NtbB = dT, dTB
        Nt, NtB = dT, dTB
        if MXSTOP < 4.0401:
            return
        for hh in range(2):
            for c in range(4):
                transpose_f32(dS[:, (hh * 4 + c) * 128:(hh * 4 + c + 1) * 128], dSB, 0 + hh, c * 128)
            sl = slice(hh * 512, (hh + 1) * 512)
            if MXSTOP < 4.0402:
                continue
            S.op("act", lambda e: e.copy(out=Ptb[:, sl], in_=pb[0 + hh][:, :]), reads=[pbB[0 + hh]], writes=[PtbB])
            if MXSTOP < 4.0403:
                continue
            S.op("dve", lambda e: e.tensor_copy(out=Nt[:, sl], in_=pb[0 + hh][:, :]), reads=[pbB[0 + hh], qkmB], writes=[NtB])
        if MXSTOP < 4.0404:
            return
        if MXSTOP < 4.041:
            return
        for lev in range(int(os.environ.get("MK_NLEV", "6"))):
            for hh in range(2):
                for c in range(4):
                    h = hh * 4 + c
                    hs = slice(h * 128, (h + 1) * 128)
                    S.op("pe", lambda e: e.matmul(pb[0 + hh][:, c * 128:(c + 1) * 128], lhsT=Ptb[:, hs], rhs=Pb[:, hs], start=True, stop=True),
                         reads=[PtbB, PbB], writes=[pbB[0 + hh]], acc=(c > 0))
                for c in range(4):
                    h = hh * 4 + c
                    hs = slice(h * 128, (h + 1) * 128)
                    S.op("pe", lambda e: e.matmul(pb[2 + hh][:, c * 128:(c + 1) * 128], lhsT=Pb[:, hs], rhs=Ptb[:, hs], start=True, stop=True),
                         reads=[PtbB, PbB], writes=[pbB[2 + hh]], acc=(c > 0))
            for hh in range(2):
                sl = slice(hh * 512, (hh + 1) * 512)
                S.op("act", lambda e: e.copy(out=Pb[:, sl], in_=pb[0 + hh][:, :]), reads=[pbB[0 + hh]], writes=[PbB])
                S.op("act", lambda e: e.copy(out=Ptb[:, sl], in_=pb[2 + hh][:, :]), reads=[pbB[2 + hh]], writes=[PtbB])
            if MXSTOP < 4.042:
                continue
            for hh in range(2):
                for c in range(4):
                    h = hh * 4 + c
                    hs = slice(h * 128, (h + 1) * 128)
                    S.op("pe", lambda e: e.matmul(pb[4 + hh][:, c * 128:(c + 1) * 128], lhsT=Pb[:, hs], rhs=Ntb[:, hs], start=True, stop=True),
                         reads=[NtbB, PbB], writes=[pbB[4 + hh]], acc=(c > 0))
            for hh in range(2):
                sl = slice(hh * 512, (hh + 1) * 512)
                S.op("dve", lambda e: e.tensor_tensor(out=Nt[:, sl], in0=pb[2 + hh][:, :], in1=Nt[:, sl], op=ALU.add), reads=[pbB[2 + hh], NtB], writes=[NtB])
                S.op("dve", lambda e: e.tensor_tensor(out=Nt[:, sl], in0=pb[4 + hh][:, :], in1=Nt[:, sl], op=ALU.add), reads=[pbB[4 + hh], NtB], writes=[NtB])
        TtB, TtBB = Nt, NtB
        S.op("dve", lambda e: e.tensor_tensor(out=f3(TtB[:], 8), in0=f3(Nt[:], 8), in1=bcm(ident[:], 8), op=ALU.add), reads=[NtB, cB], writes=[TtBB])
        if MXSTOP < 4.05:
            return
        S.op("act", lambda e: e.copy(out=Sgb[:].rearrange("p h e -> p (h e)"), in_=Sg[l][:].rearrange("p h e -> p (h e)")), reads=[SgB[l]], writes=[SgbB])
        for hh in range(2):
            for c in range(4):
                h = hh * 4 + c
                hs = slice(h * 128, (h + 1) * 128)
                S.op("pe", lambda e: e.matmul(pb[0 + hh][:, c * 128:(c + 1) * 128], lhsT=kT[:, hs], rhs=Sgb[:, h, :], start=True, stop=True),
                     reads=[kTB, SgbB], writes=[pbB[0 + hh]], acc=(c > 0))
            for c in range(4):
                h = hh * 4 + c
                hs = slice(h * 128, (h + 1) * 128)
                S.op("pe", lambda e: e.matmul(pb[2 + hh][:, c * 128:(c + 1) * 128], lhsT=qT[:, hs], rhs=Sgb[:, h, :], start=True, stop=True),
                     reads=[qTB, SgbB], writes=[pbB[2 + hh]], acc=(c > 0))
        r2, r2B = vb, vbB
        t_, tB_ = fscr[3], fB[3]
        for hh in range(2):
            sl = slice(hh * 512, (hh + 1) * 512)
            S.op("dve", lambda e: e.tensor_tensor(out=f3(t_[:, sl], 4), in0=f3(pb[0 + hh][:, :], 4), in1=bcl(bg[:, b, hh * 4:(hh + 1) * 4], 128), op=ALU.mult),
                 reads=[pbB[0 + hh], smB], writes=[tB_])
        S.op("dve", lambda e: e.tensor_tensor(out=r2[:], in0=vb[:], in1=t_[:], op=ALU.subtract), reads=[vbB, tB_], writes=[r2B])
        for hh in range(2):
            for c in range(4):
                h = hh * 4 + c
                hs = slice(h * 128, (h + 1) * 128)
                S.op("pe", lambda e: e.matmul(pb[4 + hh][:, c * 128:(c + 1) * 128], lhsT=TtB[:, hs], rhs=r2[:, hs], start=True, stop=True),
                     reads=[TtBB, r2B], writes=[pbB[4 + hh]], acc=(c > 0))
        vn, vnB = hscr[4], hB[4]
        for hh in range(2):
            sl = slice(hh * 512, (hh + 1) * 512)
            S.op("act", lambda e: e.copy(out=vn[:, sl], in_=pb[4 + hh][:, :]), reads=[pbB[4 + hh]], writes=[vnB])
        o_, oB_ = fscr[4], fB[4]
        for hh in range(2):
            sl = slice(hh * 512, (hh + 1) * 512)
            S.op("dve", lambda e: e.tensor_tensor(out=f3(o_[:, sl], 4), in0=f3(pb[2 + hh][:, :], 4), in1=bcl(egc[:, b, hh * 4:(hh + 1) * 4], 128), op=ALU.mult),
                 reads=[pbB[2 + hh], smB, PbB], writes=[oB_])
        for hh in range(2):
            for c in range(4):
                h = hh * 4 + c
                hs = slice(h * 128, (h + 1) * 128)
                S.op("pe", lambda e: e.matmul(pb[6 + hh][:, c * 128:(c + 1) * 128], lhsT=qkm[:, hs], rhs=vn[:, hs], start=True, stop=True),
                     reads=[qkmB, vnB], writes=[pbB[6 + hh]], acc=(c > 0))
            for c in range(4):
                h = hh * 4 + c
                hs = slice(h * 128, (h + 1) * 128)
                S.op("pe", lambda e: e.matmul(pb[0 + hh][:, c * 128:(c + 1) * 128], lhsT=kdec[:, hs], rhs=vn[:, hs], start=True, stop=True),
                     reads=[kdecB, vnB], writes=[pbB[0 + hh]], acc=(c > 0))
        Sf = Sg[l][:].rearrange("p h e -> p (h e)")
        for hh in range(2):
            sl = slice(hh * 512, (hh + 1) * 512)
            S.op("dve", lambda e: e.tensor_tensor(out=o_[:, sl], in0=pb[6 + hh][:, :], in1=o_[:, sl], op=ALU.add), reads=[pbB[6 + hh], oB_], writes=[oB_])
            S.op("pool", lambda e: e.tensor_tensor(out=f3(Sf[:, sl], 4), in0=f3(Sf[:, sl], 4), in1=bcl(etot[:, b, hh * 4:(hh + 1) * 4], 128), op=ALU.mult),
                 reads=[SgB[l], smB, SgbB], writes=[SgB[l]])
            S.op("dve", lambda e: e.tensor_tensor(out=Sf[:, sl], in0=pb[0 + hh][:, :], in1=Sf[:, sl], op=ALU.add), reads=[pbB[0 + hh], SgB[l]], writes=[SgB[l]])
        if DUMP == "gdn_o":
            S.dma("sp", out_d[b * 128:(b + 1) * 128, :], o_[:], reads=[oB_])
        if DUMP == "gdn_S":
            S.dma("sp", out_d[b * 128:(b + 1) * 128, :], Sf, reads=[SgB[l]])
        if MXSTOP < 4.06:
            return
        sq, sqB = fscr[3], fB[3]
        S.op("pool", lambda e: e.tensor_tensor(out=sq[:], in0=o_[:], in1=o_[:], op=ALU.mult), reads=[oB_], writes=[sqB])
        S.op("dve", lambda e: e.tensor_reduce(out=blk_small[:, 32:40], in_=f3(sq[:], 8), axis=AX.X, op=ALU.add), reads=[sqB], writes=[bsB])
        rsqrt_small(blk_small[:, 32:40], 8, 1.0 / 128, [bsB], [bsB])
        S.op("dve", lambda e: e.tensor_tensor(out=f3(o_[:], 8), in0=f3(o_[:], 8), in1=bcl(blk_small[:, 32:40], 128), op=ALU.mult), reads=[oB_, bsB], writes=[oB_])
        S.op("dve", lambda e: e.tensor_tensor(out=o_[:], in0=o_[:], in1=zh[:, b, 0:1024], op=ALU.mult), reads=[oB_, zhB], writes=[oB_])
        for hh in range(2):
            bank = 2 + hh
            for c in range(4):
                transpose_f32(o_[:, (hh * 4 + c) * 128:(hh * 4 + c + 1) * 128], oB_, bank, c * 128)
            for c in range(4):
                S.op("act", lambda e: e.activation(out=mixT[:, hh * 4 + c, b * 128:(b + 1) * 128], in_=pb[bank][:, c * 128:(c + 1) * 128],
                                                   func=AF.Identity, scale=colA(l, "gnorm")), reads=[pbB[bank], parB], writes=[mixTB[b]])

    def ssd_block(l, b):
        xs = xs_tm[:, b, :]
        dt_ = spv[:, b, 16:32]
        xdt, xdtB = hscr[0], hB[0]
        xe, xeB = hscr[1], hB[1]
        S.op("dve", lambda e: e.tensor_tensor(out=f3(xdt[:], 16), in0=f3(xs, 16), in1=bcl(dt_, 64), op=ALU.mult), reads=[xsB[b], smB], writes=[xdtB])
        S.op("pool", lambda e: e.tensor_tensor(out=f3(xe[:], 16), in0=f3(xs, 16), in1=bcl(dte[:, b, :], 64), op=ALU.mult), reads=[xsB[b], smB], writes=[xeB])
        S.op("act", lambda e: e.copy(out=Ssb[:].rearrange("p h e -> p (h e)"), in_=Ss[l][:].rearrange("p h e -> p (h e)")), reads=[SsB[l]], writes=[SsbB])
        tg, tgB = fscr[0], fB[0]
        seg, segB = fscr[1], fB[1]
        MT = (hscr[2], hscr[3]); MTB = (hB[2], hB[3])
        y, yB = fscr[2], fB[2]
        bs_ = slice(b * 128, (b + 1) * 128)
        for g in range(2):
            S.op("pe", lambda e: e.matmul(pb[4][:, g * 128:(g + 1) * 128], lhsT=bT[:, g, bs_], rhs=cT[:, g, bs_], start=True, stop=True),
                 reads=[bcTB], writes=[pbB[4]], acc=(g > 0))
        for g in range(2):
            S.op("dve", lambda e: e.tensor_tensor(out=f3(tg[:], 8), in0=bcm(triu[:], 8), in1=bcl(gcat[:, b, 8 + g * 8:16 + g * 8], 128), op=ALU.mult),
                 reads=[cB, smB], writes=[tgB])
            for hh in range(2):
                bank = 6 + hh
                S.op("pe", lambda e: e.matmul(pb[bank][:, :], lhsT=ones[:], rhs=tg[:, hh * 512:(hh + 1) * 512], start=True, stop=False),
                     reads=[tgB, cB], writes=[pbB[bank]])
                S.op("pe", lambda e: e.matmul(pb[bank][:, :], lhsT=ident[:], rhs=loR[:].rearrange("p r j -> p (r j)"), start=False, stop=True),
                     reads=[cB], writes=[pbB[bank]], acc=True)
                for c in range(4):
                    j = hh * 4 + c
                    h = g * 8 + j
                    S.op("act", lambda e: e.activation(out=seg[:, j * 128:(j + 1) * 128], in_=pb[bank][:, c * 128:(c + 1) * 128], func=AF.Exp,
                                                       scale=1.0, bias=ngc[:, b, 8 + h:9 + h]), reads=[pbB[bank], smB], writes=[segB])
            S.op("dve", lambda e: e.tensor_tensor(out=f3(MT[g][:], 8), in0=f3(seg[:], 8), in1=bcm(pb[4][:, g * 128:(g + 1) * 128], 8), op=ALU.mult),
                 reads=[segB, pbB[4]], writes=[MTB[g]])
        for g in range(2):
            for j in range(8):
                h = g * 8 + j
                S.op("pe", lambda e: e.matmul(pb[0 + g][:, j * 64:(j + 1) * 64], lhsT=MT[g][:, j * 128:(j + 1) * 128], rhs=xdt[:, h * 64:(h + 1) * 64],
                                              start=True, stop=True), reads=[MTB[g], xdtB], writes=[pbB[0 + g]], acc=(j > 0))
            S.op("pe", lambda e: e.matmul(pb[2 + g][:, :], lhsT=cT[:, g, bs_], rhs=Ssb[:, g * 8:(g + 1) * 8, :].rearrange("p h e -> p (h e)"),
                                          start=True, stop=True), reads=[bcTB, SsbB], writes=[pbB[2 + g]])
        for g in range(2):
            sl = slice(g * 512, (g + 1) * 512)
            S.op("dve", lambda e: e.tensor_tensor(out=f3(y[:, sl], 8), in0=f3(pb[2 + g][:, :], 8), in1=bcl(egc[:, b, 8 + g * 8:16 + g * 8], 64), op=ALU.mult),
                 reads=[pbB[2 + g], smB], writes=[yB])
            S.op("dve", lambda e: e.tensor_tensor(out=y[:, sl], in0=pb[0 + g][:, :], in1=y[:, sl], op=ALU.add), reads=[pbB[0 + g], yB], writes=[yB])
        Sf = Ss[l][:].rearrange("p h e -> p (h e)")
        for g in range(2):
            sl = slice(g * 512, (g + 1) * 512)
            S.op("pe", lambda e: e.matmul(pb[6 + g][:, :], lhsT=b_tm[:, b, g * 128:(g + 1) * 128], rhs=xe[:, sl], start=True, stop=True),
                 reads=[btmB[b], xeB], writes=[pbB[6 + g]])
            S.op("pool", lambda e: e.tensor_tensor(out=f3(Sf[:, sl], 8), in0=f3(Sf[:, sl], 8), in1=bcl(etot[:, b, 8 + g * 8:16 + g * 8], 64), op=ALU.mult),
                 reads=[SsB[l], smB, SsbB], writes=[SsB[l]])
            S.op("dve", lambda e: e.tensor_tensor(out=Sf[:, sl], in0=pb[6 + g][:, :], in1=Sf[:, sl], op=ALU.add), reads=[pbB[6 + g], SsB[l]], writes=[SsB[l]])
        t_, tB_ = fscr[3], fB[3]
        S.op("pool", lambda e: e.tensor_tensor(out=f3(t_[:], 16), in0=f3(xs, 16), in1=bcl(dsk[l][:, :], 64), op=ALU.mult), reads=[xsB[b], parB], writes=[tB_])
        S.op("dve", lambda e: e.tensor_tensor(out=y[:], in0=y[:], in1=t_[:], op=ALU.add), reads=[yB, tB_], writes=[yB])
        S.op("dve", lambda e: e.tensor_tensor(out=y[:], in0=y[:], in1=zh[:, b, 1024:2048], op=ALU.mult), reads=[yB, zhB], writes=[yB])
        for g in range(2):
            S.op("act", lambda e: e.activation(out=t_[:, g * 512:(g + 1) * 512], in_=y[:, g * 512:(g + 1) * 512], func=AF.Square,
                                               accum_out=blk_small[:, 40 + g:41 + g]), reads=[yB], writes=[tB_, bsB])
        rsqrt_small(blk_small[:, 40:42], 2, 1.0 / 512, [bsB], [bsB])
        S.op("dve", lambda e: e.tensor_tensor(out=f3(y[:], 2), in0=f3(y[:], 2), in1=bcl(blk_small[:, 40:42], 512), op=ALU.mult), reads=[yB, bsB], writes=[yB])
        for hh in range(2):
            bank = 4 + hh
            for c in range(4):
                transpose_f32(y[:, (hh * 4 + c) * 128:(hh * 4 + c + 1) * 128], yB, bank, c * 128)
            for c in range(4):
                cc = hh * 4 + c
                S.op("act", lambda e: e.activation(out=mixT[:, 8 + cc, b * 128:(b + 1) * 128], in_=pb[bank][:, c * 128:(c + 1) * 128],
                                                   func=AF.Identity, scale=colsB[l][:, 60 + cc:61 + cc]), reads=[pbB[bank], parB], writes=[mixTB[b]])

    for t in range(n_tiles):
        for b in range(NB):
            r0 = t * T + b * 128
            S.dma("sp", x_sb[:, b, :], x_d[r0:r0 + 128, :], writes=[xB[b]])
        for l in range(depth):
            if "F1" in DBG:
                ffn(l, 1)
            else:
                [w_next(s_) for s_ in ffn_specs(l, 1)]
            if "MX" in DBG:
                mixer(l)
            else:
                [w_next(s_) for s_ in mix_specs(l)]
            if "F2" in DBG:
                ffn(l, 2)
            else:
                [w_next(s_) for s_ in ffn_specs(l, 2)]
        for b in range(NB):
            S.op("act", lambda e: e.activation(out=fscr[0][:], in_=x_sb[:, b, :], func=AF.Square, accum_out=stat[:, b:b + 1]),
                 reads=[xB[b]], writes=[fB[0], statB])
        rsqrt_small(stat[:, 0:NB], NB, 1.0 / D, [statB], [statB])
        for b in range(NB):
            ob, obB = fscr[2 + b % 4], fB[2 + b % 4]
            S.op("dve", lambda e: e.scalar_tensor_tensor(out=ob[:], in0=x_sb[:, b, :], scalar=stat[:, b:b + 1], in1=fin_bc[:],
                                                         op0=ALU.mult, op1=ALU.mult), reads=[xB[b], statB, parB], writes=[obB])
            r0 = t * T + b * 128
            if not DUMP:
                S.dma("sp", out_d[r0:r0 + 128, :], ob[:], reads=[obB])
    S._need("sp", {"d%d" % i: S.cnt["d%d" % i] for i in range(S.ndma) if S.cnt["d%d" % i] > 0})
    print("program: %d instructions, %d waits" % (S.nins, S.nwait), S.tot)


_PROG = {}


def _get_prog(n_tiles, depth=DEPTH):
    key = (n_tiles, depth)
    if key not in _PROG:
        _PROG[key] = build_program(n_tiles, depth)
    return _PROG[key]


W_KEYS = ["ffn1_w_gate", "ffn1_w_up", "ffn1_w_down", "w_in", "w_out", "ffn2_w_gate", "ffn2_w_up", "ffn2_w_down"]
P_KEYS = ["ffn1_norm", "mix_norm", "ffn2_norm", "final_norm", "gdn_conv_w", "gdn_a_log", "gdn_dt_bias", "gdn_norm_w",
          "ssm_conv_w", "ssm_conv_b", "ssm_a_log", "ssm_dt_bias", "ssm_d", "ssm_norm_w"]


def run(inputs, n_cores=8, n_tiles=SEQ // T, depth=DEPTH):
    nc = _get_prog(n_tiles, depth)
    shared = {k: np.ascontiguousarray(np.asarray(inputs[k], dtype=np.float32)) for k in W_KEYS + P_KEYS}
    x = np.asarray(inputs["x"], dtype=np.float32)
    in_maps = []
    for c in range(n_cores):
        m = dict(shared)
        m["x"] = np.ascontiguousarray(x[c, : n_tiles * T, :])
        in_maps.append(m)
    res = run_bass_kernel_spmd(nc, in_maps, core_ids=list(range(n_cores)))
    return np.stack([np.asarray(r["out"]) for r in res.results], axis=0)


def kernel(**inputs):
    return run(inputs).astype(np.float32)
```

```python
import contextlib
import os
import numpy as np
import concourse.bass as bass
import concourse.mybir as mybir
from concourse.bass_utils import run_bass_kernel_spmd

F32 = mybir.dt.float32
BF16 = mybir.dt.bfloat16
AF = mybir.ActivationFunctionType
ALU = mybir.AluOpType
AX = mybir.AxisListType

D = 1024
DFF = 2816
SEQ = 8192
DEPTH = 2
INW = 6688
NB = 2
T = 128 * NB
EPS = 1e-6
BIG = float(os.environ.get("MK_BIG", "2000.0"))
KSLOT = 8
NSLOT = 6
LOOKAHEAD = 3
SEM_LIM = 16000
DBG = os.environ.get("MK_DBG", "F1,MX,F2").split(",")
MXSTOP = float(os.environ.get("MK_MXSTOP", "99"))
DUMP = os.environ.get("MK_DUMP", "")

O_QKV, O_GZ, O_GB, O_GA, O_SZ, O_XBC, O_DT = 0, 3072, 4096, 4104, 4112, 5136, 6672


class Buf:
    __slots__ = ("w", "r", "name", "excl")

    def __init__(self, name="", excl=False):
        self.w = {}
        self.r = {}
        self.name = name
        self.excl = excl


class Sched:
    def __init__(self, nc, es):
        self.nc = nc
        self.eng = {"pe": nc.tensor, "act": nc.scalar, "dve": nc.vector, "pool": nc.gpsimd, "sp": nc.sync}
        self.sems = {}
        self.cnt = {}
        self.seen = {e: {} for e in self.eng}
        self.es = es
        self.tot = {}
        for e in ("pe", "act", "dve", "pool"):
            self.tot[e] = 0
        self.ndma = 16
        for i in range(self.ndma):
            self.sems["d%d" % i] = es.enter_context(nc.semaphore("s_d%d" % i))
            self.cnt["d%d" % i] = 0
        self.dma_i = 0
        self.nwait = 0
        self.nins = 0

    def _need(self, eng, deps):
        sn = self.seen[eng]
        for k, c in deps.items():
            if sn.get(k, 0) < c:
                self.eng[eng].wait_ge(self.sems[k], c)
                sn[k] = c
                self.nwait += 1

    @staticmethod
    def _flat(bufs):
        out = []
        for b in bufs:
            if isinstance(b, (tuple, list)):
                out.extend(Sched._flat(b))
            else:
                out.append(b)
        return out

    def _sync(self, eng, reads, writes, acc):
        deps = {}
        for b in reads:
            for k, c in b.w.items():
                if deps.get(k, 0) < c:
                    deps[k] = c
            if b.excl:
                for k, c in b.r.items():
                    if not k.startswith(eng + "_") and deps.get(k, 0) < c:
                        deps[k] = c
        for b in writes:
            for k, c in b.w.items():
                if acc and k.startswith("pe_"):
                    continue
                if deps.get(k, 0) < c:
                    deps[k] = c
            for k, c in b.r.items():
                if deps.get(k, 0) < c:
                    deps[k] = c
        self._need(eng, deps)

    def _mark(self, key, c, reads, writes):
        for b in reads:
            if b.r.get(key, 0) < c:
                b.r[key] = c
        for b in writes:
            b.w = {key: c}
            b.r = {}

    def op(self, eng, ins_fn, reads=(), writes=(), acc=False):
        reads = self._flat(reads); writes = self._flat(writes)
        self._sync(eng, reads, writes, acc)
        ins = ins_fn(self.eng[eng])
        key = "%s_%d" % (eng, self.tot[eng] // SEM_LIM)
        self.tot[eng] += 1
        if key not in self.sems:
            self.sems[key] = self.es.enter_context(self.nc.semaphore("s_" + key))
            self.cnt[key] = 0
        self.cnt[key] += 1
        ins.then_inc(self.sems[key], 1)
        self._mark(key, self.cnt[key], reads, writes)
        self.nins += 1
        return ins

    def dma(self, eng, out, in_, reads=(), writes=(), **kw):
        reads = self._flat(reads); writes = self._flat(writes)
        key = "d%d" % (self.dma_i % self.ndma)
        self.dma_i += 1
        if self.cnt[key] > 0:
            self._need(eng, {key: self.cnt[key]})
        self._sync(eng, reads, writes, False)
        ins = self.eng[eng].dma_start(out=out, in_=in_, **kw)
        self.cnt[key] += 16
        ins.then_inc(self.sems[key], 16)
        self._mark(key, self.cnt[key], reads, writes)
        self.nins += 1
        return ins

    def wait_all(self, eng, bufs):
        deps = {}
        for b in bufs:
            for dd in (b.w, b.r):
                for k, c in dd.items():
                    if deps.get(k, 0) < c:
                        deps[k] = c
        self._need(eng, deps)


def bcl(ap2, n):
    return ap2.unsqueeze(2).to_broadcast([ap2.shape[0], ap2.shape[1], n])


def bcm(ap2, h):
    return ap2.unsqueeze(1).to_broadcast([ap2.shape[0], h, ap2.shape[1]])


def build_program(n_tiles, depth=DEPTH):
    nc = bass.Bass("TRN2", target_bir_lowering=False)
    es = contextlib.ExitStack()
    with es:
        _build(nc, es, n_tiles, depth)
    return nc


def _build(nc, es, n_tiles, depth):
    S = Sched(nc, es)
    ntok = n_tiles * T

    def din(name, shape):
        return nc.dram_tensor(name, shape, F32, kind="ExternalInput").ap()

    x_d = din("x", [ntok, D])
    out_d = nc.dram_tensor("out", [ntok, D], F32, kind="ExternalOutput").ap()
    wnames = {"ffn1_w_gate": (D, DFF), "ffn1_w_up": (D, DFF), "ffn1_w_down": (DFF, D), "w_in": (D, INW),
              "w_out": (2 * D, D), "ffn2_w_gate": (D, DFF), "ffn2_w_up": (D, DFF), "ffn2_w_down": (DFF, D)}
    w_f32 = {k: din(k, [DEPTH, r, c]) for k, (r, c) in wnames.items()}
    w_bf = {k: nc.dram_tensor(k + "_bf", [DEPTH, r, c], BF16, kind="Internal").ap() for k, (r, c) in wnames.items()}
    w_buf = {(k, l): Buf("w_%s_%d" % (k, l)) for k in wnames for l in range(DEPTH)}
    pn = {"ffn1_norm": [DEPTH, D], "mix_norm": [DEPTH, D], "ffn2_norm": [DEPTH, D], "final_norm": [D],
          "gdn_conv_w": [DEPTH, 4, 3072], "gdn_a_log": [DEPTH, 8], "gdn_dt_bias": [DEPTH, 8],
          "gdn_norm_w": [DEPTH, 128], "ssm_conv_w": [DEPTH, 4, 1536], "ssm_conv_b": [DEPTH, 1536],
          "ssm_a_log": [DEPTH, 16], "ssm_dt_bias": [DEPTH, 16], "ssm_d": [DEPTH, 16], "ssm_norm_w": [DEPTH, D]}
    p_d = {k: din(k, s) for k, s in pn.items()}

    def sb(name, shape, dt=F32):
        return es.enter_context(nc.sbuf_tensor(name, shape, dt))

    def ps(name):
        return es.enter_context(nc.psum_tensor(name, [128, 512], F32))

    ident = sb("ident", [128, 128]); ones = sb("ones", [128, 128]); triu = sb("triu", [128, 128])
    ustrR = sb("ustrR", [128, 4, 128]); loR = sb("loR", [128, 4, 128])
    cB = Buf("consts")
    S.op("pool", lambda e: e.memset(ident[:], 0.0), writes=[cB])
    S.op("pool", lambda e: e.affine_select(out=ident[:], in_=ident[:], pattern=[[-1, 128]], compare_op=ALU.not_equal,
                                           fill=1.0, base=0, channel_multiplier=1), reads=[cB], writes=[cB])
    S.op("pool", lambda e: e.memset(ones[:], 1.0), writes=[cB])
    S.op("pool", lambda e: e.memset(triu[:], 1.0), writes=[cB])
    S.op("pool", lambda e: e.affine_select(out=triu[:], in_=triu[:], pattern=[[1, 128]], compare_op=ALU.is_ge,
                                           fill=0.0, base=0, channel_multiplier=-1), reads=[cB], writes=[cB])
    S.op("pool", lambda e: e.memset(ustrR[:], 0.0), writes=[cB])
    S.op("pool", lambda e: e.memset(loR[:], 0.0), writes=[cB])
    for r in range(4):
        S.op("pool", lambda e, r=r: e.affine_select(out=ustrR[:, r, :], in_=ustrR[:, r, :], pattern=[[-1, 128]],
                                                    compare_op=ALU.is_gt, fill=BIG, base=0, channel_multiplier=1),
             reads=[cB], writes=[cB])
        S.op("pool", lambda e, r=r: e.affine_select(out=loR[:, r, :], in_=loR[:, r, :], pattern=[[1, 128]],
                                                    compare_op=ALU.is_ge, fill=-BIG, base=0, channel_multiplier=-1),
             reads=[cB], writes=[cB])

    pb = [ps("pb%d" % i) for i in range(8)]
    pbB = [Buf("pb%d" % i, excl=True) for i in range(8)]

    NRA, NRB = 121, 68
    colsA = [sb("colsA%d" % l, [128, NRA]) for l in range(DEPTH)]
    colsB = [sb("colsB%d" % l, [128, NRB]) for l in range(DEPTH)]
    stg = sb("stg", [128, 128])
    stgB = Buf("stg")
    parB = Buf("params")
    fin_bc = sb("fin_bc", [128, D])
    S.dma("sp", fin_bc[:], p_d["final_norm"].partition_broadcast(128), writes=[parB])
    hb = [sb("hb%d" % l, [128, 64]) for l in range(DEPTH)]
    negA = [sb("negA%d" % l, [128, 24]) for l in range(DEPTH)]
    spb = [sb("spb%d" % l, [128, 32]) for l in range(DEPTH)]
    dsk = [sb("dsk%d" % l, [128, 16]) for l in range(DEPTH)]
    for l in range(DEPTH):
        def stage(rows_list, dst, nrows):
            r0 = 0
            for ap2 in rows_list:
                n = ap2.shape[0]
                S.dma("sp", stg[r0:r0 + n, :], ap2, writes=[stgB])
                r0 += n
            assert r0 == nrows, (r0, nrows)
            S.op("pe", lambda e: e.transpose(pb[0][:, 0:nrows], stg[0:nrows, :], ident[0:nrows, 0:nrows]),
                 reads=[stgB, cB], writes=[pbB[0]])
            S.op("dve", lambda e: e.tensor_copy(out=dst[:, 0:nrows], in_=pb[0][:, 0:nrows]), reads=[pbB[0]], writes=[parB])

        stage([p_d["gdn_conv_w"][l].rearrange("k (c p) -> (k c) p", p=128),
               p_d["ffn1_norm"][l].rearrange("(c p) -> c p", p=128),
               p_d["mix_norm"][l].rearrange("(c p) -> c p", p=128),
               p_d["ffn2_norm"][l].rearrange("(c p) -> c p", p=128),
               p_d["gdn_norm_w"][l].rearrange("(c p) -> c p", p=128)], colsA[l], NRA)
        stage([p_d["ssm_conv_w"][l].rearrange("k (c p) -> (k c) p", p=128),
               p_d["ssm_conv_b"][l].rearrange("(c p) -> c p", p=128),
               p_d["ssm_norm_w"][l].rearrange("(c p) -> c p", p=128)], colsB[l], NRB)
        S.dma("sp", hb[l][:, 0:8], p_d["gdn_a_log"][l].partition_broadcast(128), writes=[parB])
        S.dma("sp", hb[l][:, 8:24], p_d["ssm_a_log"][l].partition_broadcast(128), writes=[parB])
        S.op("pool", lambda e: e.memset(spb[l][:, 0:8], 0.0), writes=[parB])
        S.dma("sp", spb[l][:, 8:16], p_d["gdn_dt_bias"][l].partition_broadcast(128), writes=[parB])
        S.dma("sp", spb[l][:, 16:32], p_d["ssm_dt_bias"][l].partition_broadcast(128), writes=[parB])
        S.dma("sp", dsk[l][:, :], p_d["ssm_d"][l].partition_broadcast(128), writes=[parB])
        S.op("act", lambda e: e.activation(out=negA[l][:, :], in_=hb[l][:, 0:24], func=AF.Exp), reads=[parB], writes=[parB])
        S.op("dve", lambda e: e.tensor_scalar(out=negA[l][:, :], in0=negA[l][:, :], scalar1=-1.0, scalar2=None, op0=ALU.mult),
             reads=[parB], writes=[parB])

    def colA(l, kind, c=0):
        base = {"conv": 0, "ffn1": 96, "mix": 104, "ffn2": 112, "gnorm": 120}[kind]
        return colsA[l][:, base + c:base + c + 1]

    cast_order = ["ffn1_w_gate", "ffn1_w_up", "ffn1_w_down", "w_in", "w_out", "ffn2_w_gate", "ffn2_w_up", "ffn2_w_down"]
    for l in range(depth):
        for k in cast_order:
            r, c = wnames[k]
            step = 256
            for r0 in range(0, r, step):
                S.dma("pool", w_bf[k][l, r0:r0 + step, :], w_f32[k][l, r0:r0 + step, :], writes=[], reads=[])
                key = "d%d" % ((S.dma_i - 1) % S.ndma)
                w_buf[(k, l)].w[key] = S.cnt[key]

    slots = [sb("wslot%d" % i, [128, KSLOT, 512], BF16) for i in range(NSLOT)]
    slotB = [Buf("slot%d" % i) for i in range(NSLOT)]

    def wview(k, l):
        return w_bf[k][l].rearrange("(k p) f -> p k f", p=128)

    def ffn_specs(l, which):
        sp = []
        g, u, dn = ("ffn%d_w_gate" % which, "ffn%d_w_up" % which, "ffn%d_w_down" % which)
        for c0 in range(0, DFF, 512):
            cn = min(512, DFF - c0)
            sp.append([((g, l), 0, 8, c0, cn, 0)])
            sp.append([((u, l), 0, 8, c0, cn, 0)])
        for half in range(2):
            sp.append([((dn, l), 0, 8, half * 512, 512, 0)])
            sp.append([((dn, l), 8, 8, half * 512, 512, 0)])
            sp.append([((dn, l), 16, 6, half * 512, 512, 0)])
        return sp

    def mix_specs(l):
        sp = [[(("w_in", l), 0, 8, O_GB, 16, 0), (("w_in", l), 0, 8, O_DT, 16, 16)]]
        for c0 in range(0, 3072, 512):
            sp.append([(("w_in", l), 0, 8, O_QKV + c0, 512, 0)])
        for c0 in range(0, 1536, 512):
            sp.append([(("w_in", l), 0, 8, O_XBC + c0, 512, 0)])
        for c0 in (O_GZ, O_GZ + 512, O_SZ, O_SZ + 512):
            sp.append([(("w_in", l), 0, 8, c0, 512, 0)])
        for half in range(2):
            sp.append([(("w_out", l), 0, 8, half * 512, 512, 0)])
            sp.append([(("w_out", l), 8, 8, half * 512, 512, 0)])
        return sp

    schedule = []
    for t in range(n_tiles):
        for l in range(depth):
            schedule += ffn_specs(l, 1) + mix_specs(l) + ffn_specs(l, 2)
    wst = {"issued": 0, "taken": 0}

    def w_issue():
        i = wst["issued"]
        if i >= len(schedule):
            return
        slot = i % NSLOT
        for (wk, k0, kn, c0, cn, dc) in schedule[i]:
            S.dma("sp", slots[slot][:, 0:kn, dc:dc + cn], wview(*wk)[:, k0:k0 + kn, c0:c0 + cn],
                  reads=[w_buf[wk]], writes=[])
            key = "d%d" % ((S.dma_i - 1) % S.ndma)
            slotB[slot].w[key] = S.cnt[key]
        wst["issued"] += 1

    def w_issue_guarded():
        i = wst["issued"]
        if i >= len(schedule):
            return
        slot = i % NSLOT
        S.wait_all("sp", [slotB[slot]])
        slotB[slot].w = {}
        slotB[slot].r = {}
        w_issue()

    def w_next(expect):
        i = wst["taken"]
        assert schedule[i] == expect, (i, schedule[i], expect)
        while wst["issued"] < min(len(schedule), i + 1 + LOOKAHEAD):
            w_issue_guarded()
        wst["taken"] += 1
        return slots[i % NSLOT], slotB[i % NSLOT]

    x_sb = sb("x_sb", [128, NB, D]); xB = [Buf("x%d" % b) for b in range(NB)]
    fscr = [sb("fscr%d" % i, [128, D]) for i in range(6)]; fB = [(Buf("fscr%da" % i), Buf("fscr%db" % i)) for i in range(6)]
    hscr = [sb("hscr%d" % i, [128, D], BF16) for i in range(5)]; hB = [(Buf("hscr%da" % i), Buf("hscr%db" % i)) for i in range(5)]
    pt32 = sb("pt32", [128, D]); pt32B = (Buf("pt32a"), Buf("pt32b"))
    hT = sb("hT", [128, 8, T], BF16); hTB = Buf("hT")
    zh = sb("zh", [128, NB, 2048])
    zhB = Buf("zh")
    hid = zh[:].rearrange("p b f -> p (b f)").bitcast(BF16)[:, 0:22 * T].rearrange("p (c t) -> p c t", t=T)
    stat = sb("stat", [128, 64]); statB = Buf("stat")
    qkv_tm = sb("qkv_tm", [128, NB, 3072]); qkvB = [Buf("qkv%d" % b) for b in range(NB)]
    xs_tm = sb("xs_tm", [128, NB, 1024]); xsB = [Buf("xs%d" % b) for b in range(NB)]
    b_tm = sb("b_tm", [128, NB, 256], BF16); btmB = [Buf("btm%d" % b) for b in range(NB)]
    bT = sb("bT", [128, 2, T], BF16); cT = sb("cT", [128, 2, T], BF16); bcTB = Buf("bcT")
    mixT = sb("mixT", [128, 16, T], BF16); mixTB = [Buf("mixT%d" % b) for b in range(NB)]
    xbuf = [sb("xbuf%d" % i, [128, T + 3]) for i in range(2)]; xbufB = [Buf("xbuf%d" % i) for i in range(2)]
    cacc = [sb("cacc%d" % i, [128, T]) for i in range(2)]; caccB = [Buf("cacc%d" % i) for i in range(2)]
    csil = [sb("csil%d" % i, [128, T]) for i in range(2)]; csilB = [Buf("csil%d" % i) for i in range(2)]
    tails = [sb("tails%d" % l, [128, 36, 3]) for l in range(DEPTH)]
    tailB = [[Buf("tail%d_%d" % (l, c)) for c in range(36)] for l in range(DEPTH)]
    sm = sb("sm", [128, NB, 32]); spv = sb("spv", [128, NB, 32]); beta = sb("beta", [128, NB, 8])
    gcat = sb("gcat", [128, NB, 24]); gct = sb("gct", [128, NB, 64]); egc = sb("egc", [128, NB, 24])
    edl = sb("edl", [128, NB, 24]); etot = sb("etot", [128, NB, 24]); biasA = sb("biasA", [128, NB, 8])
    ngc = sb("ngc", [128, NB, 24]); bg = sb("bg", [128, NB, 8]); dte = sb("dte", [128, NB, 16])
    smB = Buf("small")
    blk_small = sb("blk_small", [128, 64]); bsB = Buf("blk_small")
    Sg = [sb("Sg%d" % l, [128, 8, 128]) for l in range(DEPTH)]; SgB = [Buf("Sg%d" % l) for l in range(DEPTH)]
    Ss = [sb("Ss%d" % l, [128, 16, 64]) for l in range(DEPTH)]; SsB = [Buf("Ss%d" % l) for l in range(DEPTH)]
    for l in range(DEPTH):
        S.op("pool", lambda e: e.memset(Sg[l][:], 0.0), writes=[SgB[l]])
        S.op("pool", lambda e: e.memset(Ss[l][:], 0.0), writes=[SsB[l]])
        S.op("pool", lambda e: e.memset(tails[l][:], 0.0), writes=tailB[l])
    Sgb = sb("Sgb", [128, 8, 128], BF16); SgbB = Buf("Sgb")
    Ssb = sb("Ssb", [128, 16, 64], BF16); SsbB = Buf("Ssb")

    f3 = lambda ap, h: ap.rearrange("p (h e) -> p h e", h=h)

    def transpose_f32(src_ap, srcB, bank, col0, reads_extra=()):
        S.op("pe", lambda e: e.transpose(pb[bank][:, col0:col0 + 128], src_ap, ident[:]),
             reads=[srcB, cB] + list(reads_extra), writes=[pbB[bank]], acc=True)

    def rsqrt_small(ap, n, scale, readsB, writesB):
        S.op("act", lambda e: e.activation(out=ap, in_=ap, func=AF.Ln, bias=EPS, scale=scale), reads=readsB, writes=writesB)
        S.op("act", lambda e: e.activation(out=ap, in_=ap, func=AF.Exp, scale=-0.5), reads=writesB, writes=writesB)

    def norm_T(wcol_fn):
        for b in range(NB):
            S.op("act", lambda e: e.activation(out=fscr[0][:], in_=x_sb[:, b, :], func=AF.Square, accum_out=stat[:, b:b + 1]),
                 reads=[xB[b]], writes=[fB[0], statB])
        rsqrt_small(stat[:, 0:NB], NB, 1.0 / D, [statB], [statB])
        for b in range(NB):
            S.op("dve", lambda e: e.tensor_scalar(out=fscr[1][:], in0=x_sb[:, b, :], scalar1=stat[:, b:b + 1], scalar2=None,
                                                  op0=ALU.mult), reads=[xB[b], statB], writes=[fB[1]])
            for half in range(2):
                bank = 6 + half
                for c in range(4):
                    cc = half * 4 + c
                    transpose_f32(fscr[1][:, cc * 128:(cc + 1) * 128], fB[1], bank, c * 128)
                for c in range(4):
                    cc = half * 4 + c
                    S.op("act", lambda e: e.activation(out=hT[:, cc, b * 128:(b + 1) * 128], in_=pb[bank][:, c * 128:(c + 1) * 128],
                                                       func=AF.Identity, scale=wcol_fn(cc)), reads=[pbB[bank], parB], writes=[hTB])

    def ffn(l, which):
        kind = "ffn%d" % which
        norm_T(lambda cc: colA(l, kind, cc))
        specs = ffn_specs(l, which)
        si = 0
        hidB = zhB
        for c0 in range(0, DFF, 512):
            cn = min(512, DFF - c0)
            gs, gsB = w_next(specs[si]); us, usB = w_next(specs[si + 1]); si += 2
            for c in range(cn // 128):
                fc = c0 // 128 + c
                pg, pu = (0, 1) if fc % 2 == 0 else (2, 3)
                for k in range(8):
                    S.op("pe", lambda e: e.matmul(pb[pg][:, 0:T], lhsT=gs[:, k, c * 128:(c + 1) * 128], rhs=hT[:, k, :],
                                                  start=(k == 0), stop=(k == 7)), reads=[gsB, hTB], writes=[pbB[pg]], acc=(k > 0))
                for k in range(8):
                    S.op("pe", lambda e: e.matmul(pb[pu][:, 0:T], lhsT=us[:, k, c * 128:(c + 1) * 128], rhs=hT[:, k, :],
                                                  start=(k == 0), stop=(k == 7)), reads=[usB, hTB], writes=[pbB[pu]], acc=(k > 0))
                sc = fscr[2 + fc % 2]; scB = fB[2 + fc % 2]
                S.op("act", lambda e: e.activation(out=sc[:, 0:T], in_=pb[pg][:, 0:T], func=AF.Silu), reads=[pbB[pg]], writes=[scB])
                S.op("dve", lambda e: e.tensor_tensor(out=hid[:, fc, :], in0=pb[pu][:, 0:T], in1=sc[:, 0:T], op=ALU.mult),
                     reads=[pbB[pu], scB], writes=[hidB])
        for half in range(2):
            dsl = [w_next(specs[si]), w_next(specs[si + 1]), w_next(specs[si + 2])]; si += 3
            for b in range(NB):
                bank = 4 + b % 2
                for c in range(22):
                    ws, wsB = dsl[c // 8]
                    S.op("pe", lambda e: e.matmul(pb[bank][:, :], lhsT=hid[:, c, b * 128:(b + 1) * 128], rhs=ws[:, c % 8, :],
                                                  start=(c == 0), stop=(c == 21)), reads=[wsB, hidB], writes=[pbB[bank]], acc=(c > 0))
                xs_ = x_sb[:, b, half * 512:(half + 1) * 512]
                S.op("dve", lambda e: e.scalar_tensor_tensor(out=xs_, in0=pb[bank][:, :], scalar=0.5, in1=xs_, op0=ALU.mult, op1=ALU.add),
                     reads=[pbB[bank], xB[b]], writes=[xB[b]])

    def mixer(l):
        norm_T(lambda cc: colA(l, "mix", cc))
        specs = mix_specs(l)
        si = 0
        ws, wsB = w_next(specs[si]); si += 1
        for b in range(NB):
            for k in range(8):
                S.op("pe", lambda e: e.matmul(pb[0][:, 0:32], lhsT=hT[:, k, b * 128:(b + 1) * 128], rhs=ws[:, k, 0:32],
                                              start=(k == 0), stop=(k == 7)), reads=[wsB, hTB], writes=[pbB[0]], acc=(k > 0))
            S.op("dve", lambda e: e.tensor_copy(out=sm[:, b, :], in_=pb[0][:, 0:32]), reads=[pbB[0]], writes=[smB])
        S.op("act", lambda e: e.activation(out=beta[:], in_=sm[:, :, 0:8], func=AF.Tanh, scale=0.5), reads=[smB], writes=[smB])
        S.op("dve", lambda e: e.tensor_scalar(out=beta[:], in0=beta[:], scalar1=0.5, scalar2=0.5, op0=ALU.mult, op1=ALU.add),
             reads=[smB], writes=[smB])
        S.op("dve", lambda e: e.tensor_tensor(out=sm[:], in0=sm[:], in1=bcm(spb[l][:, :], NB), op=ALU.add), reads=[smB, parB], writes=[smB])
        S.op("dve", lambda e: e.tensor_scalar(out=sm[:, :, 0:8], in0=sm[:, :, 0:8], scalar1=-1.0, scalar2=None, op0=ALU.mult),
             reads=[smB], writes=[smB])
        S.op("act", lambda e: e.activation(out=spv[:], in_=sm[:], func=AF.Exp), reads=[smB], writes=[smB])
        S.op("act", lambda e: e.activation(out=spv[:], in_=spv[:], func=AF.Ln, bias=1.0, scale=1.0), reads=[smB], writes=[smB])
        S.op("dve", lambda e: e.tensor_tensor(out=gcat[:], in0=spv[:, :, 8:32], in1=bcm(negA[l][:, :], NB), op=ALU.mult),
             reads=[smB, parB], writes=[smB])
        for b in range(NB):
            S.op("pe", lambda e: e.matmul(pb[0][:, 0:24], lhsT=triu[:], rhs=gcat[:, b, :], start=True, stop=True),
                 reads=[smB, cB], writes=[pbB[0]])
            S.op("pe", lambda e: e.matmul(pb[0][:, 32:56], lhsT=ones[:], rhs=gcat[:, b, :], start=True, stop=True),
                 reads=[smB, cB], writes=[pbB[0]], acc=True)
            S.op("dve", lambda e: e.tensor_copy(out=gct[:, b, 0:56], in_=pb[0][:, 0:56]), reads=[pbB[0]], writes=[smB])
        S.op("act", lambda e: e.activation(out=egc[:], in_=gct[:, :, 0:24], func=AF.Exp), reads=[smB], writes=[smB])
        S.op("act", lambda e: e.activation(out=etot[:], in_=gct[:, :, 32:56], func=AF.Exp), reads=[smB], writes=[smB])
        S.op("dve", lambda e: e.tensor_tensor(out=edl[:], in0=gct[:, :, 32:56], in1=gct[:, :, 0:24], op=ALU.subtract), reads=[smB], writes=[smB])
        S.op("act", lambda e: e.activation(out=edl[:], in_=edl[:], func=AF.Exp), reads=[smB], writes=[smB])
        S.op("dve", lambda e: e.tensor_scalar(out=ngc[:], in0=gct[:, :, 0:24], scalar1=-1.0, scalar2=None, op0=ALU.mult), reads=[smB], writes=[smB])
        S.op("dve", lambda e: e.tensor_tensor(out=biasA[:], in0=gct[:, :, 0:8], in1=spv[:, :, 0:8], op=ALU.subtract), reads=[smB], writes=[smB])
        S.op("dve", lambda e: e.tensor_tensor(out=bg[:], in0=beta[:], in1=egc[:, :, 0:8], op=ALU.mult), reads=[smB], writes=[smB])
        S.op("dve", lambda e: e.tensor_tensor(out=dte[:], in0=spv[:, :, 16:32], in1=edl[:, :, 8:24], op=ALU.mult), reads=[smB], writes=[smB])

        if MXSTOP < 2:
            [w_next(s_) for s_ in specs[si:]]
            return
        slab_h = {}

        def stA(fc):
            grp, c = divmod(fc, 4)
            i2 = fc % 2
            if c == 0:
                slab_h[grp] = w_next(specs[si + grp])
            ws, wsB = slab_h[grp]
            bank = i2
            for k in range(8):
                S.op("pe", lambda e: e.matmul(pb[bank][:, 0:T], lhsT=ws[:, k, c * 128:(c + 1) * 128], rhs=hT[:, k, :],
                                              start=(k == 0), stop=(k == 7)), reads=[wsB, hTB], writes=[pbB[bank]], acc=(k > 0))
            xb, xbB = xbuf[i2], xbufB[i2]
            S.op("act", lambda e: e.copy(out=xb[:, 3:3 + T], in_=pb[bank][:, 0:T]), reads=[pbB[bank]], writes=[xbB])
            S.op("pool", lambda e: e.tensor_copy(out=xb[:, 0:3], in_=tails[l][:, fc, :]), reads=[tailB[l][fc]], writes=[xbB])
            S.op("pool", lambda e: e.tensor_copy(out=tails[l][:, fc, :], in_=xb[:, T:T + 3]), reads=[xbB], writes=[tailB[l][fc]])

        def stB(fc):
            i2 = fc % 2
            xb, xbB = xbuf[i2], xbufB[i2]
            if fc < 24:
                wc = lambda kk: colsA[l][:, kk * 24 + fc:kk * 24 + fc + 1]
            else:
                wc = lambda kk: colsB[l][:, kk * 12 + (fc - 24):kk * 12 + (fc - 24) + 1]
            ca, caB = cacc[i2], caccB[i2]
            S.op("dve", lambda e: e.tensor_scalar(out=ca[:], in0=xb[:, 0:T], scalar1=wc(0), scalar2=None, op0=ALU.mult),
                 reads=[xbB, parB], writes=[caB])
            for kk in (1, 2, 3):
                S.op("dve", lambda e: e.scalar_tensor_tensor(out=ca[:], in0=xb[:, kk:kk + T], scalar=wc(kk), in1=ca[:], op0=ALU.mult, op1=ALU.add),
                     reads=[xbB, parB, caB], writes=[caB])
            cs_, csB_ = csil[i2], csilB[i2]
            if fc < 24:
                S.op("act", lambda e: e.activation(out=cs_[:], in_=ca[:], func=AF.Silu), reads=[caB], writes=[csB_])
            else:
                S.op("act", lambda e: e.activation(out=cs_[:], in_=ca[:], func=AF.Silu, bias=colsB[l][:, 48 + fc - 24:48 + fc - 24 + 1]),
                     reads=[caB, parB], writes=[csB_])

        def stC(fc):
            i2 = fc % 2
            cs_, csB_ = csil[i2], csilB[i2]
            if fc >= 32:
                dst = bT if fc < 34 else cT
                S.op("dve", lambda e: e.tensor_copy(out=dst[:, fc % 2, :], in_=cs_[:]), reads=[csB_], writes=[bcTB])
            if fc < 34:
                tb = 2 + i2
                for b in range(NB):
                    transpose_f32(cs_[:, b * 128:(b + 1) * 128], csB_, tb, b * 128)
                for b in range(NB):
                    src = pb[tb][:, b * 128:(b + 1) * 128]
                    if fc < 24:
                        S.op("act", lambda e: e.copy(out=qkv_tm[:, b, fc * 128:(fc + 1) * 128], in_=src), reads=[pbB[tb]], writes=[qkvB[b]])
                    elif fc < 32:
                        S.op("act", lambda e: e.copy(out=xs_tm[:, b, (fc - 24) * 128:(fc - 23) * 128], in_=src), reads=[pbB[tb]], writes=[xsB[b]])
                    else:
                        S.op("act", lambda e: e.copy(out=b_tm[:, b, (fc - 32) * 128:(fc - 31) * 128], in_=src), reads=[pbB[tb]], writes=[btmB[b]])

        for i in range(36 + 2):
            if i < 36:
                stA(i)
            if 0 <= i - 1 < 36:
                stB(i - 1)
            if 0 <= i - 2 < 36:
                stC(i - 2)
        si += 9

        if MXSTOP < 3:
            [w_next(s_) for s_ in specs[si:]]
            return
        for zi in range(4):
            ws, wsB = w_next(specs[si]); si += 1
            for b in range(NB):
                bank = 4 + b % 2
                for k in range(8):
                    S.op("pe", lambda e: e.matmul(pb[bank][:, :], lhsT=hT[:, k, b * 128:(b + 1) * 128], rhs=ws[:, k, :],
                                                  start=(k == 0), stop=(k == 7)), reads=[wsB, hTB], writes=[pbB[bank]], acc=(k > 0))
                S.op("act", lambda e: e.activation(out=zh[:, b, zi * 512:(zi + 1) * 512], in_=pb[bank][:, :], func=AF.Silu),
                     reads=[pbB[bank]], writes=[zhB])

        if MXSTOP < 4:
            [w_next(s_) for s_ in specs[si:]]
            return
        for b in range(NB):
            if MXSTOP >= 4.0:
                gdn_block(l, b)
            if MXSTOP >= 4.2:
                ssd_block(l, b)
        if MXSTOP < 5:
            [w_next(s_) for s_ in specs[si:]]
            return

        wo = []
        for half in range(2):
            o0 = w_next(specs[si]); o1 = w_next(specs[si + 1]); si += 2
            for b in range(NB):
                bank = 4 + b % 2
                for c in range(16):
                    ws, wsB = o0 if c < 8 else o1
                    S.op("pe", lambda e: e.matmul(pb[bank][:, :], lhsT=mixT[:, c, b * 128:(b + 1) * 128], rhs=ws[:, c % 8, :],
                                                  start=(c == 0), stop=(c == 15)), reads=[wsB, mixTB[b]], writes=[pbB[bank]], acc=(c > 0))
                xs_ = x_sb[:, b, half * 512:(half + 1) * 512]
                S.op("dve", lambda e: e.tensor_tensor(out=xs_, in0=pb[bank][:, :], in1=xs_, op=ALU.add), reads=[pbB[bank], xB[b]], writes=[xB[b]])

    def gdn_block(l, b):
        q = qkv_tm[:, b, 0:1024]; k = qkv_tm[:, b, 1024:2048]; v = qkv_tm[:, b, 2048:3072]
        S.op("pool", lambda e: e.tensor_tensor(out=fscr[0][:], in0=q, in1=q, op=ALU.mult), reads=[qkvB[b]], writes=[fB[0]])
        S.op("dve", lambda e: e.tensor_reduce(out=blk_small[:, 0:8], in_=f3(fscr[0][:], 8), axis=AX.X, op=ALU.add), reads=[fB[0]], writes=[bsB])
        S.op("pool", lambda e: e.tensor_tensor(out=fscr[1][:], in0=k, in1=k, op=ALU.mult), reads=[qkvB[b]], writes=[fB[1]])
        S.op("dve", lambda e: e.tensor_reduce(out=blk_small[:, 8:16], in_=f3(fscr[1][:], 8), axis=AX.X, op=ALU.add), reads=[fB[1]], writes=[bsB])
        rsqrt_small(blk_small[:, 0:16], 16, 1.0, [bsB], [bsB])
        S.op("dve", lambda e: e.tensor_scalar(out=blk_small[:, 0:8], in0=blk_small[:, 0:8], scalar1=128.0 ** -0.5, scalar2=None, op0=ALU.mult),
             reads=[bsB], writes=[bsB])
        S.op("dve", lambda e: e.tensor_tensor(out=blk_small[:, 16:24], in0=blk_small[:, 8:16], in1=edl[:, b, 0:8], op=ALU.mult),
             reads=[bsB, smB], writes=[bsB])
        qn, qnB = fscr[0], fB[0]
        kn, knB = fscr[1], fB[1]
        S.op("dve", lambda e: e.tensor_tensor(out=f3(qn[:], 8), in0=f3(q, 8), in1=bcl(blk_small[:, 0:8], 128), op=ALU.mult),
             reads=[qkvB[b], bsB], writes=[qnB])
        S.op("pool", lambda e: e.tensor_tensor(out=f3(kn[:], 8), in0=f3(k, 8), in1=bcl(blk_small[:, 8:16], 128), op=ALU.mult),
             reads=[qkvB[b], bsB], writes=[knB])
        kdec, kdecB = hscr[0], hB[0]
        S.op("dve", lambda e: e.tensor_tensor(out=f3(kdec[:], 8), in0=f3(k, 8), in1=bcl(blk_small[:, 16:24], 128), op=ALU.mult),
             reads=[qkvB[b], bsB], writes=[kdecB])
        vb, vbB = fscr[2], fB[2]
        S.op("pool", lambda e: e.tensor_tensor(out=f3(vb[:], 8), in0=f3(v, 8), in1=bcl(beta[:, b, :], 128), op=ALU.mult),
             reads=[qkvB[b], smB], writes=[vbB])
        if MXSTOP < 4.01:
            return
        qT, qTB = hscr[1], hB[1]
        kT, kTB = hscr[2], hB[2]
        for (src, srcB, dst, dstB, banks) in ((qn, qnB, qT, qTB, (0, 1)), (kn, knB, kT, kTB, (2, 3))):
            for hh in range(2):
                for c in range(4):
                    transpose_f32(src[:, (hh * 4 + c) * 128:(hh * 4 + c + 1) * 128], srcB, banks[hh], c * 128)
                S.op("act", lambda e: e.copy(out=dst[:, hh * 512:(hh + 1) * 512], in_=pb[banks[hh]][:, :]), reads=[pbB[banks[hh]]], writes=[dstB])
        if MXSTOP < 4.02:
            return
        tg, tgB = fscr[3], fB[3]
        S.op("dve", lambda e: e.tensor_tensor(out=f3(tg[:], 8), in0=bcm(triu[:], 8), in1=bcl(gcat[:, b, 0:8], 128), op=ALU.mult),
             reads=[cB, smB], writes=[tgB])
        if MXSTOP < 4.021:
            return
        dS, dSB = fscr[4], fB[4]
        dT, dTB = fscr[5], fB[5]
        for hh in range(2):
            for (mask, bank) in ((ustrR, 4 + hh), (loR, 6 + hh)):
                S.op("pe", lambda e: e.matmul(pb[bank][:, :], lhsT=ones[:], rhs=tg[:, hh * 512:(hh + 1) * 512], start=True, stop=False),
                     reads=[tgB, cB], writes=[pbB[bank]])
                S.op("pe", lambda e: e.matmul(pb[bank][:, :], lhsT=ident[:], rhs=mask[:].rearrange("p r j -> p (r j)"), start=False, stop=True),
                     reads=[cB], writes=[pbB[bank]], acc=True)
            if MXSTOP < 4.022:
                continue
            for c in range(4):
                h = hh * 4 + c
                S.op("act", lambda e: e.activation(out=dS[:, h * 128:(h + 1) * 128], in_=pb[4 + hh][:, c * 128:(c + 1) * 128], func=AF.Exp,
                                                   scale=-1.0, bias=biasA[:, b, h:h + 1]), reads=[pbB[4 + hh], smB], writes=[dSB[hh]])
                S.op("act", lambda e: e.activation(out=dT[:, h * 128:(h + 1) * 128], in_=pb[6 + hh][:, c * 128:(c + 1) * 128], func=AF.Exp,
                                                   scale=1.0, bias=ngc[:, b, h:h + 1]), reads=[pbB[6 + hh], smB], writes=[dTB[hh]])
        if MXSTOP < 4.03:
            return
        for hh in range(2):
            for c in range(4):
                h = hh * 4 + c
                S.op("pe", lambda e: e.matmul(pb[0 + hh][:, c * 128:(c + 1) * 128], lhsT=kT[:, h * 128:(h + 1) * 128], rhs=kT[:, h * 128:(h + 1) * 128],
                                              start=True, stop=True), reads=[kTB], writes=[pbB[0 + hh]], acc=(c > 0))
            for c in range(4):
                h = hh * 4 + c
                S.op("pe", lambda e: e.matmul(pb[2 + hh][:, c * 128:(c + 1) * 128], lhsT=kT[:, h * 128:(h + 1) * 128], rhs=qT[:, h * 128:(h + 1) * 128],
                                              start=True, stop=True), reads=[kTB, qTB], writes=[pbB[2 + hh]], acc=(c > 0))
        qkm, qkmB = hscr[3], hB[3]
        for hh in range(2):
            sl = slice(hh * 512, (hh + 1) * 512)
            S.op("dve", lambda e: e.scalar_tensor_tensor(out=dS[:, sl], in0=pb[0 + hh][:, :], scalar=-1.0, in1=dS[:, sl], op0=ALU.mult, op1=ALU.mult),
                 reads=[pbB[0 + hh], dSB[hh]], writes=[dSB[hh]])
            S.op("dve", lambda e: e.tensor_tensor(out=qkm[:, sl], in0=pb[2 + hh][:, :], in1=dT[:, sl], op=ALU.mult),
                 reads=[pbB[2 + hh], dTB[hh]], writes=[qkmB[hh]])
        if MXSTOP < 4.04:
            return
        Pb, PbB = dS, dSB
        Ptb, PtbB = pt32, pt32B
        Ntb, NtbB = dT, dTB
        Nt, NtB = dT, dTB
        if MXSTOP < 4.0401:
            return
        for hh in range(2):
            for c in range(4):
                transpose_f32(dS[:, (hh * 4 + c) * 128:(hh * 4 + c + 1) * 128], dSB[hh], 0 + hh, c * 128)
            sl = slice(hh * 512, (hh + 1) * 512)
            if MXSTOP < 4.0402:
                continue
            S.op("act", lambda e: e.copy(out=Ptb[:, sl], in_=pb[0 + hh][:, :]), reads=[pbB[0 + hh]], writes=[PtbB[hh]])
            if MXSTOP < 4.0403:
                continue
            S.op("dve", lambda e: e.tensor_copy(out=Nt[:, sl], in_=pb[0 + hh][:, :]), reads=[pbB[0 + hh], qkmB[hh]], writes=[NtB[hh]])
        if MXSTOP < 4.0404:
            return
        if MXSTOP < 4.041:
            return
        for lev in range(int(os.environ.get("MK_NLEV", "6"))):
            for hh in range(2):
                for c in range(4):
                    h = hh * 4 + c
                    hs = slice(h * 128, (h + 1) * 128)
                    S.op("pe", lambda e: e.matmul(pb[0 + hh][:, c * 128:(c + 1) * 128], lhsT=Ptb[:, hs], rhs=Pb[:, hs], start=True, stop=True),
                         reads=[PtbB[hh], PbB[hh]], writes=[pbB[0 + hh]], acc=(c > 0))
                for c in range(4):
                    h = hh * 4 + c
                    hs = slice(h * 128, (h + 1) * 128)
                    S.op("pe", lambda e: e.matmul(pb[2 + hh][:, c * 128:(c + 1) * 128], lhsT=Pb[:, hs], rhs=Ptb[:, hs], start=True, stop=True),
                         reads=[PtbB[hh], PbB[hh]], writes=[pbB[2 + hh]], acc=(c > 0))
            for hh in range(2):
                sl = slice(hh * 512, (hh + 1) * 512)
                S.op("act", lambda e: e.copy(out=Pb[:, sl], in_=pb[0 + hh][:, :]), reads=[pbB[0 + hh]], writes=[PbB[hh]])
                S.op("act", lambda e: e.copy(out=Ptb[:, sl], in_=pb[2 + hh][:, :]), reads=[pbB[2 + hh]], writes=[PtbB[hh]])
            if MXSTOP < 4.042:
                continue
            for hh in range(2):
                for c in range(4):
                    h = hh * 4 + c
                    hs = slice(h * 128, (h + 1) * 128)
                    S.op("pe", lambda e: e.matmul(pb[4 + hh][:, c * 128:(c + 1) * 128], lhsT=Pb[:, hs], rhs=Ntb[:, hs], start=True, stop=True),
                         reads=[NtbB[hh], PbB[hh]], writes=[pbB[4 + hh]], acc=(c > 0))
            for hh in range(2):
                sl = slice(hh * 512, (hh + 1) * 512)
                S.op("dve", lambda e: e.tensor_tensor(out=Nt[:, sl], in0=pb[2 + hh][:, :], in1=Nt[:, sl], op=ALU.add), reads=[pbB[2 + hh], NtB[hh]], writes=[NtB[hh]])
                S.op("dve", lambda e: e.tensor_tensor(out=Nt[:, sl], in0=pb[4 + hh][:, :], in1=Nt[:, sl], op=ALU.add), reads=[pbB[4 + hh], NtB[hh]], writes=[NtB[hh]])
        TtB, TtBB = Nt, NtB
        S.op("dve", lambda e: e.tensor_tensor(out=f3(TtB[:], 8), in0=f3(Nt[:], 8), in1=bcm(ident[:], 8), op=ALU.add), reads=[NtB, cB], writes=[TtBB])
        if MXSTOP < 4.05:
            return
        S.op("act", lambda e: e.copy(out=Sgb[:].rearrange("p h e -> p (h e)"), in_=Sg[l][:].rearrange("p h e -> p (h e)")), reads=[SgB[l]], writes=[SgbB])
        for hh in range(2):
            for c in range(4):
                h = hh * 4 + c
                hs = slice(h * 128, (h + 1) * 128)
                S.op("pe", lambda e: e.matmul(pb[0 + hh][:, c * 128:(c + 1) * 128], lhsT=kT[:, hs], rhs=Sgb[:, h, :], start=True, stop=True),
                     reads=[kTB, SgbB], writes=[pbB[0 + hh]], acc=(c > 0))
            for c in range(4):
                h = hh * 4 + c
                hs = slice(h * 128, (h + 1) * 128)
                S.op("pe", lambda e: e.matmul(pb[2 + hh][:, c * 128:(c + 1) * 128], lhsT=qT[:, hs], rhs=Sgb[:, h, :], start=True, stop=True),
                     reads=[qTB, SgbB], writes=[pbB[2 + hh]], acc=(c > 0))
        r2, r2B = vb, vbB
        t_, tB_ = fscr[3], fB[3]
        for hh in range(2):
            sl = slice(hh * 512, (hh + 1) * 512)
            S.op("dve", lambda e: e.tensor_tensor(out=f3(t_[:, sl], 4), in0=f3(pb[0 + hh][:, :], 4), in1=bcl(bg[:, b, hh * 4:(hh + 1) * 4], 128), op=ALU.mult),
                 reads=[pbB[0 + hh], smB], writes=[tB_])
        S.op("dve", lambda e: e.tensor_tensor(out=r2[:], in0=vb[:], in1=t_[:], op=ALU.subtract), reads=[vbB, tB_], writes=[r2B])
        for hh in range(2):
            for c in range(4):
                h = hh * 4 + c
                hs = slice(h * 128, (h + 1) * 128)
                S.op("pe", lambda e: e.matmul(pb[4 + hh][:, c * 128:(c + 1) * 128], lhsT=TtB[:, hs], rhs=r2[:, hs], start=True, stop=True),
                     reads=[TtBB, r2B], writes=[pbB[4 + hh]], acc=(c > 0))
        vn, vnB = hscr[4], hB[4]
        for hh in range(2):
            sl = slice(hh * 512, (hh + 1) * 512)
            S.op("act", lambda e: e.copy(out=vn[:, sl], in_=pb[4 + hh][:, :]), reads=[pbB[4 + hh]], writes=[vnB])
        o_, oB_ = fscr[4], fB[4]
        for hh in range(2):
            sl = slice(hh * 512, (hh + 1) * 512)
            S.op("dve", lambda e: e.tensor_tensor(out=f3(o_[:, sl], 4), in0=f3(pb[2 + hh][:, :], 4), in1=bcl(egc[:, b, hh * 4:(hh + 1) * 4], 128), op=ALU.mult),
                 reads=[pbB[2 + hh], smB, PbB], writes=[oB_])
        for hh in range(2):
            for c in range(4):
                h = hh * 4 + c
                hs = slice(h * 128, (h + 1) * 128)
                S.op("pe", lambda e: e.matmul(pb[6 + hh][:, c * 128:(c + 1) * 128], lhsT=qkm[:, hs], rhs=vn[:, hs], start=True, stop=True),
                     reads=[qkmB, vnB], writes=[pbB[6 + hh]], acc=(c > 0))
            for c in range(4):
                h = hh * 4 + c
                hs = slice(h * 128, (h + 1) * 128)
                S.op("pe", lambda e: e.matmul(pb[0 + hh][:, c * 128:(c + 1) * 128], lhsT=kdec[:, hs], rhs=vn[:, hs], start=True, stop=True),
                     reads=[kdecB, vnB], writes=[pbB[0 + hh]], acc=(c > 0))
        Sf = Sg[l][:].rearrange("p h e -> p (h e)")
        for hh in range(2):
            sl = slice(hh * 512, (hh + 1) * 512)
            S.op("dve", lambda e: e.tensor_tensor(out=o_[:, sl], in0=pb[6 + hh][:, :], in1=o_[:, sl], op=ALU.add), reads=[pbB[6 + hh], oB_], writes=[oB_])
            S.op("pool", lambda e: e.tensor_tensor(out=f3(Sf[:, sl], 4), in0=f3(Sf[:, sl], 4), in1=bcl(etot[:, b, hh * 4:(hh + 1) * 4], 128), op=ALU.mult),
                 reads=[SgB[l], smB, SgbB], writes=[SgB[l]])
            S.op("dve", lambda e: e.tensor_tensor(out=Sf[:, sl], in0=pb[0 + hh][:, :], in1=Sf[:, sl], op=ALU.add), reads=[pbB[0 + hh], SgB[l]], writes=[SgB[l]])
        if DUMP == "gdn_o":
            S.dma("sp", out_d[b * 128:(b + 1) * 128, :], o_[:], reads=[oB_])
        if DUMP == "gdn_S":
            S.dma("sp", out_d[b * 128:(b + 1) * 128, :], Sf, reads=[SgB[l]])
        if MXSTOP < 4.06:
            return
        sq, sqB = fscr[3], fB[3]
        S.op("pool", lambda e: e.tensor_tensor(out=sq[:], in0=o_[:], in1=o_[:], op=ALU.mult), reads=[oB_], writes=[sqB])
        S.op("dve", lambda e: e.tensor_reduce(out=blk_small[:, 32:40], in_=f3(sq[:], 8), axis=AX.X, op=ALU.add), reads=[sqB], writes=[bsB])
        rsqrt_small(blk_small[:, 32:40], 8, 1.0 / 128, [bsB], [bsB])
        S.op("dve", lambda e: e.tensor_tensor(out=f3(o_[:], 8), in0=f3(o_[:], 8), in1=bcl(blk_small[:, 32:40], 128), op=ALU.mult), reads=[oB_, bsB], writes=[oB_])
        S.op("dve", lambda e: e.tensor_tensor(out=o_[:], in0=o_[:], in1=zh[:, b, 0:1024], op=ALU.mult), reads=[oB_, zhB], writes=[oB_])
        for hh in range(2):
            bank = 2 + hh
            for c in range(4):
                transpose_f32(o_[:, (hh * 4 + c) * 128:(hh * 4 + c + 1) * 128], oB_, bank, c * 128)
            for c in range(4):
                S.op("act", lambda e: e.activation(out=mixT[:, hh * 4 + c, b * 128:(b + 1) * 128], in_=pb[bank][:, c * 128:(c + 1) * 128],
                                                   func=AF.Identity, scale=colA(l, "gnorm")), reads=[pbB[bank], parB], writes=[mixTB[b]])

    def ssd_block(l, b):
        xs = xs_tm[:, b, :]
        dt_ = spv[:, b, 16:32]
        xdt, xdtB = hscr[0], hB[0]
        xe, xeB = hscr[1], hB[1]
        S.op("dve", lambda e: e.tensor_tensor(out=f3(xdt[:], 16), in0=f3(xs, 16), in1=bcl(dt_, 64), op=ALU.mult), reads=[xsB[b], smB], writes=[xdtB])
        S.op("pool", lambda e: e.tensor_tensor(out=f3(xe[:], 16), in0=f3(xs, 16), in1=bcl(dte[:, b, :], 64), op=ALU.mult), reads=[xsB[b], smB], writes=[xeB])
        S.op("act", lambda e: e.copy(out=Ssb[:].rearrange("p h e -> p (h e)"), in_=Ss[l][:].rearrange("p h e -> p (h e)")), reads=[SsB[l]], writes=[SsbB])
        tg, tgB = fscr[0], fB[0]
        seg, segB = fscr[1], fB[1]
        MT = (hscr[2], hscr[3]); MTB = (hB[2], hB[3])
        y, yB = fscr[2], fB[2]
        bs_ = slice(b * 128, (b + 1) * 128)
        for g in range(2):
            S.op("pe", lambda e: e.matmul(pb[4][:, g * 128:(g + 1) * 128], lhsT=bT[:, g, bs_], rhs=cT[:, g, bs_], start=True, stop=True),
                 reads=[bcTB], writes=[pbB[4]], acc=(g > 0))
        for g in range(2):
            S.op("dve", lambda e: e.tensor_tensor(out=f3(tg[:], 8), in0=bcm(triu[:], 8), in1=bcl(gcat[:, b, 8 + g * 8:16 + g * 8], 128), op=ALU.mult),
                 reads=[cB, smB], writes=[tgB])
            for hh in range(2):
                bank = 6 + hh
                S.op("pe", lambda e: e.matmul(pb[bank][:, :], lhsT=ones[:], rhs=tg[:, hh * 512:(hh + 1) * 512], start=True, stop=False),
                     reads=[tgB, cB], writes=[pbB[bank]])
                S.op("pe", lambda e: e.matmul(pb[bank][:, :], lhsT=ident[:], rhs=loR[:].rearrange("p r j -> p (r j)"), start=False, stop=True),
                     reads=[cB], writes=[pbB[bank]], acc=True)
                for c in range(4):
                    j = hh * 4 + c
                    h = g * 8 + j
                    S.op("act", lambda e: e.activation(out=seg[:, j * 128:(j + 1) * 128], in_=pb[bank][:, c * 128:(c + 1) * 128], func=AF.Exp,
                                                       scale=1.0, bias=ngc[:, b, 8 + h:9 + h]), reads=[pbB[bank], smB], writes=[segB])
            S.op("dve", lambda e: e.tensor_tensor(out=f3(MT[g][:], 8), in0=f3(seg[:], 8), in1=bcm(pb[4][:, g * 128:(g + 1) * 128], 8), op=ALU.mult),
                 reads=[segB, pbB[4]], writes=[MTB[g]])
        for g in range(2):
            for j in range(8):
                h = g * 8 + j
                S.op("pe", lambda e: e.matmul(pb[0 + g][:, j * 64:(j + 1) * 64], lhsT=MT[g][:, j * 128:(j + 1) * 128], rhs=xdt[:, h * 64:(h + 1) * 64],
                                              start=True, stop=True), reads=[MTB[g], xdtB], writes=[pbB[0 + g]], acc=(j > 0))
            S.op("pe", lambda e: e.matmul(pb[2 + g][:, :], lhsT=cT[:, g, bs_], rhs=Ssb[:, g * 8:(g + 1) * 8, :].rearrange("p h e -> p (h e)"),
                                          start=True, stop=True), reads=[bcTB, SsbB], writes=[pbB[2 + g]])
        for g in range(2):
            sl = slice(g * 512, (g + 1) * 512)
            S.op("dve", lambda e: e.tensor_tensor(out=f3(y[:, sl], 8), in0=f3(pb[2 + g][:, :], 8), in1=bcl(egc[:, b, 8 + g * 8:16 + g * 8], 64), op=ALU.mult),
                 reads=[pbB[2 + g], smB], writes=[yB])
            S.op("dve", lambda e: e.tensor_tensor(out=y[:, sl], in0=pb[0 + g][:, :], in1=y[:, sl], op=ALU.add), reads=[pbB[0 + g], yB], writes=[yB])
        Sf = Ss[l][:].rearrange("p h e -> p (h e)")
        for g in range(2):
            sl = slice(g * 512, (g + 1) * 512)
            S.op("pe", lambda e: e.matmul(pb[6 + g][:, :], lhsT=b_tm[:, b, g * 128:(g + 1) * 128], rhs=xe[:, sl], start=True, stop=True),
                 reads=[btmB[b], xeB], writes=[pbB[6 + g]])
            S.op("pool", lambda e: e.tensor_tensor(out=f3(Sf[:, sl], 8), in0=f3(Sf[:, sl], 8), in1=bcl(etot[:, b, 8 + g * 8:16 + g * 8], 64), op=ALU.mult),
                 reads=[SsB[l], smB, SsbB], writes=[SsB[l]])
            S.op("dve", lambda e: e.tensor_tensor(out=Sf[:, sl], in0=pb[6 + g][:, :], in1=Sf[:, sl], op=ALU.add), reads=[pbB[6 + g], SsB[l]], writes=[SsB[l]])
        t_, tB_ = fscr[3], fB[3]
        S.op("pool", lambda e: e.tensor_tensor(out=f3(t_[:], 16), in0=f3(xs, 16), in1=bcl(dsk[l][:, :], 64), op=ALU.mult), reads=[xsB[b], parB], writes=[tB_])
        S.op("dve", lambda e: e.tensor_tensor(out=y[:], in0=y[:], in1=t_[:], op=ALU.add), reads=[yB, tB_], writes=[yB])
        S.op("dve", lambda e: e.tensor_tensor(out=y[:], in0=y[:], in1=zh[:, b, 1024:2048], op=ALU.mult), reads=[yB, zhB], writes=[yB])
        for g in range(2):
            S.op("act", lambda e: e.activation(out=t_[:, g * 512:(g + 1) * 512], in_=y[:, g * 512:(g + 1) * 512], func=AF.Square,
                                               accum_out=blk_small[:, 40 + g:41 + g]), reads=[yB], writes=[tB_, bsB])
        rsqrt_small(blk_small[:, 40:42], 2, 1.0 / 512, [bsB], [bsB])
        S.op("dve", lambda e: e.tensor_tensor(out=f3(y[:], 2), in0=f3(y[:], 2), in1=bcl(blk_small[:, 40:42], 512), op=ALU.mult), reads=[yB, bsB], writes=[yB])
        for hh in range(2):
            bank = 4 + hh
            for c in range(4):
                transpose_f32(y[:, (hh * 4 + c) * 128:(hh * 4 + c + 1) * 128], yB, bank, c * 128)
            for c in range(4):
                cc = hh * 4 + c
                S.op("act", lambda e: e.activation(out=mixT[:, 8 + cc, b * 128:(b + 1) * 128], in_=pb[bank][:, c * 128:(c + 1) * 128],
                                                   func=AF.Identity, scale=colsB[l][:, 60 + cc:61 + cc]), reads=[pbB[bank], parB], writes=[mixTB[b]])

    for t in range(n_tiles):
        for b in range(NB):
            r0 = t * T + b * 128
            S.dma("sp", x_sb[:, b, :], x_d[r0:r0 + 128, :], writes=[xB[b]])
        for l in range(depth):
            if "F1" in DBG:
                ffn(l, 1)
            else:
                [w_next(s_) for s_ in ffn_specs(l, 1)]
            if "MX" in DBG:
                mixer(l)
            else:
                [w_next(s_) for s_ in mix_specs(l)]
            if "F2" in DBG:
                ffn(l, 2)
            else:
                [w_next(s_) for s_ in ffn_specs(l, 2)]
        for b in range(NB):
            S.op("act", lambda e: e.activation(out=fscr[0][:], in_=x_sb[:, b, :], func=AF.Square, accum_out=stat[:, b:b + 1]),
                 reads=[xB[b]], writes=[fB[0], statB])
        rsqrt_small(stat[:, 0:NB], NB, 1.0 / D, [statB], [statB])
        for b in range(NB):
            ob, obB = fscr[2 + b % 4], fB[2 + b % 4]
            S.op("dve", lambda e: e.scalar_tensor_tensor(out=ob[:], in0=x_sb[:, b, :], scalar=stat[:, b:b + 1], in1=fin_bc[:],
                                                         op0=ALU.mult, op1=ALU.mult), reads=[xB[b], statB, parB], writes=[obB])
            r0 = t * T + b * 128
            if not DUMP:
                S.dma("sp", out_d[r0:r0 + 128, :], ob[:], reads=[obB])
    S._need("sp", {"d%d" % i: S.cnt["d%d" % i] for i in range(S.ndma) if S.cnt["d%d" % i] > 0})
    print("program: %d instructions, %d waits" % (S.nins, S.nwait), S.tot)


_PROG = {}


def _get_prog(n_tiles, depth=DEPTH):
    key = (n_tiles, depth)
    if key not in _PROG:
        _PROG[key] = build_program(n_tiles, depth)
    return _PROG[key]


W_KEYS = ["ffn1_w_gate", "ffn1_w_up", "ffn1_w_down", "w_in", "w_out", "ffn2_w_gate", "ffn2_w_up", "ffn2_w_down"]
P_KEYS = ["ffn1_norm", "mix_norm", "ffn2_norm", "final_norm", "gdn_conv_w", "gdn_a_log", "gdn_dt_bias", "gdn_norm_w",
          "ssm_conv_w", "ssm_conv_b", "ssm_a_log", "ssm_dt_bias", "ssm_d", "ssm_norm_w"]


def run(inputs, n_cores=8, n_tiles=SEQ // T, depth=DEPTH):
    nc = _get_prog(n_tiles, depth)
    shared = {k: np.ascontiguousarray(np.asarray(inputs[k], dtype=np.float32)) for k in W_KEYS + P_KEYS}
    x = np.asarray(inputs["x"], dtype=np.float32)
    in_maps = []
    for c in range(n_cores):
        m = dict(shared)
        m["x"] = np.ascontiguousarray(x[c, : n_tiles * T, :])
        in_maps.append(m)
    res = run_bass_kernel_spmd(nc, in_maps, core_ids=list(range(n_cores)))
    return np.stack([np.asarray(r["out"]) for r in res.results], axis=0)


def kernel(**inputs):
    return run(inputs).astype(np.float32)
```

```python
import contextlib
import os
import numpy as np
import concourse.bass as bass
import concourse.mybir as mybir
from concourse.bass_utils import run_bass_kernel_spmd

F32 = mybir.dt.float32
BF16 = mybir.dt.bfloat16
AF = mybir.ActivationFunctionType
ALU = mybir.AluOpType
AX = mybir.AxisListType

D = 1024
DFF = 2816
SEQ = 8192
DEPTH = 2
INW = 6688
NB = 2
T = 128 * NB
EPS = 1e-6
BIG = float(os.environ.get("MK_BIG", "2000.0"))
KSLOT = 8
NSLOT = 6
LOOKAHEAD = 3
SEM_LIM = 16000
DBG = os.environ.get("MK_DBG", "F1,MX,F2").split(",")
MXSTOP = float(os.environ.get("MK_MXSTOP", "99"))
DUMP = os.environ.get("MK_DUMP", "")

O_QKV, O_GZ, O_GB, O_GA, O_SZ, O_XBC, O_DT = 0, 3072, 4096, 4104, 4112, 5136, 6672


class Buf:
    __slots__ = ("w", "r", "name", "excl")

    def __init__(self, name="", excl=False):
        self.w = {}
        self.r = {}
        self.name = name
        self.excl = excl


class Sched:
    def __init__(self, nc, es):
        self.nc = nc
        self.eng = {"pe": nc.tensor, "act": nc.scalar, "dve": nc.vector, "pool": nc.gpsimd, "sp": nc.sync}
        self.sems = {}
        self.cnt = {}
        self.seen = {e: {} for e in self.eng}
        self.es = es
        self.tot = {}
        for e in ("pe", "act", "dve", "pool"):
            self.tot[e] = 0
        self.ndma = 16
        for i in range(self.ndma):
            self.sems["d%d" % i] = es.enter_context(nc.semaphore("s_d%d" % i))
            self.cnt["d%d" % i] = 0
        self.dma_i = 0
        self.nwait = 0
        self.nins = 0

    def _need(self, eng, deps):
        sn = self.seen[eng]
        for k, c in deps.items():
            if sn.get(k, 0) < c:
                self.eng[eng].wait_ge(self.sems[k], c)
                sn[k] = c
                self.nwait += 1

    @staticmethod
    def _flat(bufs):
        out = []
        for b in bufs:
            if isinstance(b, (tuple, list)):
                out.extend(Sched._flat(b))
            else:
                out.append(b)
        return out

    def _sync(self, eng, reads, writes, acc):
        deps = {}
        for b in reads:
            for k, c in b.w.items():
                if deps.get(k, 0) < c:
                    deps[k] = c
            if b.excl:
                for k, c in b.r.items():
                    if not k.startswith(eng + "_") and deps.get(k, 0) < c:
                        deps[k] = c
        for b in writes:
            for k, c in b.w.items():
                if acc and k.startswith("pe_"):
                    continue
                if deps.get(k, 0) < c:
                    deps[k] = c
            for k, c in b.r.items():
                if deps.get(k, 0) < c:
                    deps[k] = c
        self._need(eng, deps)

    def _mark(self, key, c, reads, writes):
        for b in reads:
            if b.r.get(key, 0) < c:
                b.r[key] = c
        for b in writes:
            b.w = {key: c}
            b.r = {}

    def op(self, eng, ins_fn, reads=(), writes=(), acc=False):
        reads = self._flat(reads); writes = self._flat(writes)
        self._sync(eng, reads, writes, acc)
        ins = ins_fn(self.eng[eng])
        key = "%s_%d" % (eng, self.tot[eng] // SEM_LIM)
        self.tot[eng] += 1
        if key not in self.sems:
            self.sems[key] = self.es.enter_context(self.nc.semaphore("s_" + key))
            self.cnt[key] = 0
        self.cnt[key] += 1
        ins.then_inc(self.sems[key], 1)
        self._mark(key, self.cnt[key], reads, writes)
        self.nins += 1
        return ins

    def dma(self, eng, out, in_, reads=(), writes=(), **kw):
        reads = self._flat(reads); writes = self._flat(writes)
        key = "d%d" % (self.dma_i % self.ndma)
        self.dma_i += 1
        if self.cnt[key] > 0:
            self._need(eng, {key: self.cnt[key]})
        self._sync(eng, reads, writes, False)
        ins = self.eng[eng].dma_start(out=out, in_=in_, **kw)
        self.cnt[key] += 16
        ins.then_inc(self.sems[key], 16)
        self._mark(key, self.cnt[key], reads, writes)
        self.nins += 1
        return ins

    def wait_all(self, eng, bufs):
        deps = {}
        for b in bufs:
            for dd in (b.w, b.r):
                for k, c in dd.items():
                    if deps.get(k, 0) < c:
                        deps[k] = c
        self._need(eng, deps)


def bcl(ap2, n):
    return ap2.unsqueeze(2).to_broadcast([ap2.shape[0], ap2.shape[1], n])


def bcm(ap2, h):
    return ap2.unsqueeze(1).to_broadcast([ap2.shape[0], h, ap2.shape[1]])


def build_program(n_tiles, depth=DEPTH):
    nc = bass.Bass("TRN2", target_bir_lowering=False)
    es = contextlib.ExitStack()
    with es:
        _build(nc, es, n_tiles, depth)
    return nc


def _build(nc, es, n_tiles, depth):
    S = Sched(nc, es)
    ntok = n_tiles * T

    def din(name, shape):
        return nc.dram_tensor(name, shape, F32, kind="ExternalInput").ap()

    x_d = din("x", [ntok, D])
    out_d = nc.dram_tensor("out", [ntok, D], F32, kind="ExternalOutput").ap()
    wnames = {"ffn1_w_gate": (D, DFF), "ffn1_w_up": (D, DFF), "ffn1_w_down": (DFF, D), "w_in": (D, INW),
              "w_out": (2 * D, D), "ffn2_w_gate": (D, DFF), "ffn2_w_up": (D, DFF), "ffn2_w_down": (DFF, D)}
    w_f32 = {k: din(k, [DEPTH, r, c]) for k, (r, c) in wnames.items()}
    w_bf = {k: nc.dram_tensor(k + "_bf", [DEPTH, r, c], BF16, kind="Internal").ap() for k, (r, c) in wnames.items()}
    w_buf = {(k, l): Buf("w_%s_%d" % (k, l)) for k in wnames for l in range(DEPTH)}
    pn = {"ffn1_norm": [DEPTH, D], "mix_norm": [DEPTH, D], "ffn2_norm": [DEPTH, D], "final_norm": [D],
          "gdn_conv_w": [DEPTH, 4, 3072], "gdn_a_log": [DEPTH, 8], "gdn_dt_bias": [DEPTH, 8],
          "gdn_norm_w": [DEPTH, 128], "ssm_conv_w": [DEPTH, 4, 1536], "ssm_conv_b": [DEPTH, 1536],
          "ssm_a_log": [DEPTH, 16], "ssm_dt_bias": [DEPTH, 16], "ssm_d": [DEPTH, 16], "ssm_norm_w": [DEPTH, D]}
    p_d = {k: din(k, s) for k, s in pn.items()}

    def sb(name, shape, dt=F32):
        return es.enter_context(nc.sbuf_tensor(name, shape, dt))

    def ps(name):
        return es.enter_context(nc.psum_tensor(name, [128, 512], F32))

    ident = sb("ident", [128, 128]); ones = sb("ones", [128, 128]); triu = sb("triu", [128, 128])
    ustrR = sb("ustrR", [128, 4, 128]); loR = sb("loR", [128, 4, 128])
    cB = Buf("consts")
    S.op("pool", lambda e: e.memset(ident[:], 0.0), writes=[cB])
    S.op("pool", lambda e: e.affine_select(out=ident[:], in_=ident[:], pattern=[[-1, 128]], compare_op=ALU.not_equal,
                                           fill=1.0, base=0, channel_multiplier=1), reads=[cB], writes=[cB])
    S.op("pool", lambda e: e.memset(ones[:], 1.0), writes=[cB])
    S.op("pool", lambda e: e.memset(triu[:], 1.0), writes=[cB])
    S.op("pool", lambda e: e.affine_select(out=triu[:], in_=triu[:], pattern=[[1, 128]], compare_op=ALU.is_ge,
                                           fill=0.0, base=0, channel_multiplier=-1), reads=[cB], writes=[cB])
    S.op("pool", lambda e: e.memset(ustrR[:], 0.0), writes=[cB])
    S.op("pool", lambda e: e.memset(loR[:], 0.0), writes=[cB])
    for r in range(4):
        S.op("pool", lambda e, r=r: e.affine_select(out=ustrR[:, r, :], in_=ustrR[:, r, :], pattern=[[-1, 128]],
                                                    compare_op=ALU.is_gt, fill=BIG, base=0, channel_multiplier=1),
             reads=[cB], writes=[cB])
        S.op("pool", lambda e, r=r: e.affine_select(out=loR[:, r, :], in_=loR[:, r, :], pattern=[[1, 128]],
                                                    compare_op=ALU.is_ge, fill=-BIG, base=0, channel_multiplier=-1),
             reads=[cB], writes=[cB])

    pb = [ps("pb%d" % i) for i in range(8)]
    pbB = [Buf("pb%d" % i, excl=True) for i in range(8)]

    NRA, NRB = 121, 68
    colsA = [sb("colsA%d" % l, [128, NRA]) for l in range(DEPTH)]
    colsB = [sb("colsB%d" % l, [128, NRB]) for l in range(DEPTH)]
    stg = sb("stg", [128, 128])
    stgB = Buf("stg")
    parB = Buf("params")
    fin_bc = sb("fin_bc", [128, D])
    S.dma("sp", fin_bc[:], p_d["final_norm"].partition_broadcast(128), writes=[parB])
    hb = [sb("hb%d" % l, [128, 64]) for l in range(DEPTH)]
    negA = [sb("negA%d" % l, [128, 24]) for l in range(DEPTH)]
    spb = [sb("spb%d" % l, [128, 32]) for l in range(DEPTH)]
    dsk = [sb("dsk%d" % l, [128, 16]) for l in range(DEPTH)]
    for l in range(DEPTH):
        def stage(rows_list, dst, nrows):
            r0 = 0
            for ap2 in rows_list:
                n = ap2.shape[0]
                S.dma("sp", stg[r0:r0 + n, :], ap2, writes=[stgB])
                r0 += n
            assert r0 == nrows, (r0, nrows)
            S.op("pe", lambda e: e.transpose(pb[0][:, 0:nrows], stg[0:nrows, :], ident[0:nrows, 0:nrows]),
                 reads=[stgB, cB], writes=[pbB[0]])
            S.op("dve", lambda e: e.tensor_copy(out=dst[:, 0:nrows], in_=pb[0][:, 0:nrows]), reads=[pbB[0]], writes=[parB])

        stage([p_d["gdn_conv_w"][l].rearrange("k (c p) -> (k c) p", p=128),
               p_d["ffn1_norm"][l].rearrange("(c p) -> c p", p=128),
               p_d["mix_norm"][l].rearrange("(c p) -> c p", p=128),
               p_d["ffn2_norm"][l].rearrange("(c p) -> c p", p=128),
               p_d["gdn_norm_w"][l].rearrange("(c p) -> c p", p=128)], colsA[l], NRA)
        stage([p_d["ssm_conv_w"][l].rearrange("k (c p) -> (k c) p", p=128),
               p_d["ssm_conv_b"][l].rearrange("(c p) -> c p", p=128),
               p_d["ssm_norm_w"][l].rearrange("(c p) -> c p", p=128)], colsB[l], NRB)
        S.dma("sp", hb[l][:, 0:8], p_d["gdn_a_log"][l].partition_broadcast(128), writes=[parB])
        S.dma("sp", hb[l][:, 8:24], p_d["ssm_a_log"][l].partition_broadcast(128), writes=[parB])
        S.op("pool", lambda e: e.memset(spb[l][:, 0:8], 0.0), writes=[parB])
        S.dma("sp", spb[l][:, 8:16], p_d["gdn_dt_bias"][l].partition_broadcast(128), writes=[parB])
        S.dma("sp", spb[l][:, 16:32], p_d["ssm_dt_bias"][l].partition_broadcast(128), writes=[parB])
        S.dma("sp", dsk[l][:, :], p_d["ssm_d"][l].partition_broadcast(128), writes=[parB])
        S.op("act", lambda e: e.activation(out=negA[l][:, :], in_=hb[l][:, 0:24], func=AF.Exp), reads=[parB], writes=[parB])
        S.op("dve", lambda e: e.tensor_scalar(out=negA[l][:, :], in0=negA[l][:, :], scalar1=-1.0, scalar2=None, op0=ALU.mult),
             reads=[parB], writes=[parB])

    def colA(l, kind, c=0):
        base = {"conv": 0, "ffn1": 96, "mix": 104, "ffn2": 112, "gnorm": 120}[kind]
        return colsA[l][:, base + c:base + c + 1]

    cast_order = ["ffn1_w_gate", "ffn1_w_up", "ffn1_w_down", "w_in", "w_out", "ffn2_w_gate", "ffn2_w_up", "ffn2_w_down"]
    for l in range(depth):
        for k in cast_order:
            r, c = wnames[k]
            step = 256
            for r0 in range(0, r, step):
                S.dma("pool", w_bf[k][l, r0:r0 + step, :], w_f32[k][l, r0:r0 + step, :], writes=[], reads=[])
                key = "d%d" % ((S.dma_i - 1) % S.ndma)
                w_buf[(k, l)].w[key] = S.cnt[key]

    slots = [sb("wslot%d" % i, [128, KSLOT, 512], BF16) for i in range(NSLOT)]
    slotB = [Buf("slot%d" % i) for i in range(NSLOT)]

    def wview(k, l):
        return w_bf[k][l].rearrange("(k p) f -> p k f", p=128)

    def ffn_specs(l, which):
        sp = []
        g, u, dn = ("ffn%d_w_gate" % which, "ffn%d_w_up" % which, "ffn%d_w_down" % which)
        for c0 in range(0, DFF, 512):
            cn = min(512, DFF - c0)
            sp.append([((g, l), 0, 8, c0, cn, 0)])
            sp.append([((u, l), 0, 8, c0, cn, 0)])
        for half in range(2):
            sp.append([((dn, l), 0, 8, half * 512, 512, 0)])
            sp.append([((dn, l), 8, 8, half * 512, 512, 0)])
            sp.append([((dn, l), 16, 6, half * 512, 512, 0)])
        return sp

    def mix_specs(l):
        sp = [[(("w_in", l), 0, 8, O_GB, 16, 0), (("w_in", l), 0, 8, O_DT, 16, 16)]]
        for c0 in range(0, 3072, 512):
            sp.append([(("w_in", l), 0, 8, O_QKV + c0, 512, 0)])
        for c0 in range(0, 1536, 512):
            sp.append([(("w_in", l), 0, 8, O_XBC + c0, 512, 0)])
        for c0 in (O_GZ, O_GZ + 512, O_SZ, O_SZ + 512):
            sp.append([(("w_in", l), 0, 8, c0, 512, 0)])
        for half in range(2):
            sp.append([(("w_out", l), 0, 8, half * 512, 512, 0)])
            sp.append([(("w_out", l), 8, 8, half * 512, 512, 0)])
        return sp

    schedule = []
    for t in range(n_tiles):
        for l in range(depth):
            schedule += ffn_specs(l, 1) + mix_specs(l) + ffn_specs(l, 2)
    wst = {"issued": 0, "taken": 0}

    def w_issue():
        i = wst["issued"]
        if i >= len(schedule):
            return
        slot = i % NSLOT
        for (wk, k0, kn, c0, cn, dc) in schedule[i]:
            S.dma("sp", slots[slot][:, 0:kn, dc:dc + cn], wview(*wk)[:, k0:k0 + kn, c0:c0 + cn],
                  reads=[w_buf[wk]], writes=[])
            key = "d%d" % ((S.dma_i - 1) % S.ndma)
            slotB[slot].w[key] = S.cnt[key]
        wst["issued"] += 1

    def w_issue_guarded():
        i = wst["issued"]
        if i >= len(schedule):
            return
        slot = i % NSLOT
        S.wait_all("sp", [slotB[slot]])
        slotB[slot].w = {}
        slotB[slot].r = {}
        w_issue()

    def w_next(expect):
        i = wst["taken"]
        assert schedule[i] == expect, (i, schedule[i], expect)
        while wst["issued"] < min(len(schedule), i + 1 + LOOKAHEAD):
            w_issue_guarded()
        wst["taken"] += 1
        return slots[i % NSLOT], slotB[i % NSLOT]

    x_sb = sb("x_sb", [128, NB, D]); xB = [Buf("x%d" % b) for b in range(NB)]
    fscr = [sb("fscr%d" % i, [128, D]) for i in range(6)]; fB = [(Buf("fscr%da" % i), Buf("fscr%db" % i)) for i in range(6)]
    hscr = [sb("hscr%d" % i, [128, D], BF16) for i in range(5)]; hB = [(Buf("hscr%da" % i), Buf("hscr%db" % i)) for i in range(5)]
    pt32 = sb("pt32", [128, D]); pt32B = (Buf("pt32a"), Buf("pt32b"))
    hT = sb("hT", [128, 8, T], BF16); hTB = Buf("hT")
    zh = sb("zh", [128, NB, 2048])
    zhB = Buf("zh")
    hid = zh[:].rearrange("p b f -> p (b f)").bitcast(BF16)[:, 0:22 * T].rearrange("p (c t) -> p c t", t=T)
    stat = sb("stat", [128, 64]); statB = Buf("stat")
    qkv_tm = sb("qkv_tm", [128, NB, 3072]); qkvB = [Buf("qkv%d" % b) for b in range(NB)]
    xs_tm = sb("xs_tm", [128, NB, 1024]); xsB = [Buf("xs%d" % b) for b in range(NB)]
    b_tm = sb("b_tm", [128, NB, 256], BF16); btmB = [Buf("btm%d" % b) for b in range(NB)]
    bT = sb("bT", [128, 2, T], BF16); cT = sb("cT", [128, 2, T], BF16); bcTB = Buf("bcT")
    mixT = sb("mixT", [128, 16, T], BF16); mixTB = [Buf("mixT%d" % b) for b in range(NB)]
    xbuf = [sb("xbuf%d" % i, [128, T + 3]) for i in range(2)]; xbufB = [Buf("xbuf%d" % i) for i in range(2)]
    cacc = [sb("cacc%d" % i, [128, T]) for i in range(2)]; caccB = [Buf("cacc%d" % i) for i in range(2)]
    csil = [sb("csil%d" % i, [128, T]) for i in range(2)]; csilB = [Buf("csil%d" % i) for i in range(2)]
    tails = [sb("tails%d" % l, [128, 36, 3]) for l in range(DEPTH)]
    tailB = [[Buf("tail%d_%d" % (l, c)) for c in range(36)] for l in range(DEPTH)]
    sm = sb("sm", [128, NB, 32]); spv = sb("spv", [128, NB, 32]); beta = sb("beta", [128, NB, 8])
    gcat = sb("gcat", [128, NB, 24]); gct = sb("gct", [128, NB, 64]); egc = sb("egc", [128, NB, 24])
    edl = sb("edl", [128, NB, 24]); etot = sb("etot", [128, NB, 24]); biasA = sb("biasA", [128, NB, 8])
    ngc = sb("ngc", [128, NB, 24]); bg = sb("bg", [128, NB, 8]); dte = sb("dte", [128, NB, 16])
    smB = Buf("small")
    blk_small = sb("blk_small", [128, 64]); bsB = Buf("blk_small")
    Sg = [sb("Sg%d" % l, [128, 8, 128]) for l in range(DEPTH)]; SgB = [Buf("Sg%d" % l) for l in range(DEPTH)]
    Ss = [sb("Ss%d" % l, [128, 16, 64]) for l in range(DEPTH)]; SsB = [Buf("Ss%d" % l) for l in range(DEPTH)]
    for l in range(DEPTH):
        S.op("pool", lambda e: e.memset(Sg[l][:], 0.0), writes=[SgB[l]])
        S.op("pool", lambda e: e.memset(Ss[l][:], 0.0), writes=[SsB[l]])
        S.op("pool", lambda e: e.memset(tails[l][:], 0.0), writes=tailB[l])
    Sgb = sb("Sgb", [128, 8, 128], BF16); SgbB = Buf("Sgb")
    Ssb = sb("Ssb", [128, 16, 64], BF16); SsbB = Buf("Ssb")

    f3 = lambda ap, h: ap.rearrange("p (h e) -> p h e", h=h)

    def transpose_f32(src_ap, srcB, bank, col0, reads_extra=()):
        S.op("pe", lambda e: e.transpose(pb[bank][:, col0:col0 + 128], src_ap, ident[:]),
             reads=[srcB, cB] + list(reads_extra), writes=[pbB[bank]], acc=True)

    def rsqrt_small(ap, n, scale, readsB, writesB):
        S.op("act", lambda e: e.activation(out=ap, in_=ap, func=AF.Ln, bias=EPS, scale=scale), reads=readsB, writes=writesB)
        S.op("act", lambda e: e.activation(out=ap, in_=ap, func=AF.Exp, scale=-0.5), reads=writesB, writes=writesB)

    def norm_T(wcol_fn):
        for b in range(NB):
            S.op("act", lambda e: e.activation(out=fscr[0][:], in_=x_sb[:, b, :], func=AF.Square, accum_out=stat[:, b:b + 1]),
                 reads=[xB[b]], writes=[fB[0], statB])
        rsqrt_small(stat[:, 0:NB], NB, 1.0 / D, [statB], [statB])
        for b in range(NB):
            S.op("dve", lambda e: e.tensor_scalar(out=fscr[1][:], in0=x_sb[:, b, :], scalar1=stat[:, b:b + 1], scalar2=None,
                                                  op0=ALU.mult), reads=[xB[b], statB], writes=[fB[1]])
            for half in range(2):
                bank = 6 + half
                for c in range(4):
                    cc = half * 4 + c
                    transpose_f32(fscr[1][:, cc * 128:(cc + 1) * 128], fB[1], bank, c * 128)
                for c in range(4):
                    cc = half * 4 + c
                    S.op("act", lambda e: e.activation(out=hT[:, cc, b * 128:(b + 1) * 128], in_=pb[bank][:, c * 128:(c + 1) * 128],
                                                       func=AF.Identity, scale=wcol_fn(cc)), reads=[pbB[bank], parB], writes=[hTB])

    def ffn(l, which):
        kind = "ffn%d" % which
        norm_T(lambda cc: colA(l, kind, cc))
        specs = ffn_specs(l, which)
        si = 0
        hidB = zhB
        for c0 in range(0, DFF, 512):
            cn = min(512, DFF - c0)
            gs, gsB = w_next(specs[si]); us, usB = w_next(specs[si + 1]); si += 2
            for c in range(cn // 128):
                fc = c0 // 128 + c
                pg, pu = (0, 1) if fc % 2 == 0 else (2, 3)
                for k in range(8):
                    S.op("pe", lambda e: e.matmul(pb[pg][:, 0:T], lhsT=gs[:, k, c * 128:(c + 1) * 128], rhs=hT[:, k, :],
                                                  start=(k == 0), stop=(k == 7)), reads=[gsB, hTB], writes=[pbB[pg]], acc=(k > 0))
                for k in range(8):
                    S.op("pe", lambda e: e.matmul(pb[pu][:, 0:T], lhsT=us[:, k, c * 128:(c + 1) * 128], rhs=hT[:, k, :],
                                                  start=(k == 0), stop=(k == 7)), reads=[usB, hTB], writes=[pbB[pu]], acc=(k > 0))
                sc = fscr[2 + fc % 2]; scB = fB[2 + fc % 2]
                S.op("act", lambda e: e.activation(out=sc[:, 0:T], in_=pb[pg][:, 0:T], func=AF.Silu), reads=[pbB[pg]], writes=[scB])
                S.op("dve", lambda e: e.tensor_tensor(out=hid[:, fc, :], in0=pb[pu][:, 0:T], in1=sc[:, 0:T], op=ALU.mult),
                     reads=[pbB[pu], scB], writes=[hidB])
        for half in range(2):
            dsl = [w_next(specs[si]), w_next(specs[si + 1]), w_next(specs[si + 2])]; si += 3
            for b in range(NB):
                bank = 4 + b % 2
                for c in range(22):
                    ws, wsB = dsl[c // 8]
                    S.op("pe", lambda e: e.matmul(pb[bank][:, :], lhsT=hid[:, c, b * 128:(b + 1) * 128], rhs=ws[:, c % 8, :],
                                                  start=(c == 0), stop=(c == 21)), reads=[wsB, hidB], writes=[pbB[bank]], acc=(c > 0))
                xs_ = x_sb[:, b, half * 512:(half + 1) * 512]
                S.op("dve", lambda e: e.scalar_tensor_tensor(out=xs_, in0=pb[bank][:, :], scalar=0.5, in1=xs_, op0=ALU.mult, op1=ALU.add),
                     reads=[pbB[bank], xB[b]], writes=[xB[b]])

    def mixer(l):
        norm_T(lambda cc: colA(l, "mix", cc))
        specs = mix_specs(l)
        si = 0
        ws, wsB = w_next(specs[si]); si += 1
        for b in range(NB):
            for k in range(8):
                S.op("pe", lambda e: e.matmul(pb[0][:, 0:32], lhsT=hT[:, k, b * 128:(b + 1) * 128], rhs=ws[:, k, 0:32],
                                              start=(k == 0), stop=(k == 7)), reads=[wsB, hTB], writes=[pbB[0]], acc=(k > 0))
            S.op("dve", lambda e: e.tensor_copy(out=sm[:, b, :], in_=pb[0][:, 0:32]), reads=[pbB[0]], writes=[smB])
        S.op("act", lambda e: e.activation(out=beta[:], in_=sm[:, :, 0:8], func=AF.Tanh, scale=0.5), reads=[smB], writes=[smB])
        S.op("dve", lambda e: e.tensor_scalar(out=beta[:], in0=beta[:], scalar1=0.5, scalar2=0.5, op0=ALU.mult, op1=ALU.add),
             reads=[smB], writes=[smB])
        S.op("dve", lambda e: e.tensor_tensor(out=sm[:], in0=sm[:], in1=bcm(spb[l][:, :], NB), op=ALU.add), reads=[smB, parB], writes=[smB])
        S.op("dve", lambda e: e.tensor_scalar(out=sm[:, :, 0:8], in0=sm[:, :, 0:8], scalar1=-1.0, scalar2=None, op0=ALU.mult),
             reads=[smB], writes=[smB])
        S.op("act", lambda e: e.activation(out=spv[:], in_=sm[:], func=AF.Exp), reads=[smB], writes=[smB])
        S.op("act", lambda e: e.activation(out=spv[:], in_=spv[:], func=AF.Ln, bias=1.0, scale=1.0), reads=[smB], writes=[smB])
        S.op("dve", lambda e: e.tensor_tensor(out=gcat[:], in0=spv[:, :, 8:32], in1=bcm(negA[l][:, :], NB), op=ALU.mult),
             reads=[smB, parB], writes=[smB])
        for b in range(NB):
            S.op("pe", lambda e: e.matmul(pb[0][:, 0:24], lhsT=triu[:], rhs=gcat[:, b, :], start=True, stop=True),
                 reads=[smB, cB], writes=[pbB[0]])
            S.op("pe", lambda e: e.matmul(pb[0][:, 32:56], lhsT=ones[:], rhs=gcat[:, b, :], start=True, stop=True),
                 reads=[smB, cB], writes=[pbB[0]], acc=True)
            S.op("dve", lambda e: e.tensor_copy(out=gct[:, b, 0:56], in_=pb[0][:, 0:56]), reads=[pbB[0]], writes=[smB])
        S.op("act", lambda e: e.activation(out=egc[:], in_=gct[:, :, 0:24], func=AF.Exp), reads=[smB], writes=[smB])
        S.op("act", lambda e: e.activation(out=etot[:], in_=gct[:, :, 32:56], func=AF.Exp), reads=[smB], writes=[smB])
        S.op("dve", lambda e: e.tensor_tensor(out=edl[:], in0=gct[:, :, 32:56], in1=gct[:, :, 0:24], op=ALU.subtract), reads=[smB], writes=[smB])
        S.op("act", lambda e: e.activation(out=edl[:], in_=edl[:], func=AF.Exp), reads=[smB], writes=[smB])
        S.op("dve", lambda e: e.tensor_scalar(out=ngc[:], in0=gct[:, :, 0:24], scalar1=-1.0, scalar2=None, op0=ALU.mult), reads=[smB], writes=[smB])
        S.op("dve", lambda e: e.tensor_tensor(out=biasA[:], in0=gct[:, :, 0:8], in1=spv[:, :, 0:8], op=ALU.subtract), reads=[smB], writes=[smB])
        S.op("dve", lambda e: e.tensor_tensor(out=bg[:], in0=beta[:], in1=egc[:, :, 0:8], op=ALU.mult), reads=[smB], writes=[smB])
        S.op("dve", lambda e: e.tensor_tensor(out=dte[:], in0=spv[:, :, 16:32], in1=edl[:, :, 8:24], op=ALU.mult), reads=[smB], writes=[smB])

        if MXSTOP < 2:
            [w_next(s_) for s_ in specs[si:]]
            return
        slab_h = {}

        def stA(fc):
            grp, c = divmod(fc, 4)
            i2 = fc % 2
            if c == 0:
                slab_h[grp] = w_next(specs[si + grp])
            ws, wsB = slab_h[grp]
            bank = i2
            for k in range(8):
                S.op("pe", lambda e: e.matmul(pb[bank][:, 0:T], lhsT=ws[:, k, c * 128:(c + 1) * 128], rhs=hT[:, k, :],
                                              start=(k == 0), stop=(k == 7)), reads=[wsB, hTB], writes=[pbB[bank]], acc=(k > 0))
            xb, xbB = xbuf[i2], xbufB[i2]
            S.op("act", lambda e: e.copy(out=xb[:, 3:3 + T], in_=pb[bank][:, 0:T]), reads=[pbB[bank]], writes=[xbB])
            S.op("pool", lambda e: e.tensor_copy(out=xb[:, 0:3], in_=tails[l][:, fc, :]), reads=[tailB[l][fc]], writes=[xbB])
            S.op("pool", lambda e: e.tensor_copy(out=tails[l][:, fc, :], in_=xb[:, T:T + 3]), reads=[xbB], writes=[tailB[l][fc]])

        def stB(fc):
            i2 = fc % 2
            xb, xbB = xbuf[i2], xbufB[i2]
            if fc < 24:
                wc = lambda kk: colsA[l][:, kk * 24 + fc:kk * 24 + fc + 1]
            else:
                wc = lambda kk: colsB[l][:, kk * 12 + (fc - 24):kk * 12 + (fc - 24) + 1]
            ca, caB = cacc[i2], caccB[i2]
            S.op("dve", lambda e: e.tensor_scalar(out=ca[:], in0=xb[:, 0:T], scalar1=wc(0), scalar2=None, op0=ALU.mult),
                 reads=[xbB, parB], writes=[caB])
            for kk in (1, 2, 3):
                S.op("dve", lambda e: e.scalar_tensor_tensor(out=ca[:], in0=xb[:, kk:kk + T], scalar=wc(kk), in1=ca[:], op0=ALU.mult, op1=ALU.add),
                     reads=[xbB, parB, caB], writes=[caB])
            cs_, csB_ = csil[i2], csilB[i2]
            if fc < 24:
                S.op("act", lambda e: e.activation(out=cs_[:], in_=ca[:], func=AF.Silu), reads=[caB], writes=[csB_])
            else:
                S.op("act", lambda e: e.activation(out=cs_[:], in_=ca[:], func=AF.Silu, bias=colsB[l][:, 48 + fc - 24:48 + fc - 24 + 1]),
                     reads=[caB, parB], writes=[csB_])

        def stC(fc):
            i2 = fc % 2
            cs_, csB_ = csil[i2], csilB[i2]
            if fc >= 32:
                dst = bT if fc < 34 else cT
                S.op("dve", lambda e: e.tensor_copy(out=dst[:, fc % 2, :], in_=cs_[:]), reads=[csB_], writes=[bcTB])
            if fc < 34:
                tb = 2 + i2
                for b in range(NB):
                    transpose_f32(cs_[:, b * 128:(b + 1) * 128], csB_, tb, b * 128)
                for b in range(NB):
                    src = pb[tb][:, b * 128:(b + 1) * 128]
                    if fc < 24:
                        S.op("act", lambda e: e.copy(out=qkv_tm[:, b, fc * 128:(fc + 1) * 128], in_=src), reads=[pbB[tb]], writes=[qkvB[b]])
                    elif fc < 32:
                        S.op("act", lambda e: e.copy(out=xs_tm[:, b, (fc - 24) * 128:(fc - 23) * 128], in_=src), reads=[pbB[tb]], writes=[xsB[b]])
                    else:
                        S.op("act", lambda e: e.copy(out=b_tm[:, b, (fc - 32) * 128:(fc - 31) * 128], in_=src), reads=[pbB[tb]], writes=[btmB[b]])

        for i in range(36 + 2):
            if i < 36:
                stA(i)
            if 0 <= i - 1 < 36:
                stB(i - 1)
            if 0 <= i - 2 < 36:
                stC(i - 2)
        si += 9

        if MXSTOP < 3:
            [w_next(s_) for s_ in specs[si:]]
            return
        for zi in range(4):
            ws, wsB = w_next(specs[si]); si += 1
            for b in range(NB):
                bank = 4 + b % 2
                for k in range(8):
                    S.op("pe", lambda e: e.matmul(pb[bank][:, :], lhsT=hT[:, k, b * 128:(b + 1) * 128], rhs=ws[:, k, :],
                                                  start=(k == 0), stop=(k == 7)), reads=[wsB, hTB], writes=[pbB[bank]], acc=(k > 0))
                S.op("act", lambda e: e.activation(out=zh[:, b, zi * 512:(zi + 1) * 512], in_=pb[bank][:, :], func=AF.Silu),
                     reads=[pbB[bank]], writes=[zhB])

        if MXSTOP < 4:
            [w_next(s_) for s_ in specs[si:]]
            return
        for b in range(NB):
            if MXSTOP >= 4.0:
                gdn_block(l, b)
            if MXSTOP >= 4.2:
                ssd_block(l, b)
        if MXSTOP < 5:
            [w_next(s_) for s_ in specs[si:]]
            return

        wo = []
        for half in range(2):
            o0 = w_next(specs[si]); o1 = w_next(specs[si + 1]); si += 2
            for b in range(NB):
                bank = 4 + b % 2
                for c in range(16):
                    ws, wsB = o0 if c < 8 else o1
                    S.op("pe", lambda e: e.matmul(pb[bank][:, :], lhsT=mixT[:, c, b * 128:(b + 1) * 128], rhs=ws[:, c % 8, :],
                                                  start=(c == 0), stop=(c == 15)), reads=[wsB, mixTB[b]], writes=[pbB[bank]], acc=(c > 0))
                xs_ = x_sb[:, b, half * 512:(half + 1) * 512]
                S.op("dve", lambda e: e.tensor_tensor(out=xs_, in0=pb[bank][:, :], in1=xs_, op=ALU.add), reads=[pbB[bank], xB[b]], writes=[xB[b]])

    def gdn_block(l, b):
        q = qkv_tm[:, b, 0:1024]; k = qkv_tm[:, b, 1024:2048]; v = qkv_tm[:, b, 2048:3072]
        S.op("dve", lambda e: e.tensor_tensor(out=fscr[0][:], in0=q, in1=q, op=ALU.mult), reads=[qkvB[b]], writes=[fB[0]])
        S.op("dve", lambda e: e.tensor_reduce(out=blk_small[:, 0:8], in_=f3(fscr[0][:], 8), axis=AX.X, op=ALU.add), reads=[fB[0]], writes=[bsB])
        S.op("dve", lambda e: e.tensor_tensor(out=fscr[1][:], in0=k, in1=k, op=ALU.mult), reads=[qkvB[b]], writes=[fB[1]])
        S.op("dve", lambda e: e.tensor_reduce(out=blk_small[:, 8:16], in_=f3(fscr[1][:], 8), axis=AX.X, op=ALU.add), reads=[fB[1]], writes=[bsB])
        rsqrt_small(blk_small[:, 0:16], 16, 1.0, [bsB], [bsB])
        S.op("dve", lambda e: e.tensor_scalar(out=blk_small[:, 0:8], in0=blk_small[:, 0:8], scalar1=128.0 ** -0.5, scalar2=None, op0=ALU.mult),
             reads=[bsB], writes=[bsB])
        S.op("dve", lambda e: e.tensor_tensor(out=blk_small[:, 16:24], in0=blk_small[:, 8:16], in1=edl[:, b, 0:8], op=ALU.mult),
             reads=[bsB, smB], writes=[bsB])
        qn, qnB = fscr[0], fB[0]
        kn, knB = fscr[1], fB[1]
        S.op("dve", lambda e: e.tensor_tensor(out=f3(qn[:], 8), in0=f3(q, 8), in1=bcl(blk_small[:, 0:8], 128), op=ALU.mult),
             reads=[qkvB[b], bsB], writes=[qnB])
        S.op("dve", lambda e: e.tensor_tensor(out=f3(kn[:], 8), in0=f3(k, 8), in1=bcl(blk_small[:, 8:16], 128), op=ALU.mult),
             reads=[qkvB[b], bsB], writes=[knB])
        kdec, kdecB = hscr[0], hB[0]
        S.op("dve", lambda e: e.tensor_tensor(out=f3(kdec[:], 8), in0=f3(k, 8), in1=bcl(blk_small[:, 16:24], 128), op=ALU.mult),
             reads=[qkvB[b], bsB], writes=[kdecB])
        vb, vbB = fscr[2], fB[2]
        S.op("pool", lambda e: e.tensor_tensor(out=f3(vb[:], 8), in0=f3(v, 8), in1=bcl(beta[:, b, :], 128), op=ALU.mult),
             reads=[qkvB[b], smB], writes=[vbB])
        if MXSTOP < 4.01:
            return
        qT, qTB = hscr[1], hB[1]
        kT, kTB = hscr[2], hB[2]
        for (src, srcB, dst, dstB, banks) in ((qn, qnB, qT, qTB, (0, 1)), (kn, knB, kT, kTB, (2, 3))):
            for hh in range(2):
                for c in range(4):
                    transpose_f32(src[:, (hh * 4 + c) * 128:(hh * 4 + c + 1) * 128], srcB, banks[hh], c * 128)
                S.op("act", lambda e: e.copy(out=dst[:, hh * 512:(hh + 1) * 512], in_=pb[banks[hh]][:, :]), reads=[pbB[banks[hh]]], writes=[dstB])
        if MXSTOP < 4.02:
            return
        tg, tgB = fscr[3], fB[3]
        S.op("dve", lambda e: e.tensor_tensor(out=f3(tg[:], 8), in0=bcm(triu[:], 8), in1=bcl(gcat[:, b, 0:8], 128), op=ALU.mult),
             reads=[cB, smB], writes=[tgB])
        if MXSTOP < 4.021:
            return
        dS, dSB = fscr[4], fB[4]
        dT, dTB = fscr[5], fB[5]
        for hh in range(2):
            for (mask, bank) in ((ustrR, 4 + hh), (loR, 6 + hh)):
                S.op("pe", lambda e: e.matmul(pb[bank][:, :], lhsT=ones[:], rhs=tg[:, hh * 512:(hh + 1) * 512], start=True, stop=False),
                     reads=[tgB, cB], writes=[pbB[bank]])
                S.op("pe", lambda e: e.matmul(pb[bank][:, :], lhsT=ident[:], rhs=mask[:].rearrange("p r j -> p (r j)"), start=False, stop=True),
                     reads=[cB], writes=[pbB[bank]], acc=True)
            if MXSTOP < 4.022:
                continue
            for c in range(4):
                h = hh * 4 + c
                S.op("act", lambda e: e.activation(out=dS[:, h * 128:(h + 1) * 128], in_=pb[4 + hh][:, c * 128:(c + 1) * 128], func=AF.Exp,
                                                   scale=-1.0, bias=biasA[:, b, h:h + 1]), reads=[pbB[4 + hh], smB], writes=[dSB[hh]])
                S.op("act", lambda e: e.activation(out=dT[:, h * 128:(h + 1) * 128], in_=pb[6 + hh][:, c * 128:(c + 1) * 128], func=AF.Exp,
                                                   scale=1.0, bias=ngc[:, b, h:h + 1]), reads=[pbB[6 + hh], smB], writes=[dTB[hh]])
        if MXSTOP < 4.03:
            return
        for hh in range(2):
            for c in range(4):
                h = hh * 4 + c
                S.op("pe", lambda e: e.matmul(pb[0 + hh][:, c * 128:(c + 1) * 128], lhsT=kT[:, h * 128:(h + 1) * 128], rhs=kT[:, h * 128:(h + 1) * 128],
                                              start=True, stop=True), reads=[kTB], writes=[pbB[0 + hh]], acc=(c > 0))
            for c in range(4):
                h = hh * 4 + c
                S.op("pe", lambda e: e.matmul(pb[2 + hh][:, c * 128:(c + 1) * 128], lhsT=kT[:, h * 128:(h + 1) * 128], rhs=qT[:, h * 128:(h + 1) * 128],
                                              start=True, stop=True), reads=[kTB, qTB], writes=[pbB[2 + hh]], acc=(c > 0))
        qkm, qkmB = hscr[3], hB[3]
        for hh in range(2):
            sl = slice(hh * 512, (hh + 1) * 512)
            S.op("dve", lambda e: e.scalar_tensor_tensor(out=dS[:, sl], in0=pb[0 + hh][:, :], scalar=-1.0, in1=dS[:, sl], op0=ALU.mult, op1=ALU.mult),
                 reads=[pbB[0 + hh], dSB[hh]], writes=[dSB[hh]])
            S.op("dve", lambda e: e.tensor_tensor(out=qkm[:, sl], in0=pb[2 + hh][:, :], in1=dT[:, sl], op=ALU.mult),
                 reads=[pbB[2 + hh], dTB[hh]], writes=[qkmB[hh]])
        if MXSTOP < 4.04:
            return
        Pb, PbB = dS, dSB
        Ptb, PtbB = pt32, pt32B
        Ntb, NtbB = dT, dTB
        Nt, NtB = dT, dTB
        if MXSTOP < 4.0401:
            return
        for hh in range(2):
            for c in range(4):
                transpose_f32(dS[:, (hh * 4 + c) * 128:(hh * 4 + c + 1) * 128], dSB[hh], 0 + hh, c * 128)
            sl = slice(hh * 512, (hh + 1) * 512)
            if MXSTOP < 4.0402:
                continue
            S.op("act", lambda e: e.copy(out=Ptb[:, sl], in_=pb[0 + hh][:, :]), reads=[pbB[0 + hh]], writes=[PtbB[hh]])
            if MXSTOP < 4.0403:
                continue
            S.op("dve", lambda e: e.tensor_copy(out=Nt[:, sl], in_=pb[0 + hh][:, :]), reads=[pbB[0 + hh], qkmB[hh]], writes=[NtB[hh]])
        if MXSTOP < 4.0404:
            return
        if MXSTOP < 4.041:
            return
        for lev in range(int(os.environ.get("MK_NLEV", "6"))):
            for hh in range(2):
                for c in range(4):
                    h = hh * 4 + c
                    hs = slice(h * 128, (h + 1) * 128)
                    S.op("pe", lambda e: e.matmul(pb[0 + hh][:, c * 128:(c + 1) * 128], lhsT=Ptb[:, hs], rhs=Pb[:, hs], start=True, stop=True),
                         reads=[PtbB[hh], PbB[hh]], writes=[pbB[0 + hh]], acc=(c > 0))
                for c in range(4):
                    h = hh * 4 + c
                    hs = slice(h * 128, (h + 1) * 128)
                    S.op("pe", lambda e: e.matmul(pb[2 + hh][:, c * 128:(c + 1) * 128], lhsT=Pb[:, hs], rhs=Ptb[:, hs], start=True, stop=True),
                         reads=[PtbB[hh], PbB[hh]], writes=[pbB[2 + hh]], acc=(c > 0))
            for hh in range(2):
                sl = slice(hh * 512, (hh + 1) * 512)
                S.op("act", lambda e: e.copy(out=Pb[:, sl], in_=pb[0 + hh][:, :]), reads=[pbB[0 + hh]], writes=[PbB[hh]])
                S.op("act", lambda e: e.copy(out=Ptb[:, sl], in_=pb[2 + hh][:, :]), reads=[pbB[2 + hh]], writes=[PtbB[hh]])
            if MXSTOP < 4.042:
                continue
            for hh in range(2):
                for c in range(4):
                    h = hh * 4 + c
                    hs = slice(h * 128, (h + 1) * 128)
                    S.op("pe", lambda e: e.matmul(pb[4 + hh][:, c * 128:(c + 1) * 128], lhsT=Pb[:, hs], rhs=Ntb[:, hs], start=True, stop=True),
                         reads=[NtbB[hh], PbB[hh]], writes=[pbB[4 + hh]], acc=(c > 0))
            for hh in range(2):
                sl = slice(hh * 512, (hh + 1) * 512)
                S.op("dve", lambda e: e.tensor_tensor(out=Nt[:, sl], in0=pb[2 + hh][:, :], in1=Nt[:, sl], op=ALU.add), reads=[pbB[2 + hh], NtB[hh]], writes=[NtB[hh]])
                S.op("dve", lambda e: e.tensor_tensor(out=Nt[:, sl], in0=pb[4 + hh][:, :], in1=Nt[:, sl], op=ALU.add), reads=[pbB[4 + hh], NtB[hh]], writes=[NtB[hh]])
        TtB, TtBB = Nt, NtB
        S.op("dve", lambda e: e.tensor_tensor(out=f3(TtB[:], 8), in0=f3(Nt[:], 8), in1=bcm(ident[:], 8), op=ALU.add), reads=[NtB, cB], writes=[TtBB])
        if MXSTOP < 4.05:
            return
        S.op("act", lambda e: e.copy(out=Sgb[:].rearrange("p h e -> p (h e)"), in_=Sg[l][:].rearrange("p h e -> p (h e)")), reads=[SgB[l]], writes=[SgbB])
        for hh in range(2):
            for c in range(4):
                h = hh * 4 + c
                hs = slice(h * 128, (h + 1) * 128)
                S.op("pe", lambda e: e.matmul(pb[0 + hh][:, c * 128:(c + 1) * 128], lhsT=kT[:, hs], rhs=Sgb[:, h, :], start=True, stop=True),
                     reads=[kTB, SgbB], writes=[pbB[0 + hh]], acc=(c > 0))
            for c in range(4):
                h = hh * 4 + c
                hs = slice(h * 128, (h + 1) * 128)
                S.op("pe", lambda e: e.matmul(pb[2 + hh][:, c * 128:(c + 1) * 128], lhsT=qT[:, hs], rhs=Sgb[:, h, :], start=True, stop=True),
                     reads=[qTB, SgbB], writes=[pbB[2 + hh]], acc=(c > 0))
        r2, r2B = vb, vbB
        t_, tB_ = fscr[3], fB[3]
        for hh in range(2):
            sl = slice(hh * 512, (hh + 1) * 512)
            S.op("dve", lambda e: e.tensor_tensor(out=f3(t_[:, sl], 4), in0=f3(pb[0 + hh][:, :], 4), in1=bcl(bg[:, b, hh * 4:(hh + 1) * 4], 128), op=ALU.mult),
                 reads=[pbB[0 + hh], smB], writes=[tB_])
        S.op("dve", lambda e: e.tensor_tensor(out=r2[:], in0=vb[:], in1=t_[:], op=ALU.subtract), reads=[vbB, tB_], writes=[r2B])
        for hh in range(2):
            for c in range(4):
                h = hh * 4 + c
                hs = slice(h * 128, (h + 1) * 128)
                S.op("pe", lambda e: e.matmul(pb[4 + hh][:, c * 128:(c + 1) * 128], lhsT=TtB[:, hs], rhs=r2[:, hs], start=True, stop=True),
                     reads=[TtBB, r2B], writes=[pbB[4 + hh]], acc=(c > 0))
        vn, vnB = hscr[4], hB[4]
        for hh in range(2):
            sl = slice(hh * 512, (hh + 1) * 512)
            S.op("act", lambda e: e.copy(out=vn[:, sl], in_=pb[4 + hh][:, :]), reads=[pbB[4 + hh]], writes=[vnB])
        o_, oB_ = fscr[4], fB[4]
        for hh in range(2):
            sl = slice(hh * 512, (hh + 1) * 512)
            S.op("dve", lambda e: e.tensor_tensor(out=f3(o_[:, sl], 4), in0=f3(pb[2 + hh][:, :], 4), in1=bcl(egc[:, b, hh * 4:(hh + 1) * 4], 128), op=ALU.mult),
                 reads=[pbB[2 + hh], smB, PbB], writes=[oB_])
        for hh in range(2):
            for c in range(4):
                h = hh * 4 + c
                hs = slice(h * 128, (h + 1) * 128)
                S.op("pe", lambda e: e.matmul(pb[6 + hh][:, c * 128:(c + 1) * 128], lhsT=qkm[:, hs], rhs=vn[:, hs], start=True, stop=True),
                     reads=[qkmB, vnB], writes=[pbB[6 + hh]], acc=(c > 0))
            for c in range(4):
                h = hh * 4 + c
                hs = slice(h * 128, (h + 1) * 128)
                S.op("pe", lambda e: e.matmul(pb[0 + hh][:, c * 128:(c + 1) * 128], lhsT=kdec[:, hs], rhs=vn[:, hs], start=True, stop=True),
                     reads=[kdecB, vnB], writes=[pbB[0 + hh]], acc=(c > 0))
        Sf = Sg[l][:].rearrange("p h e -> p (h e)")
        for hh in range(2):
            sl = slice(hh * 512, (hh + 1) * 512)
            S.op("dve", lambda e: e.tensor_tensor(out=o_[:, sl], in0=pb[6 + hh][:, :], in1=o_[:, sl], op=ALU.add), reads=[pbB[6 + hh], oB_], writes=[oB_])
            S.op("dve", lambda e: e.tensor_tensor(out=f3(Sf[:, sl], 4), in0=f3(Sf[:, sl], 4), in1=bcl(etot[:, b, hh * 4:(hh + 1) * 4], 128), op=ALU.mult),
                 reads=[SgB[l], smB, SgbB], writes=[SgB[l]])
            S.op("dve", lambda e: e.tensor_tensor(out=Sf[:, sl], in0=pb[0 + hh][:, :], in1=Sf[:, sl], op=ALU.add), reads=[pbB[0 + hh], SgB[l]], writes=[SgB[l]])
        if DUMP == "gdn_o":
            S.dma("sp", out_d[b * 128:(b + 1) * 128, :], o_[:], reads=[oB_])
        if DUMP == "gdn_S":
            S.dma("sp", out_d[b * 128:(b + 1) * 128, :], Sf, reads=[SgB[l]])
        if MXSTOP < 4.06:
            return
        sq, sqB = fscr[3], fB[3]
        S.op("pool", lambda e: e.tensor_tensor(out=sq[:], in0=o_[:], in1=o_[:], op=ALU.mult), reads=[oB_], writes=[sqB])
        S.op("dve", lambda e: e.tensor_reduce(out=blk_small[:, 32:40], in_=f3(sq[:], 8), axis=AX.X, op=ALU.add), reads=[sqB], writes=[bsB])
        rsqrt_small(blk_small[:, 32:40], 8, 1.0 / 128, [bsB], [bsB])
        S.op("dve", lambda e: e.tensor_tensor(out=f3(o_[:], 8), in0=f3(o_[:], 8), in1=bcl(blk_small[:, 32:40], 128), op=ALU.mult), reads=[oB_, bsB], writes=[oB_])
        S.op("dve", lambda e: e.tensor_tensor(out=o_[:], in0=o_[:], in1=zh[:, b, 0:1024], op=ALU.mult), reads=[oB_, zhB], writes=[oB_])
        for hh in range(2):
            bank = 2 + hh
            for c in range(4):
                transpose_f32(o_[:, (hh * 4 + c) * 128:(hh * 4 + c + 1) * 128], oB_, bank, c * 128)
            for c in range(4):
                S.op("act", lambda e: e.activation(out=mixT[:, hh * 4 + c, b * 128:(b + 1) * 128], in_=pb[bank][:, c * 128:(c + 1) * 128],
                                                   func=AF.Identity, scale=colA(l, "gnorm")), reads=[pbB[bank], parB], writes=[mixTB[b]])

    def ssd_block(l, b):
        xs = xs_tm[:, b, :]
        dt_ = spv[:, b, 16:32]
        xdt, xdtB = hscr[0], hB[0]
        xe, xeB = hscr[1], hB[1]
        S.op("dve", lambda e: e.tensor_tensor(out=f3(xdt[:], 16), in0=f3(xs, 16), in1=bcl(dt_, 64), op=ALU.mult), reads=[xsB[b], smB], writes=[xdtB])
        S.op("pool", lambda e: e.tensor_tensor(out=f3(xe[:], 16), in0=f3(xs, 16), in1=bcl(dte[:, b, :], 64), op=ALU.mult), reads=[xsB[b], smB], writes=[xeB])
        S.op("act", lambda e: e.copy(out=Ssb[:].rearrange("p h e -> p (h e)"), in_=Ss[l][:].rearrange("p h e -> p (h e)")), reads=[SsB[l]], writes=[SsbB])
        tg, tgB = fscr[0], fB[0]
        seg, segB = fscr[1], fB[1]
        MT = (hscr[2], hscr[3]); MTB = (hB[2], hB[3])
        y, yB = fscr[2], fB[2]
        bs_ = slice(b * 128, (b + 1) * 128)
        for g in range(2):
            S.op("pe", lambda e: e.matmul(pb[4][:, g * 128:(g + 1) * 128], lhsT=bT[:, g, bs_], rhs=cT[:, g, bs_], start=True, stop=True),
                 reads=[bcTB], writes=[pbB[4]], acc=(g > 0))
        for g in range(2):
            S.op("dve", lambda e: e.tensor_tensor(out=f3(tg[:], 8), in0=bcm(triu[:], 8), in1=bcl(gcat[:, b, 8 + g * 8:16 + g * 8], 128), op=ALU.mult),
                 reads=[cB, smB], writes=[tgB])
            for hh in range(2):
                bank = 6 + hh
                S.op("pe", lambda e: e.matmul(pb[bank][:, :], lhsT=ones[:], rhs=tg[:, hh * 512:(hh + 1) * 512], start=True, stop=False),
                     reads=[tgB, cB], writes=[pbB[bank]])
                S.op("pe", lambda e: e.matmul(pb[bank][:, :], lhsT=ident[:], rhs=loR[:].rearrange("p r j -> p (r j)"), start=False, stop=True),
                     reads=[cB], writes=[pbB[bank]], acc=True)
                for c in range(4):
                    j = hh * 4 + c
                    h = g * 8 + j
                    S.op("act", lambda e: e.activation(out=seg[:, j * 128:(j + 1) * 128], in_=pb[bank][:, c * 128:(c + 1) * 128], func=AF.Exp,
                                                       scale=1.0, bias=ngc[:, b, 8 + h:9 + h]), reads=[pbB[bank], smB], writes=[segB])
            S.op("dve", lambda e: e.tensor_tensor(out=f3(MT[g][:], 8), in0=f3(seg[:], 8), in1=bcm(pb[4][:, g * 128:(g + 1) * 128], 8), op=ALU.mult),
                 reads=[segB, pbB[4]], writes=[MTB[g]])
        for g in range(2):
            for j in range(8):
                h = g * 8 + j
                S.op("pe", lambda e: e.matmul(pb[0 + g][:, j * 64:(j + 1) * 64], lhsT=MT[g][:, j * 128:(j + 1) * 128], rhs=xdt[:, h * 64:(h + 1) * 64],
                                              start=True, stop=True), reads=[MTB[g], xdtB], writes=[pbB[0 + g]], acc=(j > 0))
            S.op("pe", lambda e: e.matmul(pb[2 + g][:, :], lhsT=cT[:, g, bs_], rhs=Ssb[:, g * 8:(g + 1) * 8, :].rearrange("p h e -> p (h e)"),
                                          start=True, stop=True), reads=[bcTB, SsbB], writes=[pbB[2 + g]])
        for g in range(2):
            sl = slice(g * 512, (g + 1) * 512)
            S.op("dve", lambda e: e.tensor_tensor(out=f3(y[:, sl], 8), in0=f3(pb[2 + g][:, :], 8), in1=bcl(egc[:, b, 8 + g * 8:16 + g * 8], 64), op=ALU.mult),
                 reads=[pbB[2 + g], smB], writes=[yB])
            S.op("dve", lambda e: e.tensor_tensor(out=y[:, sl], in0=pb[0 + g][:, :], in1=y[:, sl], op=ALU.add), reads=[pbB[0 + g], yB], writes=[yB])
        Sf = Ss[l][:].rearrange("p h e -> p (h e)")
        for g in range(2):
            sl = slice(g * 512, (g + 1) * 512)
            S.op("pe", lambda e: e.matmul(pb[6 + g][:, :], lhsT=b_tm[:, b, g * 128:(g + 1) * 128], rhs=xe[:, sl], start=True, stop=True),
                 reads=[btmB[b], xeB], writes=[pbB[6 + g]])
            S.op("dve", lambda e: e.tensor_tensor(out=f3(Sf[:, sl], 8), in0=f3(Sf[:, sl], 8), in1=bcl(etot[:, b, 8 + g * 8:16 + g * 8], 64), op=ALU.mult),
                 reads=[SsB[l], smB, SsbB], writes=[SsB[l]])
            S.op("dve", lambda e: e.tensor_tensor(out=Sf[:, sl], in0=pb[6 + g][:, :], in1=Sf[:, sl], op=ALU.add), reads=[pbB[6 + g], SsB[l]], writes=[SsB[l]])
        t_, tB_ = fscr[3], fB[3]
        S.op("pool", lambda e: e.tensor_tensor(out=f3(t_[:], 16), in0=f3(xs, 16), in1=bcl(dsk[l][:, :], 64), op=ALU.mult), reads=[xsB[b], parB], writes=[tB_])
        S.op("dve", lambda e: e.tensor_tensor(out=y[:], in0=y[:], in1=t_[:], op=ALU.add), reads=[yB, tB_], writes=[yB])
        S.op("dve", lambda e: e.tensor_tensor(out=y[:], in0=y[:], in1=zh[:, b, 1024:2048], op=ALU.mult), reads=[yB, zhB], writes=[yB])
        for g in range(2):
            S.op("act", lambda e: e.activation(out=t_[:, g * 512:(g + 1) * 512], in_=y[:, g * 512:(g + 1) * 512], func=AF.Square,
                                               accum_out=blk_small[:, 40 + g:41 + g]), reads=[yB], writes=[tB_, bsB])
        rsqrt_small(blk_small[:, 40:42], 2, 1.0 / 512, [bsB], [bsB])
        S.op("dve", lambda e: e.tensor_tensor(out=f3(y[:], 2), in0=f3(y[:], 2), in1=bcl(blk_small[:, 40:42], 512), op=ALU.mult), reads=[yB, bsB], writes=[yB])
        for hh in range(2):
            bank = 4 + hh
            for c in range(4):
                transpose_f32(y[:, (hh * 4 + c) * 128:(hh * 4 + c + 1) * 128], yB, bank, c * 128)
            for c in range(4):
                cc = hh * 4 + c
                S.op("act", lambda e: e.activation(out=mixT[:, 8 + cc, b * 128:(b + 1) * 128], in_=pb[bank][:, c * 128:(c + 1) * 128],
                                                   func=AF.Identity, scale=colsB[l][:, 60 + cc:61 + cc]), reads=[pbB[bank], parB], writes=[mixTB[b]])

    for t in range(n_tiles):
        for b in range(NB):
            r0 = t * T + b * 128
            S.dma("sp", x_sb[:, b, :], x_d[r0:r0 + 128, :], writes=[xB[b]])
        for l in range(depth):
            if "F1" in DBG:
                ffn(l, 1)
            else:
                [w_next(s_) for s_ in ffn_specs(l, 1)]
            if "MX" in DBG:
                mixer(l)
            else:
                [w_next(s_) for s_ in mix_specs(l)]
            if "F2" in DBG:
                ffn(l, 2)
            else:
                [w_next(s_) for s_ in ffn_specs(l, 2)]
        for b in range(NB):
            S.op("act", lambda e: e.activation(out=fscr[0][:], in_=x_sb[:, b, :], func=AF.Square, accum_out=stat[:, b:b + 1]),
                 reads=[xB[b]], writes=[fB[0], statB])
        rsqrt_small(stat[:, 0:NB], NB, 1.0 / D, [statB], [statB])
        for b in range(NB):
            ob, obB = fscr[2 + b % 4], fB[2 + b % 4]
            S.op("dve", lambda e: e.scalar_tensor_tensor(out=ob[:], in0=x_sb[:, b, :], scalar=stat[:, b:b + 1], in1=fin_bc[:],
                                                         op0=ALU.mult, op1=ALU.mult), reads=[xB[b], statB, parB], writes=[obB])
            r0 = t * T + b * 128
            if not DUMP:
                S.dma("sp", out_d[r0:r0 + 128, :], ob[:], reads=[obB])
    S._need("sp", {"d%d" % i: S.cnt["d%d" % i] for i in range(S.ndma) if S.cnt["d%d" % i] > 0})
    print("program: %d instructions, %d waits" % (S.nins, S.nwait), S.tot)


_PROG = {}


def _get_prog(n_tiles, depth=DEPTH):
    key = (n_tiles, depth)
    if key not in _PROG:
        _PROG[key] = build_program(n_tiles, depth)
    return _PROG[key]


W_KEYS = ["ffn1_w_gate", "ffn1_w_up", "ffn1_w_down", "w_in", "w_out", "ffn2_w_gate", "ffn2_w_up", "ffn2_w_down"]
P_KEYS = ["ffn1_norm", "mix_norm", "ffn2_norm", "final_norm", "gdn_conv_w", "gdn_a_log", "gdn_dt_bias", "gdn_norm_w",
          "ssm_conv_w", "ssm_conv_b", "ssm_a_log", "ssm_dt_bias", "ssm_d", "ssm_norm_w"]


def run(inputs, n_cores=8, n_tiles=SEQ // T, depth=DEPTH):
    nc = _get_prog(n_tiles, depth)
    shared = {k: np.ascontiguousarray(np.asarray(inputs[k], dtype=np.float32)) for k in W_KEYS + P_KEYS}
    x = np.asarray(inputs["x"], dtype=np.float32)
    in_maps = []
    for c in range(n_cores):
        m = dict(shared)
        m["x"] = np.ascontiguousarray(x[c, : n_tiles * T, :])
        in_maps.append(m)
    res = run_bass_kernel_spmd(nc, in_maps, core_ids=list(range(n_cores)))
    return np.stack([np.asarray(r["out"]) for r in res.results], axis=0)


def kernel(**inputs):
    return run(inputs).astype(np.float32)
```
